# Optimizing a Trainium2 kernel written in Bass

```python
import math
import jax, jax.numpy as jnp
from jax import lax
import numpy as np

D_MODEL = 1024
BATCH = 2
SEQ = 16384
DEPTH = 2
DEC_BATCH = 8
DEC_SEQ = 64
PAST_LEN = 2048

CHUNK = 64
N_A = DEPTH // 2
N_B = DEPTH - N_A
RET_DK = 256
RET_HEADS = D_MODEL // RET_DK
RET_DV = 2 * RET_DK
FOX_DH = 64
FOX_HEADS = D_MODEL // FOX_DH
D_FF = ((8 * D_MODEL // 3 + 127) // 128) * 128
CONV_W = 3
Q_BLOCK = 128
ROPE_BASE = 10000.0
LN_EPS = 1e-5
ALPHA = (2 * DEPTH) ** 0.25
BETA = (8 * DEPTH) ** -0.25
NEG_INF = -1e30

kernel_name = 'yoco_retention_fox_convffn_step'


def _layernorm(x, g, b):
    xf = x.astype(jnp.float32)
    mu = jnp.mean(xf, axis=-1, keepdims=True)
    var = jnp.mean(jnp.square(xf - mu), axis=-1, keepdims=True)
    y = (xf - mu) * lax.rsqrt(var + LN_EPS)
    return (y * g.astype(jnp.float32) + b.astype(jnp.float32)).astype(x.dtype)


def _rope(x, pos):
    half = x.shape[-1] // 2
    inv = 1.0 / (ROPE_BASE ** (jnp.arange(half, dtype=jnp.float32) / half))
    ang = pos.astype(jnp.float32)[:, None] * inv[None, :]
    cos = jnp.cos(ang)[None, :, None, :].astype(x.dtype)
    sin = jnp.sin(ang)[None, :, None, :].astype(x.dtype)
    x1, x2 = x[..., :half], x[..., half:]
    return jnp.concatenate([x1 * cos - x2 * sin, x2 * cos + x1 * sin], axis=-1)


def _ret_log_gamma():
    return jnp.log1p(-jnp.exp2(-5.0 - jnp.arange(RET_HEADS, dtype=jnp.float32)))


def _retention_chunk(S, qkv):
    q, k, v = qkv
    C = q.shape[1]
    dt = q.dtype
    lg = _ret_log_gamma()
    idx = jnp.arange(C, dtype=jnp.float32)
    diff = idx[:, None] - idx[None, :]
    dmask = jnp.where(diff >= 0, jnp.exp(lg[:, None, None] * jnp.maximum(diff, 0.0)), 0.0).astype(dt)
    scores = jnp.einsum('bihd,bjhd->bhij', q, k) * dmask[None]
    inner = jnp.einsum('bhij,bjhe->bihe', scores, v)
    q_decay = jnp.exp((idx[:, None] + 1.0) * lg[None, :]).astype(dt)
    cross = jnp.einsum('bihd,bhde->bihe', q * q_decay[None, :, :, None], S)
    k_decay = jnp.exp((C - 1.0 - idx)[:, None] * lg[None, :]).astype(dt)
    S_new = (jnp.exp(C * lg).astype(dt)[None, :, None, None] * S
             + jnp.einsum('bjhd,bjhe->bhde', k * k_decay[None, :, :, None], v))
    return S_new, inner + cross


def _retention(q, k, v, S0):
    B, L = q.shape[0], q.shape[1]
    if L <= CHUNK:
        S, o = _retention_chunk(S0, (q, k, v))
        return o, S
    nc = L // CHUNK

    def to_chunks(t):
        return t.reshape(B, nc, CHUNK, *t.shape[2:]).swapaxes(0, 1)

    S, o = lax.scan(_retention_chunk, S0, (to_chunks(q), to_chunks(k), to_chunks(v)))
    o = o.swapaxes(0, 1).reshape(B, L, *o.shape[3:])
    return o, S


def _fox_block(qb, qpos_b, cq_b, k, v, ck, kpos):
    s = jnp.einsum('bqhd,bkhd->bhqk', qb, k).astype(jnp.float32) * (FOX_DH ** -0.5)
    s = s + (jnp.swapaxes(cq_b, 1, 2)[:, :, :, None] - jnp.swapaxes(ck, 1, 2)[:, :, None, :])
    mask = kpos[None, :] <= qpos_b[:, None]
    s = jnp.where(mask[None, None], s, NEG_INF)
    p = jax.nn.softmax(s, axis=-1).astype(v.dtype)
    return jnp.einsum('bhqk,bkhd->bqhd', p, v)


def _forgetting_attention(q, k, v, c, q_pos, k_pos):
    B, Lq = q.shape[0], q.shape[1]
    cq = c[:, -Lq:]
    if Lq <= Q_BLOCK:
        return _fox_block(q, q_pos, cq, k, v, c, k_pos)
    nb = Lq // Q_BLOCK
    qb = q.reshape(B, nb, Q_BLOCK, *q.shape[2:]).swapaxes(0, 1)
    cqb = cq.reshape(B, nb, Q_BLOCK, cq.shape[-1]).swapaxes(0, 1)
    pb = q_pos.reshape(nb, Q_BLOCK)
    o = lax.map(lambda a: _fox_block(a[0], a[1], a[2], k, v, c, k_pos), (qb, pb, cqb))
    return o.swapaxes(0, 1).reshape(q.shape)


def _conv_ffn(x, conv_state, w_up, conv_w, conv_b, w_down):
    L = x.shape[1]
    h = x @ w_up
    val, a = h[..., :D_FF], h[..., D_FF:]
    a_ext = jnp.concatenate([conv_state.astype(a.dtype), a], axis=1)
    conv = conv_b
    for j in range(CONV_W):
        conv = conv + conv_w[j] * a_ext[:, j:j + L]
    hidden = jax.nn.gelu(conv, approximate=False) * val
    return hidden @ w_down, a_ext[:, -(CONV_W - 1):]


def _trunk(x, pos0, ret_state0, conv_state0, k_past, v_past, logf_past,
           w_in_a, ln_ret_g, ln_ret_b, w_out_a, w_kvf, b_f, w_q_b, w_out_b,
           ln_mix_g, ln_mix_b, w_up, conv_w, conv_b, w_down, ln_ffn_g, ln_ffn_b):
    B, L, _ = x.shape
    pos = pos0 + jnp.arange(L, dtype=jnp.int32)
    HK, HV, HB = RET_HEADS * RET_DK, RET_HEADS * RET_DV, FOX_HEADS * FOX_DH
    ret_states, conv_states = [], []
    k_new = v_new = logf_new = None
    k_all = v_all = c_all = k_pos = None
    for layer in range(DEPTH):
        if layer < N_A:
            proj = x @ w_in_a[layer]
            q = _rope(proj[..., :HK].reshape(B, L, RET_HEADS, RET_DK), pos)
            k = _rope(proj[..., HK:2 * HK].reshape(B, L, RET_HEADS, RET_DK), pos) * (RET_DK ** -0.5)
            v = proj[..., 2 * HK:2 * HK + HV].reshape(B, L, RET_HEADS, RET_DV)
            g = proj[..., 2 * HK + HV:]
            o, S = _retention(q, k, v, ret_state0[layer].astype(q.dtype))
            ret_states.append(S)
            o = _layernorm(o, ln_ret_g[layer].reshape(RET_HEADS, RET_DV),
                           ln_ret_b[layer].reshape(RET_HEADS, RET_DV)).reshape(B, L, HV)
            mix = (jax.nn.silu(g) * o) @ w_out_a[layer]
        else:
            jb = layer - N_A
            if jb == 0:
                kvf = x @ w_kvf
                k_new = kvf[..., :HB].reshape(B, L, FOX_HEADS, FOX_DH)
                v_new = kvf[..., HB:2 * HB].reshape(B, L, FOX_HEADS, FOX_DH)
                logf32 = jax.nn.log_sigmoid((kvf[..., 2 * HB:] + b_f).astype(jnp.float32))
                logf_new = logf32.astype(x.dtype)
                k_all = jnp.concatenate([k_past.astype(x.dtype), k_new], axis=1)
                v_all = jnp.concatenate([v_past.astype(x.dtype), v_new], axis=1)
                c_all = jnp.cumsum(jnp.concatenate([logf_past.astype(jnp.float32), logf32], axis=1), axis=1)
                k_pos = jnp.arange(k_all.shape[1], dtype=jnp.int32)
            q = (x @ w_q_b[jb]).reshape(B, L, FOX_HEADS, FOX_DH)
            o = _forgetting_attention(q, k_all, v_all, c_all, pos, k_pos)
            mix = o.reshape(B, L, HB) @ w_out_b[jb]
        x = _layernorm(ALPHA * x + mix, ln_mix_g[layer], ln_mix_b[layer])
        f, cs = _conv_ffn(x, conv_state0[layer], w_up[layer], conv_w[layer], conv_b[layer], w_down[layer])
        conv_states.append(cs)
        x = _layernorm(ALPHA * x + f, ln_ffn_g[layer], ln_ffn_b[layer])
    return x, jnp.stack(ret_states), k_new, v_new, logf_new, jnp.stack(conv_states)


def setup_inputs(seed: int = 0) -> dict:
    key = jax.random.key(seed)
    ks = jax.random.split(key, 24)
    f32 = jnp.float32

    def nrm(k, shape, scale):
        return scale * jax.random.normal(k, shape, f32)

    HK, HV, HB = RET_HEADS * RET_DK, RET_HEADS * RET_DV, FOX_HEADS * FOX_DH
    fgate_bias = jnp.linspace(1.0, 6.0, FOX_HEADS, dtype=f32)
    x_prompt = nrm(ks[0], (BATCH, SEQ, D_MODEL), 1.0)
    x_sample = nrm(ks[1], (DEC_BATCH, DEC_SEQ, D_MODEL), 1.0)
    cache_k = nrm(ks[2], (DEC_BATCH, PAST_LEN, FOX_HEADS, FOX_DH), 1.0)
    cache_v = nrm(ks[3], (DEC_BATCH, PAST_LEN, FOX_HEADS, FOX_DH), BETA)
    cache_logf = jax.nn.log_sigmoid(nrm(ks[4], (DEC_BATCH, PAST_LEN, FOX_HEADS), 1.0) + fgate_bias)
    state_ret = nrm(ks[5], (N_A, DEC_BATCH, RET_HEADS, RET_DK, RET_DV), 0.1)
    state_ffn_conv = nrm(ks[6], (DEPTH, DEC_BATCH, CONV_W - 1, D_FF), 1.0)
    col_a = jnp.concatenate([jnp.ones((2 * HK,), f32), jnp.full((HV,), BETA, f32), jnp.ones((HV,), f32)])
    w_in_a = nrm(ks[7], (N_A, D_MODEL, 2 * HK + 2 * HV), D_MODEL ** -0.5) * col_a
    ln_ret_g = 1.0 + nrm(ks[8], (N_A, HV), 0.02)
    ln_ret_b = nrm(ks[9], (N_A, HV), 0.02)
    w_out_a = nrm(ks[10], (N_A, HV, D_MODEL), BETA * HV ** -0.5)
    col_b = jnp.concatenate([jnp.ones((HB,), f32), jnp.full((HB,), BETA, f32), jnp.ones((FOX_HEADS,), f32)])
    w_kvf = nrm(ks[11], (D_MODEL, 2 * HB + FOX_HEADS), D_MODEL ** -0.5) * col_b
    b_f = fgate_bias + nrm(ks[12], (FOX_HEADS,), 0.1)
    w_q_b = nrm(ks[13], (N_B, D_MODEL, HB), D_MODEL ** -0.5)
    w_out_b = nrm(ks[14], (N_B, HB, D_MODEL), BETA * HB ** -0.5)
    ln_mix_g = 1.0 + nrm(ks[15], (DEPTH, D_MODEL), 0.02)
    ln_mix_b = nrm(ks[16], (DEPTH, D_MODEL), 0.02)
    w_up = nrm(ks[17], (DEPTH, D_MODEL, 2 * D_FF), D_MODEL ** -0.5)
    conv_w = nrm(ks[18], (DEPTH, CONV_W, D_FF), CONV_W ** -0.5)
    conv_b = nrm(ks[19], (DEPTH, D_FF), 0.02)
    w_down = nrm(ks[20], (DEPTH, D_FF, D_MODEL), BETA * D_FF ** -0.5)
    ln_ffn_g = 1.0 + nrm(ks[21], (DEPTH, D_MODEL), 0.02)
    ln_ffn_b = nrm(ks[22], (DEPTH, D_MODEL), 0.02)
    return {'x_prompt': x_prompt, 'x_sample': x_sample, 'cache_k': cache_k, 'cache_v': cache_v,
            'cache_logf': cache_logf, 'state_ret': state_ret, 'state_ffn_conv': state_ffn_conv,
            'w_in_a': w_in_a, 'ln_ret_g': ln_ret_g, 'ln_ret_b': ln_ret_b, 'w_out_a': w_out_a,
            'w_kvf': w_kvf, 'b_f': b_f, 'w_q_b': w_q_b, 'w_out_b': w_out_b,
            'ln_mix_g': ln_mix_g, 'ln_mix_b': ln_mix_b, 'w_up': w_up, 'conv_w': conv_w, 'conv_b': conv_b,
            'w_down': w_down, 'ln_ffn_g': ln_ffn_g, 'ln_ffn_b': ln_ffn_b}


def reference(x_prompt, x_sample, cache_k, cache_v, cache_logf, state_ret, state_ffn_conv,
              w_in_a, ln_ret_g, ln_ret_b, w_out_a, w_kvf, b_f, w_q_b, w_out_b,
              ln_mix_g, ln_mix_b, w_up, conv_w, conv_b, w_down, ln_ffn_g, ln_ffn_b):
    weights = (w_in_a, ln_ret_g, ln_ret_b, w_out_a, w_kvf, b_f, w_q_b, w_out_b,
               ln_mix_g, ln_mix_b, w_up, conv_w, conv_b, w_down, ln_ffn_g, ln_ffn_b)
    dt = x_prompt.dtype
    Bp = x_prompt.shape[0]
    y_p, ret_p, k_p, v_p, lf_p, conv_p = _trunk(
        x_prompt, 0,
        jnp.zeros((N_A, Bp, RET_HEADS, RET_DK, RET_DV), dt),
        jnp.zeros((DEPTH, Bp, CONV_W - 1, D_FF), dt),
        jnp.zeros((Bp, 0, FOX_HEADS, FOX_DH), dt),
        jnp.zeros((Bp, 0, FOX_HEADS, FOX_DH), dt),
        jnp.zeros((Bp, 0, FOX_HEADS), jnp.float32),
        *weights)
    y_s, ret_s, k_s, v_s, lf_s, conv_s = _trunk(
        x_sample, cache_k.shape[1], state_ret, state_ffn_conv, cache_k, cache_v, cache_logf, *weights)
    return (y_p, y_s, ret_p, k_p, v_p, lf_p, conv_p, ret_s, k_s, v_s, lf_s, conv_s)
```

```python
import math
from contextlib import ExitStack
import numpy as np
import ml_dtypes
import concourse.bass as bass
import concourse.mybir as mybir
from concourse.bass_utils import run_bass_kernel_spmd

F32 = mybir.dt.float32
BF16 = mybir.dt.bfloat16
I32 = mybir.dt.int32
AF = mybir.ActivationFunctionType
ALU = mybir.AluOpType
AX = mybir.AxisListType

D = 1024
DFF = 2816
NFC = 22
PAST = 2048
NS = 64
LN_EPS = 1e-5
ALPHA = 4.0 ** 0.25
GROUPS = [[0, 1, 2, 3], [4, 5, 6, 7]]
NEGM = -30000.0


class Buf:
    __slots__ = ("name", "t", "wtok", "rtok", "dsem", "dcnt")

    def __init__(self, name, t):
        self.name = name
        self.t = t
        self.wtok = {}
        self.rtok = {}
        self.dsem = None
        self.dcnt = 0

    def __getitem__(self, k):
        return self.t[k]


class Sched:
    ENG = ("pe", "act", "dve", "pool", "sp")

    def __init__(self, nc):
        self.nc = nc
        self.e = {"pe": nc.tensor, "act": nc.scalar, "dve": nc.vector, "pool": nc.gpsimd, "sp": nc.sync}
        self.sem = {k: nc.alloc_semaphore("prog_" + k) for k in self.ENG}
        self.cnt = {k: 0 for k in self.ENG}
        self.seen = {k: {} for k in self.ENG}
        self.cc_sem = nc.alloc_semaphore("cc_sem")
        self.cc_cnt = 0
        self.dma_bufs = []
        self.sem_pool = []
        self.nsem = 0
        self.ninst = 0
        self.nwait = 0

    def _dsem(self, b):
        if b.dsem is None:
            if self.sem_pool:
                b.dsem, b.dcnt = self.sem_pool.pop()
            else:
                b.dsem = self.nc.alloc_semaphore("dq%d" % self.nsem)
                self.nsem += 1
            self.dma_bufs.append(b)
        return b.dsem

    def release(self, bufs):
        for b in bufs:
            if b.dsem is not None:
                self.sem_pool.append((b.dsem, b.dcnt))
                self.dma_bufs.remove(b)
                b.dsem = None

    def _wait(self, eng, deps, skip_self=False, attach=False):
        E = self.e[eng]
        seen = self.seen[eng]
        own = self.sem[eng]
        need = []
        for sem, val in deps.items():
            if skip_self and sem is own:
                continue
            if seen.get(sem, 0) < val:
                need.append((sem, val))
                seen[sem] = val
        last = need.pop() if (attach and need) else None
        for sem, val in need:
            E.wait_ge(sem, val)
            self.nwait += 1
        return last

    @staticmethod
    def _collect(reads, writes):
        deps = {}
        for r in reads:
            for s, v in r.wtok.items():
                if deps.get(s, 0) < v:
                    deps[s] = v
        for w in writes:
            for s, v in w.wtok.items():
                if deps.get(s, 0) < v:
                    deps[s] = v
            for s, v in w.rtok.items():
                if deps.get(s, 0) < v:
                    deps[s] = v
        return deps

    @staticmethod
    def _record(tok, reads, writes):
        s, v = tok
        for r in reads:
            if r.rtok.get(s, 0) < v:
                r.rtok[s] = v
        for w in writes:
            w.wtok = {s: v}
            w.rtok = {}

    def op(self, eng, fn, reads=(), writes=(), signal=True):
        deps = self._collect(reads, writes)
        last = self._wait(eng, deps, skip_self=(eng == "pe"), attach=True)
        ins = fn(self.e[eng])
        if last is not None:
            ins._wait_ge(last[0], last[1])
        self.ninst += 1
        if signal:
            self.cnt[eng] += 1
            ins.then_inc(self.sem[eng], 1)
            tok = (self.sem[eng], self.cnt[eng])
        else:
            tok = (self.sem[eng], self.cnt[eng] + 1)
        self._record(tok, reads, writes)
        return ins

    def dma(self, q, out_ap, in_ap, reads=(), writes=(), indirect=None):
        dst = writes[0]
        deps = self._collect(reads, writes)
        last = self._wait(q, deps, attach=(indirect is None))
        sem = self._dsem(dst)
        dst.dcnt += 1
        if indirect is not None:
            self.e[q].indirect_dma_start(out=out_ap, out_offset=None, in_=in_ap,
                                         in_offset=bass.IndirectOffsetOnAxis(ap=indirect, axis=0)).then_inc(sem, 16)
        else:
            ins = self.e[q].dma_start(out=out_ap, in_=in_ap)
            if last is not None:
                ins._wait_ge(last[0], last[1])
            ins.then_inc(sem, 16)
        self.ninst += 1
        s, v = sem, 16 * dst.dcnt
        for r in reads:
            if r.rtok.get(s, 0) < v:
                r.rtok[s] = v
        dst.wtok = {s: v}
        dst.rtok = {}

    def collective(self, kind, in_buf, out_buf, in_ap, out_ap):
        deps = self._collect([in_buf], [out_buf])
        self._wait("pool", deps)
        self.cc_cnt += 1
        self.nc.gpsimd.collective_compute(kind, ALU.bypass, replica_groups=GROUPS, ins=[in_ap],
                                          outs=[out_ap]).then_inc(self.cc_sem, 1)
        self._record((self.cc_sem, self.cc_cnt), [in_buf], [out_buf])

    def _all_tokens(self, with_cc):
        deps = {self.sem[k]: self.cnt[k] for k in self.ENG if self.cnt[k] > 0}
        for b in self.dma_bufs:
            if b.dcnt:
                deps[b.dsem] = 16 * b.dcnt
        for s, c in self.sem_pool:
            if c:
                deps[s] = 16 * c
        if with_cc and self.cc_cnt:
            deps[self.cc_sem] = self.cc_cnt
        return deps

    def barrier(self):
        deps = self._all_tokens(False)
        for k in self.ENG:
            self._wait(k, deps)

    def final_wait(self):
        self._wait("sp", self._all_tokens(True))


class Job:
    def __init__(self, **kw):
        self.__dict__.update(kw)


class Builder:
    def __init__(self, SEQ):
        self.SEQ = SEQ
        self.SEG = SEQ // 4
        self.nc = nc = bass.Bass("TRN2", target_bir_lowering=False)
        self.S = Sched(nc)
        self.uid = 0
        self.inputs = {}
        self.outputs = {}
        self.ps = [Buf("ps%d" % i, nc.alloc_psum_tensor("ps%d" % i, [128, 512], F32)) for i in range(8)]
        self.epsc = nc.alloc_sbuf_tensor("epsc", [128, 1], F32)
        self.epsb = Buf("epsc", self.epsc)
        self.S.op("pool", lambda e: e.memset(self.epsc[:], LN_EPS), writes=[self.epsb])

    def din(self, name, shape, dt=F32):
        h = self.nc.dram_tensor(name, list(shape), dt, kind="ExternalInput")
        self.inputs[name] = (tuple(shape), dt)
        return Buf(name, h)

    def dout(self, name, shape, dt=F32):
        h = self.nc.dram_tensor(name, list(shape), dt, kind="ExternalOutput")
        self.outputs[name] = (tuple(shape), dt)
        return Buf(name, h)

    def dscr(self, name, shape, dt):
        h = self.nc.dram_tensor(name, list(shape), dt, kind="Internal")
        return Buf(name, h)

    def sbuf(self, es, name, shape, dt, lst):
        self.uid += 1
        h = es.enter_context(self.nc.sbuf_tensor("%s_%d" % (name, self.uid), list(shape), dt))
        b = Buf("%s_%d" % (name, self.uid), h)
        lst.append(b)
        return b

    def mm(self, out, lhsT, rhs, start, stop, reads, writes):
        self.S.op("pe", lambda e: e.matmul(out, lhsT, rhs, start=start, stop=stop), reads=reads, writes=writes,
                  signal=bool(stop))

    def tr(self, out, in_, ident, reads, writes, signal):
        self.S.op("pe", lambda e: e.transpose(out, in_, ident), reads=reads, writes=writes, signal=signal)

    def act(self, out, in_, func, reads, writes, bias=None, scale=None, accum_out=None):
        kw = {}
        if bias is not None:
            kw["bias"] = bias
        if scale is not None:
            kw["scale"] = scale
        if accum_out is not None:
            kw["accum_out"] = accum_out
        self.S.op("act", lambda e: e.activation(out=out, in_=in_, func=func, **kw), reads=reads, writes=writes)

    def tt(self, eng, out, in0, in1, op, reads, writes):
        self.S.op(eng, lambda e: e.tensor_tensor(out=out, in0=in0, in1=in1, op=op), reads=reads, writes=writes)

    def ts(self, eng, out, in0, s1, op0, reads, writes, s2=None, op1=None):
        if op1 is None:
            self.S.op(eng, lambda e: e.tensor_scalar(out=out, in0=in0, scalar1=s1, scalar2=None, op0=op0),
                      reads=reads, writes=writes)
        else:
            self.S.op(eng, lambda e: e.tensor_scalar(out=out, in0=in0, scalar1=s1, scalar2=s2, op0=op0, op1=op1),
                      reads=reads, writes=writes)

    def stt(self, eng, out, in0, scalar, in1, op0, op1, reads, writes):
        self.S.op(eng, lambda e: e.scalar_tensor_tensor(out=out, in0=in0, scalar=scalar, in1=in1, op0=op0, op1=op1),
                  reads=reads, writes=writes)

    def cp(self, eng, out, in_, reads, writes):
        if eng == "act":
            self.S.op("act", lambda e: e.copy(out=out, in_=in_), reads=reads, writes=writes)
        else:
            self.S.op(eng, lambda e: e.tensor_copy(out=out, in_=in_), reads=reads, writes=writes)

    def ln_rows(self, src_ap_fn, nrows, width, src_bufs, stat, es_name=""):
        nchunk = (width + 511) // 512
        st = stat
        for c in range(nchunk):
            lo, hi = c * 512, min(width, (c + 1) * 512)
            self.S.op("dve", lambda e, c=c, lo=lo, hi=hi: e.bn_stats(out=st[0:nrows, 8 + 6 * c: 14 + 6 * c],
                                                                     in_=src_ap_fn(lo, hi)),
                      reads=src_bufs, writes=[st])
        self.S.op("dve", lambda e: e.bn_aggr(out=st[0:nrows, 4:6],
                                             in_=st[0:nrows, 8:8 + 6 * nchunk].rearrange("p (c s) -> p c s", s=6)),
                  reads=[st], writes=[st])
        self.act(st[0:nrows, 2:3], st[0:nrows, 5:6], AF.Sqrt, [st, self.epsb], [st], bias=self.epsc[0:nrows, 0:1],
                 scale=1.0)
        self.S.op("dve", lambda e: e.reciprocal(out=st[0:nrows, 0:1], in_=st[0:nrows, 2:3]), reads=[st], writes=[st])
        self.stt("dve", st[0:nrows, 1:2], st[0:nrows, 4:5], -1.0, st[0:nrows, 0:1], ALU.mult, ALU.mult, [st], [st])
        return st[0:nrows, 0:1], st[0:nrows, 1:2]

    def phaseA(self, J, C):
        nc, S, ps = self.nc, self.S, self.ps
        TB, NB = J.TB, J.NB
        T = TB * NB
        ntile = J.ntok // T
        bufs = []
        with ExitStack() as es:
            sb = lambda n, s, d: self.sbuf(es, n, s, d, bufs)
            w = sb("Aw", [128, 8, 1536], BF16)
            xtok = [sb("Axtok", [128, NB, 1024], BF16) for _ in range(2)]
            rope = [sb("Arope", [128, 4, T], F32) for _ in range(2)]
            xT = sb("AxT", [128, 8, T], BF16)
            t1 = sb("At1", [128, T], F32)
            t2 = sb("At2", [128, T], F32)
            qdT = sb("AqdT", [128, 2, T], BF16)
            kiT = sb("AkiT", [128, 2, T], BF16)
            kitok = sb("Akitok", [128, NB, 256], BF16)
            vtok = sb("Avtok", [128, NB, 512], BF16)
            gs = sb("Ags", [128, NB, 512], F32)
            pT = sb("ApT", [128, NB, TB], BF16)
            Sf = sb("ASf", [128, 2, 512], F32)
            Sb = sb("ASb", [128, 2, 512], BF16)
            tmp = sb("Atmp", [128, 2, 512], F32)
            on = [sb("Aon", [128, 512], F32) for _ in range(2)]
            og = [sb("Aog", [128, NB, 512], BF16) for _ in range(2)]
            lnp = sb("Alnp", [128, 2, 512], F32)
            gam = sb("Agam", [128, 1], F32)
            stat = [sb("Astat", [128, 32], F32) for _ in range(2)]

            S.dma("pool", w[:], J.w.t[:], reads=[J.w], writes=[w])
            S.dma("sp", lnp[:, 0, :], J.lng.partition_broadcast(128), reads=[J.lnb_buf], writes=[lnp])
            S.dma("sp", lnp[:, 1, :], J.lnb.partition_broadcast(128), reads=[J.lnb_buf], writes=[lnp])
            S.dma("sp", gam[:], J.gam, reads=[J.gam_buf], writes=[gam])
            if J.S0 is not None:
                S.dma("sp", Sf[:], J.S0.rearrange("(h p) e -> p h e", p=128), reads=[J.S0_buf], writes=[Sf])
            else:
                S.op("pool", lambda e: e.memset(Sf[:], 0.0), writes=[Sf])
            self.cp("pool", Sb[:], Sf[:], [Sf], [Sb])

            def load(t):
                xs = J.x[t * T:(t + 1) * T, :].rearrange("(nb p) f -> p nb f", p=TB)
                S.dma("pool", xtok[t % 2][0:TB, :, :], xs, reads=[J.x_buf], writes=[xtok[t % 2]])
                S.dma("sp", rope[t % 2][:], J.rope[:, :, t * T:(t + 1) * T].rearrange("a p t -> p a t"),
                      reads=[J.rope_buf], writes=[rope[t % 2]])

            load(0)
            for t in range(ntile):
                if t + 1 < ntile:
                    load(t + 1)
                xk, rp = xtok[t % 2], rope[t % 2]
                for kc in range(8):
                    pb = ps[kc % 2]
                    pv = pb[:].bitcast(BF16)
                    for nb in range(NB):
                        self.tr(pv[:, nb * TB:(nb + 1) * TB], xk[0:TB, nb, kc * 128:(kc + 1) * 128],
                                C.identb[0:TB, 0:TB], [xk, C.identb_buf], [pb], signal=(nb == NB - 1))
                    self.cp("act" if kc % 2 else "dve", xT[:, kc, :], pv[:, 0:T], [pb], [xT])
                for qk in range(2):
                    pa, pbk = ps[2], ps[3]
                    for half, pp in enumerate((pa, pbk)):
                        col = qk * 256 + half * 128
                        for kc in range(8):
                            self.mm(pp[:, 0:T], w[:, kc, col:col + 128], xT[:, kc, :], kc == 0, kc == 7,
                                    [w, xT], [pp])
                    cs, sn = rp[:, 2 * qk, :], rp[:, 2 * qk + 1, :]
                    dst = qdT if qk == 0 else kiT
                    self.tt("dve", t1[:], pa[:, 0:T], cs, ALU.mult, [pa, rp], [t1])
                    self.tt("dve", t2[:], pbk[:, 0:T], sn, ALU.mult, [pbk, rp], [t2])
                    self.tt("pool", dst[:, 0, :], t1[:], t2[:], ALU.subtract, [t1, t2], [dst])
                    self.tt("dve", t1[:], pbk[:, 0:T], cs, ALU.mult, [pbk, rp], [t1])
                    self.tt("dve", t2[:], pa[:, 0:T], sn, ALU.mult, [pa, rp], [t2])
                    self.tt("pool", dst[:, 1, :], t1[:], t2[:], ALU.add, [t1, t2], [dst])
                pb = ps[0]
                pv = pb[:].bitcast(BF16)
                for c in range(NB):
                    for half in range(2):
                        self.tr(pv[0:TB, (c * 2 + half) * 128:(c * 2 + half + 1) * 128],
                                kiT[:, half, c * TB:(c + 1) * TB], C.identb[:, :], [kiT, C.identb_buf], [pb],
                                signal=(c == NB - 1 and half == 1))
                self.cp("act", kitok[0:TB, :, :], pv[0:TB, 0:NB * 256].rearrange("p (c d) -> p c d", d=256),
                        [pb], [kitok])
                pb = ps[1]
                for c in range(NB):
                    for half in range(2):
                        self.mm(pb[0:TB, c * TB:(c + 1) * TB], kiT[:, half, c * TB:(c + 1) * TB],
                                qdT[:, half, c * TB:(c + 1) * TB], half == 0, half == 1, [kiT, qdT], [pb])
                for c in range(NB):
                    self.tt("dve", pT[0:TB, c, :], pb[0:TB, c * TB:(c + 1) * TB], C.maskT[0:TB, 0:TB], ALU.mult,
                            [pb, C.maskT_buf], [pT])
                for nb in range(NB):
                    pp = ps[2 + nb % 2]
                    for kc in range(8):
                        self.mm(pp[0:TB, :], xT[:, kc, nb * TB:(nb + 1) * TB], w[:, kc, 512:1024], kc == 0, kc == 7,
                                [xT, w], [pp])
                    self.cp("act", vtok[0:TB, nb, :], pp[0:TB, :], [pp], [vtok])
                for nb in range(NB):
                    pp = ps[2 + nb % 2]
                    for kc in range(8):
                        self.mm(pp[0:TB, :], xT[:, kc, nb * TB:(nb + 1) * TB], w[:, kc, 1024:1536], kc == 0, kc == 7,
                                [xT, w], [pp])
                    self.act(gs[0:TB, nb, :], pp[0:TB, :], AF.Sigmoid, [pp], [gs])
                    self.tt("dve", gs[0:TB, nb, :], gs[0:TB, nb, :], pp[0:TB, :], ALU.mult, [gs, pp], [gs])
                ogt = og[t % 2]
                for c in range(NB):
                    po = ps[4 + c % 2]
                    self.mm(po[0:TB, :], pT[0:TB, c, :], vtok[0:TB, c, :], True, False, [pT, vtok], [po])
                    for half in range(2):
                        self.mm(po[0:TB, :], qdT[:, half, c * TB:(c + 1) * TB], Sb[:, half, :], False, half == 1,
                                [qdT, Sb], [po])
                    for half in range(2):
                        pd = ps[6 + half]
                        self.mm(pd[:, :], kitok[0:TB, c, half * 128:(half + 1) * 128], vtok[0:TB, c, :], True, True,
                                [kitok, vtok], [pd])
                    for half in range(2):
                        self.act(tmp[:, half, :], ps[6 + half][:, :], AF.Identity, [ps[6 + half], gam], [tmp],
                                 scale=gam[:, 0:1])
                    self.stt("dve", Sf[:], Sf[:], gam[:, 0:1], tmp[:], ALU.mult, ALU.add, [Sf, tmp, gam], [Sf])
                    self.cp("pool", Sb[:], Sf[:], [Sf], [Sb])
                    st = stat[c % 2]
                    rstd, nbias = self.ln_rows(lambda lo, hi, po=po: po[0:TB, lo:hi], TB, 512, [po], st)
                    o_n = on[c % 2]
                    self.act(o_n[0:TB, :], po[0:TB, :], AF.Identity, [po, st], [o_n], bias=nbias, scale=rstd)
                    self.tt("pool", o_n[0:TB, :], o_n[0:TB, :], lnp[0:TB, 0, :], ALU.mult, [o_n, lnp], [o_n])
                    self.tt("pool", o_n[0:TB, :], o_n[0:TB, :], lnp[0:TB, 1, :], ALU.add, [o_n, lnp], [o_n])
                    self.tt("dve", ogt[0:TB, c, :], o_n[0:TB, :], gs[0:TB, c, :], ALU.mult, [o_n, gs], [ogt])
                S.dma("pool", J.og_dst(t * T, T).rearrange("(nb p) e -> p nb e", p=TB), ogt[0:TB, :, :],
                      reads=[ogt], writes=[J.og_buf])
                if getattr(J, "after_tile", None) is not None:
                    J.after_tile(t)
            S.dma("pool", J.Sout.rearrange("(h p) e -> p h e", p=128), Sf[:], reads=[Sf], writes=[J.Sout_buf])
            S.barrier()
            S.release(bufs)

    def phaseB(self, J, C):
        nc, S, ps = self.nc, self.S, self.ps
        TB, NB = J.TB, J.NB
        T = TB * NB
        ntile = J.ntok // T
        KC = J.KC
        bufs = []
        with ExitStack() as es:
            sb = lambda n, s, d: self.sbuf(es, n, s, d, bufs)
            mixT = sb("BmixT", [128, KC, T], BF16)
            mtok = sb("Bmtok", [128, 2048], BF16) if J.tokmajor_in else None
            xres = sb("Bxres", [128, NB, 1024], F32)
            xmid = sb("Bxmid", [128, NB, 1024], F32)
            xmb = [sb("Bxmb", [128, 1024], BF16) for _ in range(2)]
            xmT = sb("BxmT", [128, 8, T], BF16)
            hid = sb("Bhid", [128, NFC, T], BF16)
            aext = [sb("Baext", [128, T + 2], F32) for _ in range(2)]
            u = [sb("Bu", [128, T], F32) for _ in range(2)]
            ge = [sb("Bge", [128, T], F32) for _ in range(2)]
            aprev = sb("Baprev", [128, NFC, 2], F32)
            wo = [sb("Bwo", [128, 8, 512], BF16) for _ in range(2)]
            wu = [sb("Bwu", [128, 8, 256], BF16) for _ in range(3)]
            wd = [sb("Bwd", [128, 11, 512], BF16) for _ in range(2)]
            lnt = sb("Blnt", [128, 4, 1024], F32)
            cvp = sb("Bcvp", [128, NFC, 4], F32)
            yout = [sb("Byout", [128, 1024], F32) for _ in range(2)]
            stat = [sb("Bstat", [128, 32], F32) for _ in range(2)]
            flag = sb("Bflag", [128, 1], F32)
            idx = sb("Bidx", [128, J.idx.t.shape[1]], I32) if J.idx is not None else None
            xhb_s = sb("Bxhb", [64, 1024], BF16) if J.halo else None

            for i in range(4):
                S.dma("sp", lnt[:, i, :], J.ln[i, :].partition_broadcast(128), reads=[J.ln_buf], writes=[lnt])
            S.dma("sp", cvp[:], J.convp, reads=[J.convp_buf], writes=[cvp])
            S.dma("sp", flag[:], C.flag.t[:], reads=[C.flag], writes=[flag])
            if idx is not None:
                S.dma("sp", idx[:], J.idx.t[:], reads=[J.idx], writes=[idx])
            if J.aprev0 is not None:
                S.dma("sp", aprev[:], J.aprev0, reads=[J.aprev0_buf], writes=[aprev])
            else:
                S.op("pool", lambda e: e.memset(aprev[:], 0.0), writes=[aprev])

            def run_tile(tb, nb_, t, halo):
                Tt = tb * nb_
                if J.tokmajor_in:
                    for nb in range(nb_):
                        J.load_mix_tok(self, t, nb, tb, halo, mtok, idx)
                        for kc in range(KC):
                            pb = ps[4 + (kc // 4) % 2]
                            pv = pb[:].bitcast(BF16)
                            q = kc % 4
                            self.tr(pv[:, q * 128:q * 128 + tb], mtok[0:tb, kc * 128:(kc + 1) * 128],
                                    C.identb[0:tb, 0:tb], [mtok, C.identb_buf], [pb], signal=(q == 3))
                            if q == 3:
                                k0 = kc - 3
                                self.cp("act" if (kc // 4) % 2 else "dve",
                                        mixT[:, k0:k0 + 4, nb * tb:(nb + 1) * tb],
                                        pv[:, 0:512].rearrange("p (q c) -> p q c", c=128)[:, :, 0:tb], [pb], [mixT])
                else:
                    J.load_mix_T(self, t, halo, mixT, idx, Tt, hid)
                if halo:
                    J.load_xres_halo(self, xres, idx, xhb_s)
                else:
                    S.dma("sp", xres[0:tb, 0:nb_, :],
                          J.xres[t * T:(t + 1) * T, :].rearrange("(nb p) f -> p nb f", p=tb),
                          reads=[J.xres_buf], writes=[xres])
                for half in range(2):
                    for g in range(KC // 8):
                        wslot = wo[(half * (KC // 8) + g) % 2]
                        S.dma("sp", wslot[:], J.w_out[half, :, g * 8:(g + 1) * 8, :], reads=[J.w_out_buf], writes=[wslot])
                        for k8 in range(8):
                            kc = g * 8 + k8
                            for nb in range(nb_):
                                self.mm(ps[nb][0:tb, :], mixT[:, kc, nb * tb:(nb + 1) * tb], wslot[:, k8, :],
                                        kc == 0, kc == KC - 1, [mixT, wslot], [ps[nb]])
                    for nb in range(nb_):
                        self.stt("dve", xmid[0:tb, nb, half * 512:(half + 1) * 512],
                                 xres[0:tb, nb, half * 512:(half + 1) * 512], ALPHA, ps[nb][0:tb, :],
                                 ALU.mult, ALU.add, [xres, ps[nb]], [xmid])
                for nb in range(nb_):
                    st = stat[nb % 2]
                    rstd, nbias = self.ln_rows(lambda lo, hi, nb=nb: xmid[0:tb, nb, lo:hi], tb, 1024, [xmid], st)
                    self.act(xmid[0:tb, nb, :], xmid[0:tb, nb, :], AF.Identity, [xmid, st], [xmid], bias=nbias,
                             scale=rstd)
                    self.tt("pool", xmid[0:tb, nb, :], xmid[0:tb, nb, :], lnt[0:tb, 0, :], ALU.mult, [xmid, lnt], [xmid])
                    xb_ = xmb[nb % 2]
                    self.tt("pool", xmid[0:tb, nb, :], xmid[0:tb, nb, :], lnt[0:tb, 1, :], ALU.add, [xmid, lnt], [xmid])
                    self.cp("pool", xb_[0:tb, :], xmid[0:tb, nb, :], [xmid], [xb_])
                    for kc in range(8):
                        pb = ps[4 + (kc // 4) % 2]
                        pv = pb[:].bitcast(BF16)
                        q = kc % 4
                        self.tr(pv[:, q * 128:q * 128 + tb], xb_[0:tb, kc * 128:(kc + 1) * 128], C.identb[0:tb, 0:tb],
                                [xb_, C.identb_buf], [pb], signal=(q == 3))
                        if q == 3:
                            k0 = kc - 3
                            self.cp("act" if (kc // 4) % 2 else "dve", xmT[:, k0:k0 + 4, nb * tb:(nb + 1) * tb],
                                    pv[:, 0:512].rearrange("p (q c) -> p q c", c=128)[:, :, 0:tb], [pb], [xmT])
                for j in range(NFC):
                    ws = wu[j % 3]
                    S.dma("sp", ws[:], J.w_up[j], reads=[J.w_up_buf], writes=[ws])
                    pg = ps[4 + 2 * (j % 2)]
                    pvv = ps[5 + 2 * (j % 2)]
                    for kc in range(8):
                        self.mm(pg[:, 0:Tt], ws[:, kc, 128:256], xmT[:, kc, 0:Tt], kc == 0, kc == 7, [ws, xmT], [pg])
                    ae = aext[j % 2]
                    self.cp("pool", ae[:, 0:2], aprev[:, j, :], [aprev], [ae])
                    self.cp("act", ae[:, 2:2 + Tt], pg[:, 0:Tt], [pg], [ae])
                    if halo:
                        self.ts("pool", aprev[:, j, :], ae[:, Tt:Tt + 2], flag[:, 0:1], ALU.mult, [ae, flag], [aprev])
                        continue
                    self.cp("pool", aprev[:, j, :], ae[:, Tt:Tt + 2], [ae], [aprev])
                    for kc in range(8):
                        self.mm(pvv[:, 0:Tt], ws[:, kc, 0:128], xmT[:, kc, 0:Tt], kc == 0, kc == 7, [ws, xmT], [pvv])
                    uu = u[j % 2]
                    self.act(uu[:, 0:Tt], pg[:, 0:Tt], AF.Identity, [pg, cvp], [uu], bias=cvp[:, j, 3:4],
                             scale=cvp[:, j, 2:3])
                    self.stt("dve", uu[:, 0:Tt], ae[:, 1:1 + Tt], cvp[:, j, 1:2], uu[:, 0:Tt], ALU.mult, ALU.add,
                             [ae, cvp, uu], [uu])
                    self.stt("dve", uu[:, 0:Tt], ae[:, 0:Tt], cvp[:, j, 0:1], uu[:, 0:Tt], ALU.mult, ALU.add,
                             [ae, cvp, uu], [uu])
                    gg = ge[j % 2]
                    self.act(gg[:, 0:Tt], uu[:, 0:Tt], AF.Gelu, [uu], [gg])
                    self.tt("dve", hid[:, j, 0:Tt], gg[:, 0:Tt], pvv[:, 0:Tt], ALU.mult, [gg, pvv], [hid])
                if halo:
                    return
                for half in range(2):
                    for g in range(2):
                        wslot = wd[(half * 2 + g) % 2]
                        S.dma("sp", wslot[:], J.w_down[half, :, g * 11:(g + 1) * 11, :], reads=[J.w_down_buf],
                              writes=[wslot])
                        for k11 in range(11):
                            kc = g * 11 + k11
                            for nb in range(nb_):
                                self.mm(ps[nb][0:tb, :], hid[:, kc, nb * tb:(nb + 1) * tb], wslot[:, k11, :],
                                        kc == 0, kc == NFC - 1, [hid, wslot], [ps[nb]])
                    for nb in range(nb_):
                        sl = xmid[0:tb, nb, half * 512:(half + 1) * 512]
                        self.stt("dve", sl, sl, ALPHA, ps[nb][0:tb, :], ALU.mult, ALU.add, [xmid, ps[nb]], [xmid])
                for nb in range(nb_):
                    st = stat[nb % 2]
                    yo = yout[nb % 2]
                    rstd, nbias = self.ln_rows(lambda lo, hi, nb=nb: xmid[0:tb, nb, lo:hi], tb, 1024, [xmid], st)
                    self.act(yo[0:tb, :], xmid[0:tb, nb, :], AF.Identity, [xmid, st], [yo], bias=nbias, scale=rstd)
                    self.tt("pool", yo[0:tb, :], yo[0:tb, :], lnt[0:tb, 2, :], ALU.mult, [yo, lnt], [yo])
                    self.tt("pool", yo[0:tb, :], yo[0:tb, :], lnt[0:tb, 3, :], ALU.add, [yo, lnt], [yo])
                    r0 = t * T + nb * tb
                    S.dma("pool", J.y[r0:r0 + tb, :], yo[0:tb, :], reads=[yo], writes=[J.y_buf])
                    if J.ybf is not None:
                        S.dma("pool", J.ybf[r0:r0 + tb, :], yo[0:tb, :], reads=[yo], writes=[J.ybf_buf])
                if getattr(J, "after_tile", None) is not None:
                    J.after_tile(t)

            if J.halo:
                run_tile(64, 1, None, True)
            for t in range(ntile):
                run_tile(TB, NB, t, False)
            S.dma("pool", J.conv_out, aprev[:], reads=[aprev], writes=[J.conv_out_buf])
            S.barrier()
            S.release(bufs)

    def phaseC1(self, J, C):
        nc, S, ps = self.nc, self.S, self.ps
        bufs = []
        with ExitStack() as es:
            sb = lambda n, s, d: self.sbuf(es, n, s, d, bufs)
            w = sb("Cw", [128, 8, 776], BF16)
            xtok = [sb("Cxtok", [128, 1024], BF16) for _ in range(2)]
            xT = sb("CxT", [128, 8, 128], BF16)
            kf = [sb("Ckf", [128, 256], F32) for _ in range(2)]
            vf = [sb("Cvf", [128, 256], F32) for _ in range(2)]
            qf = sb("Cqf", [128, 256], F32)
            sq = sb("Csq", [128, 256], F32)
            sm = [sb("Csm", [128, 64], F32) for _ in range(2)]
            smb = [sb("Csmb", [128, 32], BF16) for _ in range(2)]
            lf = [sb("Clf", [128, 4], F32) for _ in range(2)]
            carry = sb("Ccarry", [1, 4], F32)
            Kp = [sb("CKp", [128, 4, 70], BF16) for _ in range(2)]
            Qp = [sb("CQp", [128, 4, 70], BF16) for _ in range(2)]
            Vp = [sb("CVp", [128, 4, 65], BF16) for _ in range(2)]
            KT = [sb("CKT", [128, 4, 128], BF16) for _ in range(2)]
            QT = [sb("CQT", [128, 4, 128], BF16) for _ in range(2)]
            bfb = sb("Cbfb", [128, 4], F32)

            S.dma("pool", w[:], J.w.t[:], reads=[J.w], writes=[w])
            S.dma("sp", bfb[:], J.bf.partition_broadcast(128), reads=[J.bf_buf], writes=[bfb])
            S.op("pool", lambda e: e.memset(carry[:], 0.0), writes=[carry])
            import os
            DBG0 = int(os.environ.get("KC1", "0"))
            for i in range(2):
                if DBG0 & 16:
                    break
                S.op("pool", lambda e, i=i: e.memset(Kp[i][:, :, 67:70], 1.0), writes=[Kp[i]])
                S.op("pool", lambda e, i=i: e.memset(Qp[i][:, :, 64:67], 1.0), writes=[Qp[i]])

            def split3(src_ap, tb, dst_fn, smt, smbt, neg):
                self.cp("dve", smbt[0:tb, 0:4], src_ap, [smt], [smbt])
                self.cp("dve", smt[0:tb, 32:36], smbt[0:tb, 0:4], [smbt], [smt])
                self.tt("dve", smt[0:tb, 36:40], src_ap, smt[0:tb, 32:36], ALU.subtract, [smt], [smt])
                self.cp("dve", smbt[0:tb, 4:8], smt[0:tb, 36:40], [smt], [smbt])
                self.cp("dve", smt[0:tb, 40:44], smbt[0:tb, 4:8], [smbt], [smt])
                self.tt("dve", smt[0:tb, 44:48], smt[0:tb, 36:40], smt[0:tb, 40:44], ALU.subtract, [smt], [smt])
                self.cp("dve", smbt[0:tb, 8:12], smt[0:tb, 44:48], [smt], [smbt])
                for i in range(3):
                    dst, dbuf = dst_fn(i)
                    if neg:
                        self.ts("dve", dst, smbt[0:tb, 4 * i:4 * i + 4], -1.0, ALU.mult, [smbt], [dbuf])
                    else:
                        self.cp("dve", dst, smbt[0:tb, 4 * i:4 * i + 4], [smbt], [dbuf])

            import os
            DBG = int(os.environ.get("KC1", "0"))
            blk = 0
            for (kind, nblk, tb, src) in J.segments:
                if os.environ.get("KSEG", "") and kind != os.environ.get("KSEG"):
                    continue
                for bi in range(nblk):
                    i2 = blk % 2
                    kft, vft, lft, smt, smbt = kf[i2], vf[i2], lf[i2], sm[i2], smb[i2]
                    Kpt, Qpt, Vpt, KTt, QTt = Kp[i2], Qp[i2], Vp[i2], KT[i2], QT[i2]
                    r0 = bi * tb
                    if kind == "cache":
                        S.dma("sp", kft[0:tb, :], src.k[r0:r0 + tb, :], reads=[src.k_buf], writes=[kft])
                        S.dma("sp", vft[0:tb, :], src.v[r0:r0 + tb, :], reads=[src.v_buf], writes=[vft])
                        if not (DBG & 32):
                            S.dma("sp", lft[0:tb, :], src.lf[r0:r0 + tb, :], reads=[src.lf_buf], writes=[lft])
                    else:
                        xk = xtok[i2]
                        xr0 = src.xrow(r0) if getattr(src, "xrow", None) is not None else r0
                        S.dma("sp", xk[0:tb, :], src.x[xr0:xr0 + tb, :], reads=[src.x_buf], writes=[xk])
                        for kc in range(8):
                            pb = ps[kc // 4]
                            pv = pb[:].bitcast(BF16)
                            q = kc % 4
                            self.tr(pv[:, q * 128:q * 128 + tb], xk[0:tb, kc * 128:(kc + 1) * 128],
                                    C.identb[0:tb, 0:tb], [xk, C.identb_buf], [pb], signal=(q == 3))
                            if q == 3:
                                k0 = kc - 3
                                self.cp("act" if kc // 4 else "dve", xT[:, k0:k0 + 4, 0:tb],
                                        pv[:, 0:512].rearrange("p (q c) -> p q c", c=128)[:, :, 0:tb], [pb], [xT])
                        p1, p2 = ps[2], ps[3]
                        for kc in range(8):
                            self.mm(p1[0:tb, 0:512], xT[:, kc, 0:tb], w[:, kc, 0:512], kc == 0, kc == 7, [xT, w], [p1])
                        for kc in range(8):
                            self.mm(p2[0:tb, 0:260], xT[:, kc, 0:tb], w[:, kc, 512:772], kc == 0, kc == 7, [xT, w], [p2])
                        self.cp("act", qf[0:tb, :], p1[0:tb, 0:256], [p1], [qf])
                        self.cp("act", kft[0:tb, :], p1[0:tb, 256:512], [p1], [kft])
                        self.cp("act", vft[0:tb, :], p2[0:tb, 0:256], [p2], [vft])
                        if not (DBG & 64):
                            self.cp("act", smt[0:tb, 28:32], p2[0:tb, 256:260], [p2], [smt])
                            self.tt("dve", smt[0:tb, 0:4], smt[0:tb, 28:32], bfb[0:tb, :], ALU.add, [smt, bfb], [smt])
                        if not (DBG & 128):
                            self.act(smt[0:tb, 0:4], smt[0:tb, 0:4], AF.Exp, [smt], [smt], scale=-1.0)
                        if not (DBG & 256):
                            self.act(smt[0:tb, 0:4], smt[0:tb, 0:4], AF.Ln, [smt], [smt], bias=C.one_col[0:tb, 0:1])
                        self.ts("dve", lft[0:tb, :], smt[0:tb, 0:4], -1.0, ALU.mult, [smt], [lft])
                        g0 = src.out_row0 + r0
                        S.dma("pool", J.pk[g0:g0 + tb, :], kft[0:tb, :], reads=[kft], writes=[J.pk_buf])
                        S.dma("pool", J.pv[g0:g0 + tb, :], vft[0:tb, :], reads=[vft], writes=[J.pv_buf])
                        if not (DBG & 32):
                            S.dma("pool", J.plf[g0:g0 + tb, :], lft[0:tb, :], reads=[lft], writes=[J.plf_buf])
                    if DBG & 1:
                        blk += 1
                        continue
                    pc = ps[4]
                    self.mm(pc[0:tb, 0:4], C.tri[0:tb, 0:tb], lft[0:tb, :], True, False, [C.tri_buf, lft], [pc])
                    self.mm(pc[0:tb, 0:4], C.ones_f[0:1, 0:tb], carry[0:1, :], False, True, [C.ones_buf, carry], [pc])
                    pcar = ps[5]
                    self.mm(pcar[0:1, 0:4], C.ones_f[0:tb, 0:1], lft[0:tb, :], True, False, [C.ones_buf, lft], [pcar])
                    self.mm(pcar[0:1, 0:4], C.ones_f[0:1, 0:1], carry[0:1, :], False, True, [C.ones_buf, carry], [pcar])
                    self.cp("act", carry[0:1, :], pcar[0:1, 0:4], [pcar], [carry])
                    self.cp("act", smt[0:tb, 4:8], pc[0:tb, 0:4], [pc], [smt])
                    if DBG & 2:
                        blk += 1
                        continue
                    self.tt("pool", sq[0:tb, :], kft[0:tb, :], kft[0:tb, :], ALU.mult, [kft], [sq])
                    S.op("dve", lambda e, tb=tb, smt=smt: e.reduce_sum(
                        out=smt[0:tb, 8:12], in_=sq[0:tb, :].rearrange("p (h d) -> p h d", d=64), axis=AX.X),
                        reads=[sq], writes=[smt])
                    self.ts("dve", smt[0:tb, 8:12], smt[0:tb, 8:12], 1.0 / 16.0, ALU.mult, [smt], [smt])
                    self.act(smt[0:tb, 12:16], smt[0:tb, 8:12], AF.Exp, [smt], [smt])
                    self.tt("dve", smt[0:tb, 16:20], smt[0:tb, 4:8], smt[0:tb, 8:12], ALU.add, [smt], [smt])
                    split3(smt[0:tb, 16:20], tb, lambda i: (Kpt[0:tb, :, 64 + i], Kpt), smt, smbt, True)
                    self.cp("pool", Kpt[0:tb, :, 0:64], kft[0:tb, :].rearrange("p (h d) -> p h d", d=64), [kft], [Kpt])
                    for hh in range(4):
                        self.ts("pool" if hh % 2 else "dve", Vpt[0:tb, hh, 0:64], vft[0:tb, hh * 64:(hh + 1) * 64],
                                smt[0:tb, 12 + hh:13 + hh], ALU.mult, [vft, smt], [Vpt])
                    self.cp("dve", Vpt[0:tb, :, 64], smt[0:tb, 12:16], [smt], [Vpt])
                    if DBG & 4:
                        blk += 1
                        continue
                    k0 = J.kpos0[kind] + r0
                    S.dma("pool", J.VV[:, k0:k0 + tb, :].rearrange("h p e -> p h e"), Vpt[0:tb, :, :], reads=[Vpt],
                          writes=[J.VV_buf])
                    if DBG & 8:
                        blk += 1
                        continue
                    pk_ = ps[6]
                    pkv = pk_[:].bitcast(BF16)
                    for hh in range(4):
                        self.tr(pkv[0:70, hh * 128:hh * 128 + tb], Kpt[0:tb, hh, :], C.identb[0:tb, 0:tb],
                                [Kpt, C.identb_buf], [pk_], signal=(hh == 3))
                    self.cp("act", KTt[0:70, :, 0:tb], pkv[0:70, 0:512].rearrange("p (h c) -> p h c", c=128)[:, :, 0:tb],
                            [pk_], [KTt])
                    S.dma("pool", J.KT[:, :, k0:k0 + tb].rearrange("h p t -> p h t"), KTt[0:70, :, 0:tb], reads=[KTt],
                          writes=[J.KT_buf])
                    if kind == "new":
                        self.tt("pool", sq[0:tb, :], qf[0:tb, :], qf[0:tb, :], ALU.mult, [qf], [sq])
                        S.op("dve", lambda e, tb=tb, smt=smt: e.reduce_sum(
                            out=smt[0:tb, 20:24], in_=sq[0:tb, :].rearrange("p (h d) -> p h d", d=64), axis=AX.X),
                            reads=[sq], writes=[smt])
                        self.stt("dve", smt[0:tb, 24:28], smt[0:tb, 20:24], -1.0 / 16.0, smt[0:tb, 4:8], ALU.mult, ALU.add,
                                 [smt], [smt])
                        split3(smt[0:tb, 24:28], tb, lambda i: (Qpt[0:tb, :, 67 + i], Qpt), smt, smbt, False)
                        self.ts("pool", Qpt[0:tb, :, 0:64], qf[0:tb, :].rearrange("p (h d) -> p h d", d=64), 0.125,
                                ALU.mult, [qf], [Qpt])
                        pq_ = ps[7]
                        pqv = pq_[:].bitcast(BF16)
                        for hh in range(4):
                            self.tr(pqv[0:70, hh * 128:hh * 128 + tb], Qpt[0:tb, hh, :], C.identb[0:tb, 0:tb],
                                    [Qpt, C.identb_buf], [pq_], signal=(hh == 3))
                        self.cp("act", QTt[0:70, :, 0:tb],
                                pqv[0:70, 0:512].rearrange("p (h c) -> p h c", c=128)[:, :, 0:tb], [pq_], [QTt])
                        q0 = src.out_row0 + r0
                        S.dma("pool", J.QT[:, :, q0:q0 + tb].rearrange("h p t -> p h t"), QTt[0:70, :, 0:tb],
                              reads=[QTt], writes=[J.QT_buf])
                    blk += 1
            S.barrier()
            S.release(bufs)

    def phaseC2(self, J, C):
        nc, S, ps = self.nc, self.S, self.ps
        NK, NQ, TQ, TBq = J.NK, J.NQ, J.TQ, J.TBq
        ncache = J.ncache
        NKB = (NK + 127) // 128
        bufs = []
        with ExitStack() as es:
            sb = lambda n, s, d: self.sbuf(es, n, s, d, bufs)
            KTs = [sb("AtK", [128, NK], BF16) for _ in range(2)]
            Vs = [sb("AtV", [128, NKB, 65], BF16) for _ in range(2)]
            QTs = [sb("AtQ", [128, TQ], BF16) for _ in range(2)]
            PT = [sb("AtP", [128, TQ], BF16) for _ in range(3)]
            num = [sb("Atnum", [128, TQ], F32) for _ in range(2)]
            rd = [sb("Atrd", [128, TQ], F32) for _ in range(2)]
            oT = [sb("AtoT", [128, TQ], BF16) for _ in range(2)]
            nqt = NQ // TQ
            nsub = TQ // TBq
            it = 0
            for hh in range(4):
                Kt, Vt = KTs[hh % 2], Vs[hh % 2]
                S.dma("sp", Kt[0:70, :], J.KT[hh], reads=[J.KT_buf], writes=[Kt])
                nfull = NK // 128
                if nfull:
                    S.dma("sp", Vt[:, 0:nfull, :], J.VV[hh, 0:nfull * 128, :].rearrange("(b p) e -> p b e", p=128),
                          reads=[J.VV_buf], writes=[Vt])
                if NK % 128:
                    S.dma("sp", Vt[0:NK % 128, nfull, :], J.VV[hh, nfull * 128:NK, :], reads=[J.VV_buf], writes=[Vt])
                for qt in range(nqt):
                    Qt = QTs[(hh * nqt + qt) % 2]
                    S.dma("sp", Qt[0:70, :], J.QT[hh, :, qt * TQ:(qt + 1) * TQ], reads=[J.QT_buf], writes=[Qt])
                    po = ps[4 + (hh * nqt + qt) % 2]
                    kblocks = [(kb * 128, 128, None) for kb in range(ncache // 128)]
                    for u_all in range((qt + 1) * nsub):
                        k0 = ncache + u_all * TBq
                        u = u_all - qt * nsub
                        kblocks.append((k0, TBq, u if u >= 0 else None))
                    for bi, (k0, kb, u) in enumerate(kblocks):
                        c0 = 0 if u is None else u * TBq
                        pss = ps[it % 3]
                        pt = PT[it % 3]
                        it += 1
                        self.mm(pss[0:kb, c0:TQ], Kt[0:70, k0:k0 + kb], Qt[0:70, c0:TQ], True, u is None, [Kt, Qt], [pss])
                        if u is not None:
                            self.mm(pss[0:kb, c0:c0 + TBq], C.identb[0:kb, 0:kb], C.negm[0:kb, 0:TBq], False, True,
                                    [C.identb_buf, C.negm_buf], [pss])
                        self.act(pt[0:kb, c0:TQ], pss[0:kb, c0:TQ], AF.Exp, [pss], [pt])
                        self.mm(po[0:65, c0:TQ], Vt[0:kb, k0 // 128, :], pt[0:kb, c0:TQ], bi == 0, bi == len(kblocks) - 1,
                                [Vt, pt], [po])
                    i2 = (hh * nqt + qt) % 2
                    S.op("dve", lambda e, i2=i2, po=po: e.reciprocal(out=rd[i2][64:65, :], in_=po[64:65, 0:TQ]),
                         reads=[po], writes=[rd[i2]])
                    self.cp("act", num[i2][0:64, :], po[0:64, 0:TQ], [po], [num[i2]])
                    pbc = ps[6 + i2]
                    self.mm(pbc[0:64, 0:TQ], C.ones_f[64:65, 0:64], rd[i2][64:65, :], True, True, [C.ones_buf, rd[i2]], [pbc])
                    self.tt("dve", oT[i2][0:64, :], num[i2][0:64, :], pbc[0:64, 0:TQ], ALU.mult, [num[i2], pbc], [oT[i2]])
                    S.dma("pool", J.oT_dst(hh, qt), oT[i2][0:64, :], reads=[oT[i2]], writes=[J.oT_buf])
                    if getattr(J, "after_unit", None) is not None:
                        J.after_unit(hh * nqt + qt)
            S.barrier()
            S.release(bufs)


def build_program(SEQ, stop=99):
    B = Builder(SEQ)
    nc, S = B.nc, B.S
    SEG = SEQ // 4
    NQT = SEQ // 512
    NTB_B = SEG // 128
    ntile_B = SEG // 512

    xA = B.din("xA", [SEQ, D])
    xB = B.din("xB", [SEG, D])
    xH = B.din("xH", [64, D])
    xS = B.din("xS", [NS, D])
    w_inP = B.din("w_inP", [128, 8, 1536])
    w_inS = [B.din("w_inS%d" % h, [128, 8, 1536]) for h in range(4)]
    ropeP = B.din("ropeP", [4, 128, SEQ])
    ropeS = B.din("ropeS", [4, 4, 128, NS])
    gamP = B.din("gamP", [128, 1])
    gamS = B.din("gamS", [4, 128, 1])
    lnretP = B.din("lnretP", [2, 512])
    lnretS = B.din("lnretS", [4, 2, 512])
    sret = B.din("sret", [4, 256, 512])
    w_outA = B.din("w_outA", [2, 128, 16, 512])
    w_up = B.din("w_up", [2, NFC, 128, 8, 256])
    w_down = B.din("w_down", [2, 2, 128, NFC, 512])
    convp = B.din("convp", [2, 128, NFC, 4])
    ln4 = B.din("ln4", [2, 4, D])
    sconv = B.din("sconv", [2, 128, NFC, 2])
    w_qkvP = B.din("w_qkvP", [128, 8, 776])
    w_qkvS = [B.din("w_qkvS%d" % g, [128, 8, 776]) for g in range(4)]
    bfP = B.din("bfP", [1, 4])
    bfS = B.din("bfS", [4, 4])
    w_outB = B.din("w_outB", [2, 128, 8, 512])
    ck = B.din("ck", [PAST, D])
    cv = B.din("cv", [PAST, D])
    clf = B.din("clf", [PAST, 16])
    flag_in = B.din("flag", [128, 1])
    idxB_in = B.din("idxB", [128, (NTB_B + 1) * 4], I32)
    idxD_in = B.din("idxD", [128, (ntile_B + 1) * 8 + 1], I32)
    cst = B.din("cst", [5, 128, 128])

    yP = B.dout("yP", [SEG, D])
    yS = B.dout("yS", [NS, D])
    retP = B.dout("retP", [256, 512])
    retS = B.dout("retS", [4, 256, 512])
    pk = B.dout("pk", [SEQ, 256])
    pv = B.dout("pv", [SEQ, 256])
    plf = B.dout("plf", [SEQ, 4])
    sk = B.dout("sk", [4, NS, 256])
    sv = B.dout("sv", [4, NS, 256])
    slf = B.dout("slf", [4, NS, 4])
    pconv = B.dout("pconv", [2, 128, NFC, 2])
    sconv_o = B.dout("sconv_o", [2, 128, NFC, 2])

    og_loc = B.dscr("og_loc", [SEQ, 512], BF16)
    og_all = B.dscr("og_all", [4 * SEQ, 512], BF16)
    og_s = B.dscr("og_s", [4 * NS, 512], BF16)
    x1_loc = B.dscr("x1_loc", [SEG, D], F32)
    x1_bf = B.dscr("x1_bf", [SEG, D], BF16)
    x1_all = B.dscr("x1_all", [SEQ, D], BF16)
    x1_s = B.dscr("x1_s", [NS, D], F32)
    x1_sb = B.dscr("x1_sb", [NS, D], BF16)
    KTp = B.dscr("KTp", [4, 70, SEQ], BF16)
    VVp = B.dscr("VVp", [4, SEQ, 65], BF16)
    QTp = B.dscr("QTp", [4, 70, SEQ], BF16)
    NKS = PAST + NS
    KTs_ = B.dscr("KTs", [4, 70, NKS], BF16)
    VVs_ = B.dscr("VVs", [4, NKS, 65], BF16)
    QTs_ = B.dscr("QTs", [4, 70, NS], BF16)
    oT_loc = B.dscr("oT_loc", [4 * NQT * 64, 512], BF16)
    oT_all = B.dscr("oT_all", [4 * 4 * NQT * 64, 512], BF16)
    oT_s = B.dscr("oT_s", [16 * 64, NS], BF16)
    wb_outA = B.dscr("wb_outA", [2, 128, 16, 512], BF16)
    wb_up = B.dscr("wb_up", [2, NFC, 128, 8, 256], BF16)
    wb_down = B.dscr("wb_down", [2, 2, 128, NFC, 512], BF16)
    wb_outB = B.dscr("wb_outB", [2, 128, 8, 512], BF16)

    C = Job()
    cf = nc.alloc_sbuf_tensor("c_f32", [128, 5, 128], F32)
    cb = nc.alloc_sbuf_tensor("c_bf", [128, 5, 128], BF16)
    cfb, cbb = Buf("c_f32", cf), Buf("c_bf", cb)
    S.dma("sp", cf[:], cst.t[:].rearrange("a p n -> p a n"), reads=[cst], writes=[cfb])
    S.dma("pool", cb[:], cst.t[:].rearrange("a p n -> p a n"), reads=[cst], writes=[cbb])
    C.identb, C.identb_buf = cb[:, 0, :], cbb
    C.maskT, C.maskT_buf = cf[:, 1, :], cfb
    C.tri, C.tri_buf = cf[:, 2, :], cfb
    C.negm, C.negm_buf = cb[:, 3, :], cbb
    C.ones_f, C.ones_buf = cf[:, 4, :], cfb
    C.one_col = cf[:, 4, 0:1]
    C.flag = flag_in

    for l in range(2):
        for h in range(2):
            for pg in range(0, 128, 32):
                if l == 0:
                    S.dma("pool", wb_outA.t[h, pg:pg + 32], w_outA.t[h, pg:pg + 32], reads=[w_outA], writes=[wb_outA])
                else:
                    S.dma("pool", wb_outB.t[h, pg:pg + 32], w_outB.t[h, pg:pg + 32], reads=[w_outB], writes=[wb_outB])
        for j in range(NFC):
            S.dma("pool", wb_up.t[l, j], w_up.t[l, j], reads=[w_up], writes=[wb_up])
        for h in range(2):
            for pg in range(0, 128, 32):
                S.dma("pool", wb_down.t[l, h, pg:pg + 32], w_down.t[l, h, pg:pg + 32], reads=[w_down],
                      writes=[wb_down])

    if stop <= 0:
        S.final_wait()
        return B
    def gmap(rho, h, CR):
        return (rho // CR) * 4 * CR + h * CR + (rho % CR)

    def exch_chunk(src, dst, i, CR):
        S.collective("AllGather", src, dst, src.t[i * CR:(i + 1) * CR, :], dst.t[i * 4 * CR:(i + 1) * 4 * CR, :])

    JA = Job(TB=128, NB=4, ntok=SEQ, x=xA.t, x_buf=xA, w=w_inP, rope=ropeP.t, rope_buf=ropeP,
             gam=gamP.t[:], gam_buf=gamP, lng=lnretP.t[0, :], lnb=lnretP.t[1, :], lnb_buf=lnretP,
             S0=None, S0_buf=None, Sout=retP.t[:], Sout_buf=retP,
             og_dst=lambda r0, n: og_loc.t[r0:r0 + n, :], og_buf=og_loc,
             after_tile=lambda t: exch_chunk(og_loc, og_all, t // 2, 1024) if t % 2 == 1 else None)
    B.phaseA(JA, C)
    if stop <= 1:
        S.final_wait()
        return B
    pass

    if stop <= 2:
        S.final_wait()
        return B
    for h in range(4):
        JS = Job(TB=NS, NB=1, ntok=NS, x=xS.t, x_buf=xS, w=w_inS[h], rope=ropeS.t[h], rope_buf=ropeS,
                 gam=gamS.t[h], gam_buf=gamS, lng=lnretS.t[h, 0, :], lnb=lnretS.t[h, 1, :], lnb_buf=lnretS,
                 S0=sret.t[h], S0_buf=sret, Sout=retS.t[h], Sout_buf=retS,
                 og_dst=lambda r0, n, h=h: og_s.t[h * NS + r0:h * NS + r0 + n, :], og_buf=og_s)
        B.phaseA(JS, C)

    if stop <= 3:
        S.final_wait()
        return B
    def load_mix_tok_sample(Bd, t, nb, tb, halo, mtok, idx):
        for h in range(4):
            Bd.S.dma("sp", mtok[0:tb, h * 512:(h + 1) * 512], og_s.t[h * NS:(h + 1) * NS, :], reads=[og_s],
                     writes=[mtok])

    JBs = Job(TB=NS, NB=1, ntok=NS, KC=16, tokmajor_in=True, load_mix_tok=load_mix_tok_sample, halo=False,
              xres=xS.t, xres_buf=xS, w_out=wb_outA.t, w_out_buf=wb_outA, w_up=wb_up.t[0], w_up_buf=wb_up,
              w_down=wb_down.t[0], w_down_buf=wb_down, convp=convp.t[0], convp_buf=convp, ln=ln4.t[0], ln_buf=ln4,
              aprev0=sconv.t[0], aprev0_buf=sconv, idx=None, y=x1_s.t, y_buf=x1_s, ybf=x1_sb.t, ybf_buf=x1_sb,
              conv_out=sconv_o.t[0], conv_out_buf=sconv_o)
    B.phaseB(JBs, C)

    if stop <= 4:
        S.final_wait()
        return B
    def load_mix_tok_prompt(Bd, t, nb, tb, halo, mtok, idx):
        col0 = NTB_B * 4 if halo else (t * 4 + nb) * 4
        for h in range(4):
            Bd.S.dma("pool", mtok[0:tb, h * 512:(h + 1) * 512], og_all.t[:, :], reads=[og_all, idx], writes=[mtok],
                     indirect=idx[0:tb, col0 + h:col0 + h + 1])

    def load_xres_halo_B(Bd, xres, idx, xhb_s):
        Bd.S.dma("sp", xres[0:64, 0, :], xH.t[:, :], reads=[xH], writes=[xres])

    JBp = Job(TB=128, NB=4, ntok=SEG, KC=16, tokmajor_in=True, load_mix_tok=load_mix_tok_prompt, halo=True,
              load_xres_halo=load_xres_halo_B,
              xres=xB.t, xres_buf=xB, w_out=wb_outA.t, w_out_buf=wb_outA, w_up=wb_up.t[0], w_up_buf=wb_up,
              w_down=wb_down.t[0], w_down_buf=wb_down, convp=convp.t[0], convp_buf=convp, ln=ln4.t[0], ln_buf=ln4,
              aprev0=None, aprev0_buf=None, idx=idxB_in, y=x1_loc.t, y_buf=x1_loc, ybf=x1_bf.t, ybf_buf=x1_bf,
              conv_out=pconv.t[0], conv_out_buf=pconv,
              after_tile=lambda t: exch_chunk(x1_bf, x1_all, t, 512))
    B.phaseB(JBp, C)
    if stop <= 5:
        S.final_wait()
        return B
    pass

    if stop <= 6:
        S.final_wait()
        return B
    for g in range(4):
        srcc = Job(k=ck.t[:, g * 256:(g + 1) * 256], k_buf=ck, v=cv.t[:, g * 256:(g + 1) * 256], v_buf=cv,
                   lf=clf.t[:, g * 4:(g + 1) * 4], lf_buf=clf)
        srcn = Job(x=x1_sb.t, x_buf=x1_sb, out_row0=0)
        JC = Job(w=w_qkvS[g], bf=bfS.t[g, :], bf_buf=bfS,
                 segments=[("cache", PAST // 128, 128, srcc), ("new", 1, NS, srcn)],
                 kpos0={"cache": 0, "new": PAST}, pk=sk.t[g], pk_buf=sk, pv=sv.t[g], pv_buf=sv, plf=slf.t[g],
                 plf_buf=slf, KT=KTs_.t, KT_buf=KTs_, VV=VVs_.t, VV_buf=VVs_, QT=QTs_.t, QT_buf=QTs_)
        B.phaseC1(JC, C)
        if stop == 61:
            S.final_wait()
            return B
        JC2 = Job(NK=NKS, NQ=NS, TQ=NS, TBq=NS, ncache=PAST, KT=KTs_.t, KT_buf=KTs_, VV=VVs_.t, VV_buf=VVs_,
                  QT=QTs_.t, QT_buf=QTs_,
                  oT_dst=lambda hh, qt, g=g: oT_s.t[(g * 4 + hh) * 64:(g * 4 + hh + 1) * 64, :], oT_buf=oT_s)
        B.phaseC2(JC2, C)

    if stop <= 7:
        S.final_wait()
        return B
    def load_mix_T_sample(Bd, t, halo, mixT, idx, Tt, hid):
        Bd.S.dma("sp", mixT[:, :, 0:NS], oT_s.t[:, :].rearrange("(kc p) t -> p kc t", p=128), reads=[oT_s],
                 writes=[mixT])

    JDs = Job(TB=NS, NB=1, ntok=NS, KC=8, tokmajor_in=False, load_mix_T=load_mix_T_sample, halo=False,
              xres=x1_s.t, xres_buf=x1_s, w_out=wb_outB.t, w_out_buf=wb_outB, w_up=wb_up.t[1], w_up_buf=wb_up,
              w_down=wb_down.t[1], w_down_buf=wb_down, convp=convp.t[1], convp_buf=convp, ln=ln4.t[1], ln_buf=ln4,
              aprev0=sconv.t[1], aprev0_buf=sconv, idx=None, y=yS.t, y_buf=yS, ybf=None, ybf_buf=None,
              conv_out=sconv_o.t[1], conv_out_buf=sconv_o)
    B.phaseB(JDs, C)

    if stop <= 8:
        S.final_wait()
        return B
    srcn = Job(x=x1_all.t, x_buf=x1_all, out_row0=0, xrow=lambda tok: gmap(tok % SEG, tok // SEG, 512))
    JCp = Job(w=w_qkvP, bf=bfP.t[0, :], bf_buf=bfP, segments=[("new", SEQ // 128, 128, srcn)],
              kpos0={"new": 0}, pk=pk.t, pk_buf=pk, pv=pv.t, pv_buf=pv, plf=plf.t, plf_buf=plf,
              KT=KTp.t, KT_buf=KTp, VV=VVp.t, VV_buf=VVp, QT=QTp.t, QT_buf=QTp)
    B.phaseC1(JCp, C)
    JC2p = Job(NK=SEQ, NQ=SEQ, TQ=512, TBq=128, ncache=0, KT=KTp.t, KT_buf=KTp, VV=VVp.t, VV_buf=VVp, QT=QTp.t,
               QT_buf=QTp, oT_dst=lambda hh, qt: oT_loc.t[(hh * NQT + qt) * 64:(hh * NQT + qt + 1) * 64, :],
               oT_buf=oT_loc,
               after_unit=lambda u: exch_chunk(oT_loc, oT_all, u // 16, 1024) if u % 16 == 15 else None)
    B.phaseC2(JC2p, C)
    if stop <= 9:
        S.final_wait()
        return B
    pass

    if stop <= 10:
        S.final_wait()
        return B
    def load_mix_T_prompt(Bd, t, halo, mixT, idx, Tt, hid):
        if halo:
            for kc in range(8):
                Bd.S.dma("pool", hid[:, kc, :], oT_all.t[:, :], reads=[oT_all, idx], writes=[hid],
                         indirect=idx[:, ntile_B * 8 + kc:ntile_B * 8 + kc + 1])
            Bd.cp("dve", mixT[:, :, 0:64], hid[:, 0:8, 448:512], [hid], [mixT])
        else:
            for kc in range(8):
                Bd.S.dma("pool", mixT[:, kc, :], oT_all.t[:, :], reads=[oT_all, idx], writes=[mixT],
                         indirect=idx[:, t * 8 + kc:t * 8 + kc + 1])

    def load_xres_halo_D(Bd, xres, idx, xhb_s):
        Bd.S.dma("pool", xhb_s[0:64, :], x1_all.t[:, :], reads=[x1_all, idx], writes=[xhb_s],
                 indirect=idx[0:64, (ntile_B + 1) * 8:(ntile_B + 1) * 8 + 1])
        Bd.cp("dve", xres[0:64, 0, :], xhb_s[0:64, :], [xhb_s], [xres])

    JDp = Job(TB=128, NB=4, ntok=SEG, KC=8, tokmajor_in=False, load_mix_T=load_mix_T_prompt, halo=True,
              load_xres_halo=load_xres_halo_D,
              xres=x1_loc.t, xres_buf=x1_loc, w_out=wb_outB.t, w_out_buf=wb_outB, w_up=wb_up.t[1], w_up_buf=wb_up,
              w_down=wb_down.t[1], w_down_buf=wb_down, convp=convp.t[1], convp_buf=convp, ln=ln4.t[1], ln_buf=ln4,
              aprev0=None, aprev0_buf=None, idx=idxD_in, y=yP.t, y_buf=yP, ybf=None, ybf_buf=None,
              conv_out=pconv.t[1], conv_out_buf=pconv)
    B.phaseB(JDp, C)
    S.final_wait()
    return B


def _kc_layout(wm):
    K, N = wm.shape
    return np.ascontiguousarray(wm.reshape(K // 128, 128, N).transpose(1, 0, 2))


def _rope_tables(pos, chunk, h):
    half = 128
    inv = (1.0 / (10000.0 ** (np.arange(half, dtype=np.float32) / np.float32(half)))).astype(np.float32)
    ang = pos.astype(np.float32)[:, None] * inv[None, :]
    cos = np.cos(ang).astype(np.float32).T
    sin = np.sin(ang).astype(np.float32).T
    lg = np.log1p(-np.exp2(-5.0 - h))
    i = (np.arange(len(pos)) % chunk).astype(np.float64)
    dq = np.exp((i + 1.0) * lg).astype(np.float32)[None, :]
    dk = (np.exp(-(i + 1.0) * lg) * (256.0 ** -0.5)).astype(np.float32)[None, :]
    tab = np.stack([cos * dq, sin * dq, cos * dk, sin * dk]).astype(np.float32)
    gam = np.float32(np.exp(chunk * lg))
    return tab, gam


def _gmap(rho, h, CR):
    return (rho // CR) * 4 * CR + h * CR + (rho % CR)


def _consts():
    j = np.arange(128)[:, None]
    i = np.arange(128)[None, :]
    ident = (i == j).astype(np.float32)
    maskT = (i >= j).astype(np.float32)
    tri = (j <= i).astype(np.float32)
    negm = np.where(i >= j, 0.0, NEGM).astype(np.float32)
    ones = np.ones((128, 128), np.float32)
    return np.stack([ident, maskT, tri, negm, ones])


def make_in_maps(inp, SEQ):
    f = lambda a: np.ascontiguousarray(np.asarray(a, dtype=np.float32))
    SEG = SEQ // 4
    NQT = SEQ // 512
    NTB_B = SEG // 128
    ntile_B = SEG // 512
    xp, xs = f(inp["x_prompt"]), f(inp["x_sample"])
    w_in = f(inp["w_in_a"])[0]
    HK, HV = 1024, 2048
    lnrg, lnrb = f(inp["ln_ret_g"])[0], f(inp["ln_ret_b"])[0]
    w_kvf, w_q = f(inp["w_kvf"]), f(inp["w_q_b"])[0]
    b_f = f(inp["b_f"])

    def w_in_head(h):
        cols = np.concatenate([np.arange(h * 256, (h + 1) * 256), HK + np.arange(h * 256, (h + 1) * 256),
                               2 * HK + np.arange(h * 512, (h + 1) * 512),
                               2 * HK + HV + np.arange(h * 512, (h + 1) * 512)])
        return _kc_layout(w_in[:, cols])

    def w_qkv_group(g):
        c = np.arange(g * 256, (g + 1) * 256)
        return _kc_layout(np.concatenate([w_q[:, c], w_kvf[:, c], w_kvf[:, 1024 + c],
                                          w_kvf[:, 2048 + g * 4:2048 + (g + 1) * 4],
                                          np.zeros((1024, 4), np.float32)], axis=1))

    w_in_heads = [w_in_head(h) for h in range(4)]
    w_qkv_groups = [w_qkv_group(g) for g in range(4)]
    ropeS = np.stack([_rope_tables(PAST + np.arange(NS), NS, h)[0] for h in range(4)])
    gamS = np.stack([np.full((128, 1), _rope_tables(np.arange(1), NS, h)[1], np.float32) for h in range(4)])
    lnretS = np.stack([np.stack([lnrg[h * 512:(h + 1) * 512], lnrb[h * 512:(h + 1) * 512]]) for h in range(4)])
    w_outA = f(inp["w_out_a"])[0]
    w_outA_l = np.stack([_kc_layout(w_outA[:, hf * 512:(hf + 1) * 512]) for hf in range(2)])
    w_outB = f(inp["w_out_b"])[0]
    w_outB_l = np.stack([_kc_layout(w_outB[:, hf * 512:(hf + 1) * 512]) for hf in range(2)])
    wu = f(inp["w_up"])
    w_up_l = np.empty((2, NFC, 128, 8, 256), np.float32)
    for l in range(2):
        for j in range(NFC):
            w_up_l[l, j, :, :, 0:128] = _kc_layout(wu[l][:, j * 128:(j + 1) * 128])
            w_up_l[l, j, :, :, 128:256] = _kc_layout(wu[l][:, DFF + j * 128:DFF + (j + 1) * 128])
    wdn = f(inp["w_down"])
    w_down_l = np.stack([np.stack([_kc_layout(wdn[l][:, hf * 512:(hf + 1) * 512]) for hf in range(2)])
                         for l in range(2)])
    cw, cbias = f(inp["conv_w"]), f(inp["conv_b"])
    convp = np.empty((2, 128, NFC, 4), np.float32)
    for l in range(2):
        for k in range(3):
            convp[l, :, :, k] = cw[l, k].reshape(NFC, 128).T
        convp[l, :, :, 3] = cbias[l].reshape(NFC, 128).T
    ln4 = np.stack([np.stack([f(inp["ln_mix_g"])[l], f(inp["ln_mix_b"])[l], f(inp["ln_ffn_g"])[l],
                              f(inp["ln_ffn_b"])[l]]) for l in range(2)])
    sfc = f(inp["state_ffn_conv"])
    cst = _consts()
    ckk, cvv, clf = f(inp["cache_k"]), f(inp["cache_v"]), f(inp["cache_logf"])
    sret = f(inp["state_ret"])[0]
    maps = []
    for c in range(8):
        b, r = c // 4, c % 4
        m = {}
        m["xA"] = xp[b]
        m["xB"] = np.ascontiguousarray(xp[b, r * SEG:(r + 1) * SEG])
        m["xH"] = np.ascontiguousarray(xp[b, r * SEG - 64:r * SEG]) if r > 0 else np.zeros((64, D), np.float32)
        m["xS"] = xs[c]
        m["w_inP"] = w_in_heads[r]
        for h in range(4):
            m["w_inS%d" % h] = w_in_heads[h]
        tab, gam = _rope_tables(np.arange(SEQ), 128, r)
        m["ropeP"] = tab
        m["ropeS"] = ropeS
        m["gamP"] = np.full((128, 1), gam, np.float32)
        m["gamS"] = gamS
        m["lnretP"] = lnretS[r]
        m["lnretS"] = lnretS
        m["sret"] = sret[c]
        m["w_outA"] = w_outA_l
        m["w_up"] = w_up_l
        m["w_down"] = w_down_l
        m["convp"] = convp
        m["ln4"] = ln4
        m["sconv"] = np.ascontiguousarray(
            np.stack([sfc[l, c].reshape(2, NFC, 128).transpose(2, 1, 0) for l in range(2)]))
        m["w_qkvP"] = w_qkv_groups[r]
        for g in range(4):
            m["w_qkvS%d" % g] = w_qkv_groups[g]
        m["bfP"] = b_f[r * 4:(r + 1) * 4].reshape(1, 4)
        m["bfS"] = b_f.reshape(4, 4)
        m["w_outB"] = w_outB_l
        m["ck"] = ckk[c].reshape(PAST, D)
        m["cv"] = cvv[c].reshape(PAST, D)
        m["clf"] = clf[c]
        m["flag"] = np.full((128, 1), 1.0 if r > 0 else 0.0, np.float32)
        p = np.arange(128)
        idxB = np.zeros((128, (NTB_B + 1) * 4), np.int32)
        for tbi in range(NTB_B):
            for h in range(4):
                idxB[:, tbi * 4 + h] = _gmap(r * SEG + tbi * 128 + p, h, 1024)
        h0 = max(r * SEG - 64, 0)
        for h in range(4):
            idxB[:, NTB_B * 4 + h] = _gmap(h0 + (p % 64), h, 1024)
        m["idxB"] = idxB
        idxD = np.zeros((128, (ntile_B + 1) * 8 + 1), np.int32)
        R = 4 * NQT * 64
        for t in range(ntile_B + 1):
            qt = r * ntile_B + t if t < ntile_B else max(r * ntile_B - 1, 0)
            for kc in range(8):
                g = kc // 2
                hh = 2 * (kc % 2) + p // 64
                idxD[:, t * 8 + kc] = _gmap((hh * NQT + qt) * 64 + (p % 64), g, 1024)
        tokh = h0 + (p % 64)
        idxD[:, (ntile_B + 1) * 8] = _gmap(tokh % SEG, tokh // SEG, 512)
        m["idxD"] = idxD
        m["cst"] = cst
        maps.append({k: np.ascontiguousarray(v) for k, v in m.items()})
    return maps


def assemble(res, SEQ, Bp=2):
    SEG = SEQ // 4
    R = [r for r in res]
    g = lambda c, k: np.asarray(R[c][k], dtype=np.float32)
    y_p = np.stack([np.concatenate([g(b * 4 + r, "yP") for r in range(4)], 0) for b in range(Bp)])
    y_s = np.stack([g(c, "yS") for c in range(8)])
    ret_p = np.stack([np.stack([g(b * 4 + r, "retP") for r in range(4)]) for b in range(Bp)])[None]
    k_p = np.stack([np.concatenate([g(b * 4 + r, "pk").reshape(SEQ, 4, 64) for r in range(4)], 1) for b in range(Bp)])
    v_p = np.stack([np.concatenate([g(b * 4 + r, "pv").reshape(SEQ, 4, 64) for r in range(4)], 1) for b in range(Bp)])
    lf_p = np.stack([np.concatenate([g(b * 4 + r, "plf") for r in range(4)], 1) for b in range(Bp)])

    def conv_fix(a):
        return a.transpose(2, 1, 0).reshape(2, DFF)

    conv_p = np.stack([np.stack([conv_fix(g(b * 4 + 3, "pconv")[l]) for b in range(Bp)]) for l in range(2)])
    ret_s = np.stack([g(c, "retS") for c in range(8)])[None]
    k_s = np.stack([g(c, "sk").transpose(1, 0, 2).reshape(NS, 16, 64) for c in range(8)])
    v_s = np.stack([g(c, "sv").transpose(1, 0, 2).reshape(NS, 16, 64) for c in range(8)])
    lf_s = np.stack([g(c, "slf").transpose(1, 0, 2).reshape(NS, 16) for c in range(8)])
    conv_s = np.stack([np.stack([conv_fix(g(c, "sconv_o")[l]) for c in range(8)]) for l in range(2)])
    return (y_p, y_s, ret_p, k_p, v_p, lf_p, conv_p, ret_s, k_s, v_s, lf_s, conv_s)


_CACHE = {}


def kernel(**inputs):
    SEQ = int(np.asarray(inputs["x_prompt"]).shape[1])
    if SEQ not in _CACHE:
        import os
        _CACHE[SEQ] = build_program(SEQ, int(os.environ.get("KSTOP", "99")))
    B = _CACHE[SEQ]
    in_maps = make_in_maps(inputs, SEQ)
    import os
    if os.environ.get("KTRACE"):
        res = run_bass_kernel_spmd(B.nc, in_maps, core_ids=list(range(8)), trace=True)
        print("EXEC_TIME_NS", res.exec_time_ns, flush=True)
    else:
        res = run_bass_kernel_spmd(B.nc, in_maps, core_ids=list(range(8)))
    return assemble(res.results, SEQ)
```

```python
import math
from contextlib import ExitStack
import numpy as np
import ml_dtypes
import concourse.bass as bass
import concourse.mybir as mybir
from concourse.bass_utils import run_bass_kernel_spmd

F32 = mybir.dt.float32
BF16 = mybir.dt.bfloat16
I32 = mybir.dt.int32
AF = mybir.ActivationFunctionType
ALU = mybir.AluOpType
AX = mybir.AxisListType

D = 1024
DFF = 2816
NFC = 22
PAST = 2048
NS = 64
LN_EPS = 1e-5
ALPHA = 4.0 ** 0.25
GROUPS = [[0, 1, 2, 3], [4, 5, 6, 7]]
NEGM = -30000.0


class Buf:
    __slots__ = ("name", "t", "wtok", "rtok", "dsem", "dcnt")

    def __init__(self, name, t):
        self.name = name
        self.t = t
        self.wtok = {}
        self.rtok = {}
        self.dsem = None
        self.dcnt = 0

    def __getitem__(self, k):
        return self.t[k]


class Sched:
    ENG = ("pe", "act", "dve", "pool", "sp")

    def __init__(self, nc):
        self.nc = nc
        self.e = {"pe": nc.tensor, "act": nc.scalar, "dve": nc.vector, "pool": nc.gpsimd, "sp": nc.sync}
        self.sem = {k: nc.alloc_semaphore("prog_" + k) for k in self.ENG}
        self.cnt = {k: 0 for k in self.ENG}
        self.seen = {k: {} for k in self.ENG}
        self.cc_sem = nc.alloc_semaphore("cc_sem")
        self.cc_cnt = 0
        self.dma_bufs = []
        self.sem_pool = []
        self.nsem = 0
        self.ninst = 0
        self.nwait = 0
        self.semeng = {id(self.sem[k]): k for k in self.ENG}
        self.vc = {k: {} for k in self.ENG}

    def _dsem(self, b):
        if b.dsem is None:
            if self.sem_pool:
                b.dsem, b.dcnt = self.sem_pool.pop()
            else:
                b.dsem = self.nc.alloc_semaphore("dq%d" % self.nsem)
                self.nsem += 1
            self.dma_bufs.append(b)
        return b.dsem

    def release(self, bufs):
        for b in bufs:
            if b.dsem is not None:
                self.sem_pool.append((b.dsem, b.dcnt))
                self.dma_bufs.remove(b)
                b.dsem = None

    def _wait(self, eng, deps, skip_self=False, attach=False):
        E = self.e[eng]
        seen = self.seen[eng]
        own = self.sem[eng]
        cand = []
        for sem, val in deps.items():
            if skip_self and sem is own:
                continue
            if seen.get(sem, 0) < val:
                pe = self.semeng.get(id(sem))
                snap = self.vc[pe].get(val) if pe is not None else None
                cand.append((sem, val, snap))
        need = []
        for k, (sem, val, snap) in enumerate(cand):
            implied = False
            for k2, (s2, v2, snap2) in enumerate(cand):
                if k2 != k and snap2 is not None and snap2.get(sem, 0) >= val:
                    implied = True
                    break
            if not implied:
                need.append((sem, val))
        for sem, val, snap in cand:
            if seen.get(sem, 0) < val:
                seen[sem] = val
            if snap is not None:
                for s2, v2 in snap.items():
                    if seen.get(s2, 0) < v2:
                        seen[s2] = v2
        last = need.pop() if (attach and need) else None
        for sem, val in need:
            E.wait_ge(sem, val)
            self.nwait += 1
        return last

    @staticmethod
    def _collect(reads, writes):
        deps = {}
        for r in reads:
            for s, v in r.wtok.items():
                if deps.get(s, 0) < v:
                    deps[s] = v
        for w in writes:
            for s, v in w.wtok.items():
                if deps.get(s, 0) < v:
                    deps[s] = v
            for s, v in w.rtok.items():
                if deps.get(s, 0) < v:
                    deps[s] = v
        return deps

    @staticmethod
    def _record(tok, reads, writes):
        s, v = tok
        for r in reads:
            if r.rtok.get(s, 0) < v:
                r.rtok[s] = v
        for w in writes:
            w.wtok = {s: v}
            w.rtok = {}

    def op(self, eng, fn, reads=(), writes=(), signal=True):
        deps = self._collect(reads, writes)
        last = self._wait(eng, deps, skip_self=(eng == "pe"), attach=True)
        ins = fn(self.e[eng])
        if last is not None:
            ins._wait_ge(last[0], last[1])
        self.ninst += 1
        if signal:
            self.cnt[eng] += 1
            ins.then_inc(self.sem[eng], 1)
            tok = (self.sem[eng], self.cnt[eng])
            snap = {self.sem[k]: self.seen[eng].get(self.sem[k], 0) for k in self.ENG}
            self.vc[eng][self.cnt[eng]] = snap
        else:
            tok = (self.sem[eng], self.cnt[eng] + 1)
        self._record(tok, reads, writes)
        return ins

    def dma(self, q, out_ap, in_ap, reads=(), writes=(), indirect=None):
        dst = writes[0]
        deps = self._collect(reads, writes)
        last = self._wait(q, deps, attach=(indirect is None))
        sem = self._dsem(dst)
        dst.dcnt += 1
        if indirect is not None:
            self.e[q].indirect_dma_start(out=out_ap, out_offset=None, in_=in_ap,
                                         in_offset=bass.IndirectOffsetOnAxis(ap=indirect, axis=0)).then_inc(sem, 16)
        else:
            ins = self.e[q].dma_start(out=out_ap, in_=in_ap)
            if last is not None:
                ins._wait_ge(last[0], last[1])
            ins.then_inc(sem, 16)
        self.ninst += 1
        s, v = sem, 16 * dst.dcnt
        for r in reads:
            if r.rtok.get(s, 0) < v:
                r.rtok[s] = v
        dst.wtok = {s: v}
        dst.rtok = {}

    def collective(self, kind, in_buf, out_buf, in_ap, out_ap):
        deps = self._collect([in_buf], [out_buf])
        self._wait("pool", deps)
        self.cc_cnt += 1
        self.nc.gpsimd.collective_compute(kind, ALU.bypass, replica_groups=GROUPS, ins=[in_ap],
                                          outs=[out_ap]).then_inc(self.cc_sem, 1)
        self._record((self.cc_sem, self.cc_cnt), [in_buf], [out_buf])

    def _all_tokens(self, with_cc):
        deps = {self.sem[k]: self.cnt[k] for k in self.ENG if self.cnt[k] > 0}
        for b in self.dma_bufs:
            if b.dcnt:
                deps[b.dsem] = 16 * b.dcnt
        for s, c in self.sem_pool:
            if c:
                deps[s] = 16 * c
        if with_cc and self.cc_cnt:
            deps[self.cc_sem] = self.cc_cnt
        return deps

    def barrier(self):
        deps = self._all_tokens(False)
        for k in self.ENG:
            self._wait(k, deps)

    def final_wait(self):
        self._wait("sp", self._all_tokens(True))


class Job:
    def __init__(self, **kw):
        self.__dict__.update(kw)


class Builder:
    def __init__(self, SEQ):
        self.SEQ = SEQ
        self.SEG = SEQ // 4
        self.nc = nc = bass.Bass("TRN2", target_bir_lowering=False)
        self.S = Sched(nc)
        self.uid = 0
        self.inputs = {}
        self.outputs = {}
        self.ps = [Buf("ps%d" % i, nc.alloc_psum_tensor("ps%d" % i, [128, 512], F32)) for i in range(8)]
        self.epsc = nc.alloc_sbuf_tensor("epsc", [128, 1], F32)
        self.epsb = Buf("epsc", self.epsc)
        self.S.op("pool", lambda e: e.memset(self.epsc[:], LN_EPS), writes=[self.epsb])

    def din(self, name, shape, dt=F32):
        h = self.nc.dram_tensor(name, list(shape), dt, kind="ExternalInput")
        self.inputs[name] = (tuple(shape), dt)
        return Buf(name, h)

    def dout(self, name, shape, dt=F32):
        h = self.nc.dram_tensor(name, list(shape), dt, kind="ExternalOutput")
        self.outputs[name] = (tuple(shape), dt)
        return Buf(name, h)

    def dscr(self, name, shape, dt):
        h = self.nc.dram_tensor(name, list(shape), dt, kind="Internal")
        return Buf(name, h)

    def sbuf(self, es, name, shape, dt, lst):
        self.uid += 1
        h = es.enter_context(self.nc.sbuf_tensor("%s_%d" % (name, self.uid), list(shape), dt))
        b = Buf("%s_%d" % (name, self.uid), h)
        lst.append(b)
        return b

    def mm(self, out, lhsT, rhs, start, stop, reads, writes):
        self.S.op("pe", lambda e: e.matmul(out, lhsT, rhs, start=start, stop=stop), reads=reads, writes=writes,
                  signal=bool(stop))

    def tr(self, out, in_, ident, reads, writes, signal):
        self.S.op("pe", lambda e: e.transpose(out, in_, ident), reads=reads, writes=writes, signal=signal)

    def act(self, out, in_, func, reads, writes, bias=None, scale=None, accum_out=None):
        kw = {}
        if bias is not None:
            kw["bias"] = bias
        if scale is not None:
            kw["scale"] = scale
        if accum_out is not None:
            kw["accum_out"] = accum_out
        self.S.op("act", lambda e: e.activation(out=out, in_=in_, func=func, **kw), reads=reads, writes=writes)

    def tt(self, eng, out, in0, in1, op, reads, writes):
        self.S.op(eng, lambda e: e.tensor_tensor(out=out, in0=in0, in1=in1, op=op), reads=reads, writes=writes)

    def ts(self, eng, out, in0, s1, op0, reads, writes, s2=None, op1=None):
        if op1 is None:
            self.S.op(eng, lambda e: e.tensor_scalar(out=out, in0=in0, scalar1=s1, scalar2=None, op0=op0),
                      reads=reads, writes=writes)
        else:
            self.S.op(eng, lambda e: e.tensor_scalar(out=out, in0=in0, scalar1=s1, scalar2=s2, op0=op0, op1=op1),
                      reads=reads, writes=writes)

    def stt(self, eng, out, in0, scalar, in1, op0, op1, reads, writes):
        self.S.op(eng, lambda e: e.scalar_tensor_tensor(out=out, in0=in0, scalar=scalar, in1=in1, op0=op0, op1=op1),
                  reads=reads, writes=writes)

    def cp(self, eng, out, in_, reads, writes):
        if eng == "act":
            self.S.op("act", lambda e: e.copy(out=out, in_=in_), reads=reads, writes=writes)
        else:
            self.S.op(eng, lambda e: e.tensor_copy(out=out, in_=in_), reads=reads, writes=writes)

    def ln_rows(self, src_ap_fn, nrows, width, src_bufs, stat, es_name=""):
        nchunk = (width + 511) // 512
        st = stat
        for c in range(nchunk):
            lo, hi = c * 512, min(width, (c + 1) * 512)
            self.S.op("dve", lambda e, c=c, lo=lo, hi=hi: e.bn_stats(out=st[0:nrows, 8 + 6 * c: 14 + 6 * c],
                                                                     in_=src_ap_fn(lo, hi)),
                      reads=src_bufs, writes=[st])
        self.S.op("dve", lambda e: e.bn_aggr(out=st[0:nrows, 4:6],
                                             in_=st[0:nrows, 8:8 + 6 * nchunk].rearrange("p (c s) -> p c s", s=6)),
                  reads=[st], writes=[st])
        self.act(st[0:nrows, 2:3], st[0:nrows, 5:6], AF.Sqrt, [st, self.epsb], [st], bias=self.epsc[0:nrows, 0:1],
                 scale=1.0)
        self.S.op("dve", lambda e: e.reciprocal(out=st[0:nrows, 0:1], in_=st[0:nrows, 2:3]), reads=[st], writes=[st])
        self.stt("dve", st[0:nrows, 1:2], st[0:nrows, 4:5], -1.0, st[0:nrows, 0:1], ALU.mult, ALU.mult, [st], [st])
        return st[0:nrows, 0:1], st[0:nrows, 1:2]

    def phaseA(self, J, C):
        nc, S, ps = self.nc, self.S, self.ps
        TB, NB = J.TB, J.NB
        T = TB * NB
        ntile = J.ntok // T
        bufs = []
        with ExitStack() as es:
            sb = lambda n, s, d: self.sbuf(es, n, s, d, bufs)
            w = sb("Aw", [128, 8, 1536], BF16)
            xtok = [sb("Axtok", [128, NB, 1024], BF16) for _ in range(2)]
            rope = [sb("Arope", [128, 4, T], F32) for _ in range(2)]
            xT = sb("AxT", [128, 8, T], BF16)
            t1 = sb("At1", [128, T], F32)
            t2 = sb("At2", [128, T], F32)
            qdT = sb("AqdT", [128, 2, T], BF16)
            kiT = sb("AkiT", [128, 2, T], BF16)
            kitok = sb("Akitok", [128, NB, 256], BF16)
            vtok = sb("Avtok", [128, NB, 512], BF16)
            gs = sb("Ags", [128, NB, 512], F32)
            pT = sb("ApT", [128, NB, TB], BF16)
            Sf = sb("ASf", [128, 2, 512], F32)
            Sb = sb("ASb", [128, 2, 512], BF16)
            tmp = sb("Atmp", [128, 2, 512], F32)
            on = [sb("Aon", [128, 512], F32) for _ in range(2)]
            og = [sb("Aog", [128, NB, 512], BF16) for _ in range(2)]
            lnp = sb("Alnp", [128, 2, 512], F32)
            gam = sb("Agam", [128, 1], F32)
            stat = [sb("Astat", [128, 32], F32) for _ in range(2)]

            S.dma("pool", w[:], J.w.t[:], reads=[J.w], writes=[w])
            S.dma("sp", lnp[:, 0, :], J.lng.partition_broadcast(128), reads=[J.lnb_buf], writes=[lnp])
            S.dma("sp", lnp[:, 1, :], J.lnb.partition_broadcast(128), reads=[J.lnb_buf], writes=[lnp])
            S.dma("sp", gam[:], J.gam, reads=[J.gam_buf], writes=[gam])
            if J.S0 is not None:
                S.dma("sp", Sf[:], J.S0.rearrange("(h p) e -> p h e", p=128), reads=[J.S0_buf], writes=[Sf])
            else:
                S.op("pool", lambda e: e.memset(Sf[:], 0.0), writes=[Sf])
            self.cp("pool", Sb[:], Sf[:], [Sf], [Sb])

            def load(t):
                xs = J.x[t * T:(t + 1) * T, :].rearrange("(nb p) f -> p nb f", p=TB)
                S.dma("pool", xtok[t % 2][0:TB, :, :], xs, reads=[J.x_buf], writes=[xtok[t % 2]])
                S.dma("sp", rope[t % 2][:], J.rope[:, :, t * T:(t + 1) * T].rearrange("a p t -> p a t"),
                      reads=[J.rope_buf], writes=[rope[t % 2]])

            load(0)
            for t in range(ntile):
                if t + 1 < ntile:
                    load(t + 1)
                xk, rp = xtok[t % 2], rope[t % 2]
                for kc in range(8):
                    pb = ps[kc % 2]
                    pv = pb[:].bitcast(BF16)
                    for nb in range(NB):
                        self.tr(pv[:, nb * TB:(nb + 1) * TB], xk[0:TB, nb, kc * 128:(kc + 1) * 128],
                                C.identb[0:TB, 0:TB], [xk, C.identb_buf], [pb], signal=(nb == NB - 1))
                    self.cp("act" if kc % 2 else "dve", xT[:, kc, :], pv[:, 0:T], [pb], [xT])
                for qk in range(2):
                    pa, pbk = ps[2], ps[3]
                    for half, pp in enumerate((pa, pbk)):
                        col = qk * 256 + half * 128
                        for kc in range(8):
                            self.mm(pp[:, 0:T], w[:, kc, col:col + 128], xT[:, kc, :], kc == 0, kc == 7,
                                    [w, xT], [pp])
                    cs, sn = rp[:, 2 * qk, :], rp[:, 2 * qk + 1, :]
                    dst = qdT if qk == 0 else kiT
                    self.tt("dve", t1[:], pa[:, 0:T], cs, ALU.mult, [pa, rp], [t1])
                    self.tt("dve", t2[:], pbk[:, 0:T], sn, ALU.mult, [pbk, rp], [t2])
                    self.tt("pool", dst[:, 0, :], t1[:], t2[:], ALU.subtract, [t1, t2], [dst])
                    self.tt("dve", t1[:], pbk[:, 0:T], cs, ALU.mult, [pbk, rp], [t1])
                    self.tt("dve", t2[:], pa[:, 0:T], sn, ALU.mult, [pa, rp], [t2])
                    self.tt("pool", dst[:, 1, :], t1[:], t2[:], ALU.add, [t1, t2], [dst])
                pb = ps[0]
                pv = pb[:].bitcast(BF16)
                for c in range(NB):
                    for half in range(2):
                        self.tr(pv[0:TB, (c * 2 + half) * 128:(c * 2 + half + 1) * 128],
                                kiT[:, half, c * TB:(c + 1) * TB], C.identb[:, :], [kiT, C.identb_buf], [pb],
                                signal=(c == NB - 1 and half == 1))
                self.cp("act", kitok[0:TB, :, :], pv[0:TB, 0:NB * 256].rearrange("p (c d) -> p c d", d=256),
                        [pb], [kitok])
                pb = ps[1]
                for c in range(NB):
                    for half in range(2):
                        self.mm(pb[0:TB, c * TB:(c + 1) * TB], kiT[:, half, c * TB:(c + 1) * TB],
                                qdT[:, half, c * TB:(c + 1) * TB], half == 0, half == 1, [kiT, qdT], [pb])
                for c in range(NB):
                    self.tt("dve", pT[0:TB, c, :], pb[0:TB, c * TB:(c + 1) * TB], C.maskT[0:TB, 0:TB], ALU.mult,
                            [pb, C.maskT_buf], [pT])
                for nb in range(NB):
                    pp = ps[2 + nb % 2]
                    for kc in range(8):
                        self.mm(pp[0:TB, :], xT[:, kc, nb * TB:(nb + 1) * TB], w[:, kc, 512:1024], kc == 0, kc == 7,
                                [xT, w], [pp])
                    self.cp("act", vtok[0:TB, nb, :], pp[0:TB, :], [pp], [vtok])
                for nb in range(NB):
                    pp = ps[2 + nb % 2]
                    for kc in range(8):
                        self.mm(pp[0:TB, :], xT[:, kc, nb * TB:(nb + 1) * TB], w[:, kc, 1024:1536], kc == 0, kc == 7,
                                [xT, w], [pp])
                    self.act(gs[0:TB, nb, :], pp[0:TB, :], AF.Sigmoid, [pp], [gs])
                    self.tt("dve", gs[0:TB, nb, :], gs[0:TB, nb, :], pp[0:TB, :], ALU.mult, [gs, pp], [gs])
                ogt = og[t % 2]
                for c in range(NB):
                    po = ps[4 + c % 2]
                    self.mm(po[0:TB, :], pT[0:TB, c, :], vtok[0:TB, c, :], True, False, [pT, vtok], [po])
                    for half in range(2):
                        self.mm(po[0:TB, :], qdT[:, half, c * TB:(c + 1) * TB], Sb[:, half, :], False, half == 1,
                                [qdT, Sb], [po])
                    for half in range(2):
                        pd = ps[6 + half]
                        self.mm(pd[:, :], kitok[0:TB, c, half * 128:(half + 1) * 128], vtok[0:TB, c, :], True, True,
                                [kitok, vtok], [pd])
                    for half in range(2):
                        self.act(tmp[:, half, :], ps[6 + half][:, :], AF.Identity, [ps[6 + half], gam], [tmp],
                                 scale=gam[:, 0:1])
                    self.stt("dve", Sf[:], Sf[:], gam[:, 0:1], tmp[:], ALU.mult, ALU.add, [Sf, tmp, gam], [Sf])
                    self.cp("pool", Sb[:], Sf[:], [Sf], [Sb])
                    st = stat[c % 2]
                    rstd, nbias = self.ln_rows(lambda lo, hi, po=po: po[0:TB, lo:hi], TB, 512, [po], st)
                    o_n = on[c % 2]
                    self.act(o_n[0:TB, :], po[0:TB, :], AF.Identity, [po, st], [o_n], bias=nbias, scale=rstd)
                    self.tt("pool", o_n[0:TB, :], o_n[0:TB, :], lnp[0:TB, 0, :], ALU.mult, [o_n, lnp], [o_n])
                    self.tt("pool", o_n[0:TB, :], o_n[0:TB, :], lnp[0:TB, 1, :], ALU.add, [o_n, lnp], [o_n])
                    self.tt("dve", ogt[0:TB, c, :], o_n[0:TB, :], gs[0:TB, c, :], ALU.mult, [o_n, gs], [ogt])
                S.dma("pool", J.og_dst(t * T, T).rearrange("(nb p) e -> p nb e", p=TB), ogt[0:TB, :, :],
                      reads=[ogt], writes=[J.og_buf])
                if getattr(J, "after_tile", None) is not None:
                    J.after_tile(t)
            S.dma("pool", J.Sout.rearrange("(h p) e -> p h e", p=128), Sf[:], reads=[Sf], writes=[J.Sout_buf])
            S.barrier()
            S.release(bufs)

    def phaseB(self, J, C):
        nc, S, ps = self.nc, self.S, self.ps
        TB, NB = J.TB, J.NB
        T = TB * NB
        ntile = J.ntok // T
        KC = J.KC
        bufs = []
        with ExitStack() as es:
            sb = lambda n, s, d: self.sbuf(es, n, s, d, bufs)
            mixT = sb("BmixT", [128, KC, T], BF16)
            mtok = sb("Bmtok", [128, 2048], BF16) if J.tokmajor_in else None
            xres = sb("Bxres", [128, NB, 1024], F32)
            xmid = sb("Bxmid", [128, NB, 1024], F32)
            xmb = [sb("Bxmb", [128, 1024], BF16) for _ in range(2)]
            xmT = sb("BxmT", [128, 8, T], BF16)
            hid = sb("Bhid", [128, NFC, T], BF16)
            aext = [sb("Baext", [128, T + 2], F32) for _ in range(2)]
            u = [sb("Bu", [128, T], F32) for _ in range(2)]
            ge = [sb("Bge", [128, T], F32) for _ in range(2)]
            aprev = sb("Baprev", [128, NFC, 2], F32)
            wo = [sb("Bwo", [128, 8, 512], BF16) for _ in range(2)]
            wu = [sb("Bwu", [128, 8, 256], BF16) for _ in range(3)]
            wd = [sb("Bwd", [128, 11, 512], BF16) for _ in range(2)]
            lnt = sb("Blnt", [128, 4, 1024], F32)
            cvp = sb("Bcvp", [128, NFC, 4], F32)
            yout = [sb("Byout", [128, 1024], F32) for _ in range(2)]
            stat = [sb("Bstat", [128, 32], F32) for _ in range(2)]
            flag = sb("Bflag", [128, 1], F32)
            idx = sb("Bidx", [128, J.idx.t.shape[1]], I32) if J.idx is not None else None
            xhb_s = sb("Bxhb", [64, 1024], BF16) if J.halo else None

            for i in range(4):
                S.dma("sp", lnt[:, i, :], J.ln[i, :].partition_broadcast(128), reads=[J.ln_buf], writes=[lnt])
            S.dma("sp", cvp[:], J.convp, reads=[J.convp_buf], writes=[cvp])
            S.dma("sp", flag[:], C.flag.t[:], reads=[C.flag], writes=[flag])
            if idx is not None:
                S.dma("sp", idx[:], J.idx.t[:], reads=[J.idx], writes=[idx])
            if J.aprev0 is not None:
                S.dma("sp", aprev[:], J.aprev0, reads=[J.aprev0_buf], writes=[aprev])
            else:
                S.op("pool", lambda e: e.memset(aprev[:], 0.0), writes=[aprev])

            def run_tile(tb, nb_, t, halo):
                Tt = tb * nb_
                if J.tokmajor_in:
                    for nb in range(nb_):
                        J.load_mix_tok(self, t, nb, tb, halo, mtok, idx)
                        for kc in range(KC):
                            pb = ps[4 + (kc // 4) % 2]
                            pv = pb[:].bitcast(BF16)
                            q = kc % 4
                            self.tr(pv[:, q * 128:q * 128 + tb], mtok[0:tb, kc * 128:(kc + 1) * 128],
                                    C.identb[0:tb, 0:tb], [mtok, C.identb_buf], [pb], signal=(q == 3))
                            if q == 3:
                                k0 = kc - 3
                                self.cp("act" if (kc // 4) % 2 else "dve",
                                        mixT[:, k0:k0 + 4, nb * tb:(nb + 1) * tb],
                                        pv[:, 0:512].rearrange("p (q c) -> p q c", c=128)[:, :, 0:tb], [pb], [mixT])
                else:
                    J.load_mix_T(self, t, halo, mixT, idx, Tt, hid)
                if halo:
                    J.load_xres_halo(self, xres, idx, xhb_s)
                else:
                    S.dma("sp", xres[0:tb, 0:nb_, :],
                          J.xres[t * T:(t + 1) * T, :].rearrange("(nb p) f -> p nb f", p=tb),
                          reads=[J.xres_buf], writes=[xres])
                for half in range(2):
                    for g in range(KC // 8):
                        wslot = wo[(half * (KC // 8) + g) % 2]
                        S.dma("sp", wslot[:], J.w_out[half, :, g * 8:(g + 1) * 8, :], reads=[J.w_out_buf], writes=[wslot])
                        for k8 in range(8):
                            kc = g * 8 + k8
                            for nb in range(nb_):
                                self.mm(ps[nb][0:tb, :], mixT[:, kc, nb * tb:(nb + 1) * tb], wslot[:, k8, :],
                                        kc == 0, kc == KC - 1, [mixT, wslot], [ps[nb]])
                    for nb in range(nb_):
                        self.stt("dve", xmid[0:tb, nb, half * 512:(half + 1) * 512],
                                 xres[0:tb, nb, half * 512:(half + 1) * 512], ALPHA, ps[nb][0:tb, :],
                                 ALU.mult, ALU.add, [xres, ps[nb]], [xmid])
                for nb in range(nb_):
                    st = stat[nb % 2]
                    rstd, nbias = self.ln_rows(lambda lo, hi, nb=nb: xmid[0:tb, nb, lo:hi], tb, 1024, [xmid], st)
                    self.act(xmid[0:tb, nb, :], xmid[0:tb, nb, :], AF.Identity, [xmid, st], [xmid], bias=nbias,
                             scale=rstd)
                    self.tt("pool", xmid[0:tb, nb, :], xmid[0:tb, nb, :], lnt[0:tb, 0, :], ALU.mult, [xmid, lnt], [xmid])
                    xb_ = xmb[nb % 2]
                    self.tt("pool", xmid[0:tb, nb, :], xmid[0:tb, nb, :], lnt[0:tb, 1, :], ALU.add, [xmid, lnt], [xmid])
                    self.cp("pool", xb_[0:tb, :], xmid[0:tb, nb, :], [xmid], [xb_])
                    for kc in range(8):
                        pb = ps[4 + (kc // 4) % 2]
                        pv = pb[:].bitcast(BF16)
                        q = kc % 4
                        self.tr(pv[:, q * 128:q * 128 + tb], xb_[0:tb, kc * 128:(kc + 1) * 128], C.identb[0:tb, 0:tb],
                                [xb_, C.identb_buf], [pb], signal=(q == 3))
                        if q == 3:
                            k0 = kc - 3
                            self.cp("act" if (kc // 4) % 2 else "dve", xmT[:, k0:k0 + 4, nb * tb:(nb + 1) * tb],
                                    pv[:, 0:512].rearrange("p (q c) -> p q c", c=128)[:, :, 0:tb], [pb], [xmT])
                for j in range(NFC):
                    ws = wu[j % 3]
                    S.dma("sp", ws[:], J.w_up[j], reads=[J.w_up_buf], writes=[ws])
                    pg = ps[4 + 2 * (j % 2)]
                    pvv = ps[5 + 2 * (j % 2)]
                    for kc in range(8):
                        self.mm(pg[:, 0:Tt], ws[:, kc, 128:256], xmT[:, kc, 0:Tt], kc == 0, kc == 7, [ws, xmT], [pg])
                    ae = aext[j % 2]
                    self.cp("pool", ae[:, 0:2], aprev[:, j, :], [aprev], [ae])
                    self.cp("act", ae[:, 2:2 + Tt], pg[:, 0:Tt], [pg], [ae])
                    if halo:
                        self.ts("pool", aprev[:, j, :], ae[:, Tt:Tt + 2], flag[:, 0:1], ALU.mult, [ae, flag], [aprev])
                        continue
                    self.cp("pool", aprev[:, j, :], ae[:, Tt:Tt + 2], [ae], [aprev])
                    for kc in range(8):
                        self.mm(pvv[:, 0:Tt], ws[:, kc, 0:128], xmT[:, kc, 0:Tt], kc == 0, kc == 7, [ws, xmT], [pvv])
                    uu = u[j % 2]
                    self.act(uu[:, 0:Tt], pg[:, 0:Tt], AF.Identity, [pg, cvp], [uu], bias=cvp[:, j, 3:4],
                             scale=cvp[:, j, 2:3])
                    self.stt("dve", uu[:, 0:Tt], ae[:, 1:1 + Tt], cvp[:, j, 1:2], uu[:, 0:Tt], ALU.mult, ALU.add,
                             [ae, cvp, uu], [uu])
                    self.stt("dve", uu[:, 0:Tt], ae[:, 0:Tt], cvp[:, j, 0:1], uu[:, 0:Tt], ALU.mult, ALU.add,
                             [ae, cvp, uu], [uu])
                    gg = ge[j % 2]
                    self.act(gg[:, 0:Tt], uu[:, 0:Tt], AF.Gelu, [uu], [gg])
                    self.tt("dve", hid[:, j, 0:Tt], gg[:, 0:Tt], pvv[:, 0:Tt], ALU.mult, [gg, pvv], [hid])
                if halo:
                    return
                for half in range(2):
                    for g in range(2):
                        wslot = wd[(half * 2 + g) % 2]
                        S.dma("sp", wslot[:], J.w_down[half, :, g * 11:(g + 1) * 11, :], reads=[J.w_down_buf],
                              writes=[wslot])
                        for k11 in range(11):
                            kc = g * 11 + k11
                            for nb in range(nb_):
                                self.mm(ps[nb][0:tb, :], hid[:, kc, nb * tb:(nb + 1) * tb], wslot[:, k11, :],
                                        kc == 0, kc == NFC - 1, [hid, wslot], [ps[nb]])
                    for nb in range(nb_):
                        sl = xmid[0:tb, nb, half * 512:(half + 1) * 512]
                        self.stt("dve", sl, sl, ALPHA, ps[nb][0:tb, :], ALU.mult, ALU.add, [xmid, ps[nb]], [xmid])
                for nb in range(nb_):
                    st = stat[nb % 2]
                    yo = yout[nb % 2]
                    rstd, nbias = self.ln_rows(lambda lo, hi, nb=nb: xmid[0:tb, nb, lo:hi], tb, 1024, [xmid], st)
                    self.act(yo[0:tb, :], xmid[0:tb, nb, :], AF.Identity, [xmid, st], [yo], bias=nbias, scale=rstd)
                    self.tt("pool", yo[0:tb, :], yo[0:tb, :], lnt[0:tb, 2, :], ALU.mult, [yo, lnt], [yo])
                    self.tt("pool", yo[0:tb, :], yo[0:tb, :], lnt[0:tb, 3, :], ALU.add, [yo, lnt], [yo])
                    r0 = t * T + nb * tb
                    S.dma("pool", J.y[r0:r0 + tb, :], yo[0:tb, :], reads=[yo], writes=[J.y_buf])
                    if J.ybf is not None:
                        S.dma("pool", J.ybf[r0:r0 + tb, :], yo[0:tb, :], reads=[yo], writes=[J.ybf_buf])
                if getattr(J, "after_tile", None) is not None:
                    J.after_tile(t)

            if J.halo:
                run_tile(64, 1, None, True)
            for t in range(ntile):
                run_tile(TB, NB, t, False)
            S.dma("pool", J.conv_out, aprev[:], reads=[aprev], writes=[J.conv_out_buf])
            S.barrier()
            S.release(bufs)

    def phaseC1(self, J, C):
        nc, S, ps = self.nc, self.S, self.ps
        bufs = []
        with ExitStack() as es:
            sb = lambda n, s, d: self.sbuf(es, n, s, d, bufs)
            w = sb("Cw", [128, 8, 776], BF16)
            xtok = [sb("Cxtok", [128, 1024], BF16) for _ in range(2)]
            xT = sb("CxT", [128, 8, 128], BF16)
            kf = [sb("Ckf", [128, 256], F32) for _ in range(2)]
            vf = [sb("Cvf", [128, 256], F32) for _ in range(2)]
            qf = sb("Cqf", [128, 256], F32)
            sq = sb("Csq", [128, 256], F32)
            sm = [sb("Csm", [128, 64], F32) for _ in range(2)]
            smb = [sb("Csmb", [128, 32], BF16) for _ in range(2)]
            lf = [sb("Clf", [128, 4], F32) for _ in range(2)]
            carry = sb("Ccarry", [1, 4], F32)
            Kp = [sb("CKp", [128, 4, 70], BF16) for _ in range(2)]
            Qp = [sb("CQp", [128, 4, 70], BF16) for _ in range(2)]
            Vp = [sb("CVp", [128, 4, 65], BF16) for _ in range(2)]
            KT = [sb("CKT", [128, 4, 128], BF16) for _ in range(2)]
            QT = [sb("CQT", [128, 4, 128], BF16) for _ in range(2)]
            bfb = sb("Cbfb", [128, 4], F32)

            S.dma("pool", w[:], J.w.t[:], reads=[J.w], writes=[w])
            S.dma("sp", bfb[:], J.bf.partition_broadcast(128), reads=[J.bf_buf], writes=[bfb])
            S.op("pool", lambda e: e.memset(carry[:], 0.0), writes=[carry])
            import os
            DBG0 = int(os.environ.get("KC1", "0"))
            for i in range(2):
                if DBG0 & 16:
                    break
                S.op("pool", lambda e, i=i: e.memset(Kp[i][:, :, 67:70], 1.0), writes=[Kp[i]])
                S.op("pool", lambda e, i=i: e.memset(Qp[i][:, :, 64:67], 1.0), writes=[Qp[i]])

            def split3(src_ap, tb, dst_fn, smt, smbt, neg):
                self.cp("dve", smbt[0:tb, 0:4], src_ap, [smt], [smbt])
                self.cp("dve", smt[0:tb, 32:36], smbt[0:tb, 0:4], [smbt], [smt])
                self.tt("dve", smt[0:tb, 36:40], src_ap, smt[0:tb, 32:36], ALU.subtract, [smt], [smt])
                self.cp("dve", smbt[0:tb, 4:8], smt[0:tb, 36:40], [smt], [smbt])
                self.cp("dve", smt[0:tb, 40:44], smbt[0:tb, 4:8], [smbt], [smt])
                self.tt("dve", smt[0:tb, 44:48], smt[0:tb, 36:40], smt[0:tb, 40:44], ALU.subtract, [smt], [smt])
                self.cp("dve", smbt[0:tb, 8:12], smt[0:tb, 44:48], [smt], [smbt])
                for i in range(3):
                    dst, dbuf = dst_fn(i)
                    if neg:
                        self.ts("dve", dst, smbt[0:tb, 4 * i:4 * i + 4], -1.0, ALU.mult, [smbt], [dbuf])
                    else:
                        self.cp("dve", dst, smbt[0:tb, 4 * i:4 * i + 4], [smbt], [dbuf])

            import os
            DBG = int(os.environ.get("KC1", "0"))
            blk = 0
            for (kind, nblk, tb, src) in J.segments:
                if os.environ.get("KSEG", "") and kind != os.environ.get("KSEG"):
                    continue
                for bi in range(nblk):
                    i2 = blk % 2
                    kft, vft, lft, smt, smbt = kf[i2], vf[i2], lf[i2], sm[i2], smb[i2]
                    Kpt, Qpt, Vpt, KTt, QTt = Kp[i2], Qp[i2], Vp[i2], KT[i2], QT[i2]
                    r0 = bi * tb
                    if kind == "cache":
                        S.dma("sp", kft[0:tb, :], src.k[r0:r0 + tb, :], reads=[src.k_buf], writes=[kft])
                        S.dma("sp", vft[0:tb, :], src.v[r0:r0 + tb, :], reads=[src.v_buf], writes=[vft])
                        if not (DBG & 32):
                            S.dma("sp", lft[0:tb, :], src.lf[r0:r0 + tb, :], reads=[src.lf_buf], writes=[lft])
                    else:
                        xk = xtok[i2]
                        xr0 = src.xrow(r0) if getattr(src, "xrow", None) is not None else r0
                        S.dma("sp", xk[0:tb, :], src.x[xr0:xr0 + tb, :], reads=[src.x_buf], writes=[xk])
                        for kc in range(8):
                            pb = ps[kc // 4]
                            pv = pb[:].bitcast(BF16)
                            q = kc % 4
                            self.tr(pv[:, q * 128:q * 128 + tb], xk[0:tb, kc * 128:(kc + 1) * 128],
                                    C.identb[0:tb, 0:tb], [xk, C.identb_buf], [pb], signal=(q == 3))
                            if q == 3:
                                k0 = kc - 3
                                self.cp("act" if kc // 4 else "dve", xT[:, k0:k0 + 4, 0:tb],
                                        pv[:, 0:512].rearrange("p (q c) -> p q c", c=128)[:, :, 0:tb], [pb], [xT])
                        p1, p2 = ps[2], ps[3]
                        for kc in range(8):
                            self.mm(p1[0:tb, 0:512], xT[:, kc, 0:tb], w[:, kc, 0:512], kc == 0, kc == 7, [xT, w], [p1])
                        for kc in range(8):
                            self.mm(p2[0:tb, 0:260], xT[:, kc, 0:tb], w[:, kc, 512:772], kc == 0, kc == 7, [xT, w], [p2])
                        self.cp("act", qf[0:tb, :], p1[0:tb, 0:256], [p1], [qf])
                        self.cp("act", kft[0:tb, :], p1[0:tb, 256:512], [p1], [kft])
                        self.cp("act", vft[0:tb, :], p2[0:tb, 0:256], [p2], [vft])
                        if not (DBG & 64):
                            self.cp("act", smt[0:tb, 28:32], p2[0:tb, 256:260], [p2], [smt])
                            self.tt("dve", smt[0:tb, 0:4], smt[0:tb, 28:32], bfb[0:tb, :], ALU.add, [smt, bfb], [smt])
                        if not (DBG & 128):
                            self.act(smt[0:tb, 0:4], smt[0:tb, 0:4], AF.Exp, [smt], [smt], scale=-1.0)
                        if not (DBG & 256):
                            self.act(smt[0:tb, 0:4], smt[0:tb, 0:4], AF.Ln, [smt], [smt], bias=C.one_col[0:tb, 0:1])
                        self.ts("dve", lft[0:tb, :], smt[0:tb, 0:4], -1.0, ALU.mult, [smt], [lft])
                        g0 = src.out_row0 + r0
                        S.dma("pool", J.pk[g0:g0 + tb, :], kft[0:tb, :], reads=[kft], writes=[J.pk_buf])
                        S.dma("pool", J.pv[g0:g0 + tb, :], vft[0:tb, :], reads=[vft], writes=[J.pv_buf])
                        if not (DBG & 32):
                            S.dma("pool", J.plf[g0:g0 + tb, :], lft[0:tb, :], reads=[lft], writes=[J.plf_buf])
                    if DBG & 1:
                        blk += 1
                        continue
                    pc = ps[4]
                    self.mm(pc[0:tb, 0:4], C.tri[0:tb, 0:tb], lft[0:tb, :], True, False, [C.tri_buf, lft], [pc])
                    self.mm(pc[0:tb, 0:4], C.ones_f[0:1, 0:tb], carry[0:1, :], False, True, [C.ones_buf, carry], [pc])
                    pcar = ps[5]
                    self.mm(pcar[0:1, 0:4], C.ones_f[0:tb, 0:1], lft[0:tb, :], True, False, [C.ones_buf, lft], [pcar])
                    self.mm(pcar[0:1, 0:4], C.ones_f[0:1, 0:1], carry[0:1, :], False, True, [C.ones_buf, carry], [pcar])
                    self.cp("act", carry[0:1, :], pcar[0:1, 0:4], [pcar], [carry])
                    self.cp("act", smt[0:tb, 4:8], pc[0:tb, 0:4], [pc], [smt])
                    if DBG & 2:
                        blk += 1
                        continue
                    self.tt("pool", sq[0:tb, :], kft[0:tb, :], kft[0:tb, :], ALU.mult, [kft], [sq])
                    S.op("dve", lambda e, tb=tb, smt=smt: e.reduce_sum(
                        out=smt[0:tb, 8:12], in_=sq[0:tb, :].rearrange("p (h d) -> p h d", d=64), axis=AX.X),
                        reads=[sq], writes=[smt])
                    self.ts("dve", smt[0:tb, 8:12], smt[0:tb, 8:12], 1.0 / 16.0, ALU.mult, [smt], [smt])
                    self.act(smt[0:tb, 12:16], smt[0:tb, 8:12], AF.Exp, [smt], [smt])
                    self.tt("dve", smt[0:tb, 16:20], smt[0:tb, 4:8], smt[0:tb, 8:12], ALU.add, [smt], [smt])
                    split3(smt[0:tb, 16:20], tb, lambda i: (Kpt[0:tb, :, 64 + i], Kpt), smt, smbt, True)
                    self.cp("pool", Kpt[0:tb, :, 0:64], kft[0:tb, :].rearrange("p (h d) -> p h d", d=64), [kft], [Kpt])
                    for hh in range(4):
                        self.ts("pool" if hh % 2 else "dve", Vpt[0:tb, hh, 0:64], vft[0:tb, hh * 64:(hh + 1) * 64],
                                smt[0:tb, 12 + hh:13 + hh], ALU.mult, [vft, smt], [Vpt])
                    self.cp("dve", Vpt[0:tb, :, 64], smt[0:tb, 12:16], [smt], [Vpt])
                    if DBG & 4:
                        blk += 1
                        continue
                    k0 = J.kpos0[kind] + r0
                    S.dma("pool", J.VV[:, k0:k0 + tb, :].rearrange("h p e -> p h e"), Vpt[0:tb, :, :], reads=[Vpt],
                          writes=[J.VV_buf])
                    if DBG & 8:
                        blk += 1
                        continue
                    pk_ = ps[6]
                    pkv = pk_[:].bitcast(BF16)
                    for hh in range(4):
                        self.tr(pkv[0:70, hh * 128:hh * 128 + tb], Kpt[0:tb, hh, :], C.identb[0:tb, 0:tb],
                                [Kpt, C.identb_buf], [pk_], signal=(hh == 3))
                    self.cp("act", KTt[0:70, :, 0:tb], pkv[0:70, 0:512].rearrange("p (h c) -> p h c", c=128)[:, :, 0:tb],
                            [pk_], [KTt])
                    S.dma("pool", J.KT[:, :, k0:k0 + tb].rearrange("h p t -> p h t"), KTt[0:70, :, 0:tb], reads=[KTt],
                          writes=[J.KT_buf])
                    if kind == "new":
                        self.tt("pool", sq[0:tb, :], qf[0:tb, :], qf[0:tb, :], ALU.mult, [qf], [sq])
                        S.op("dve", lambda e, tb=tb, smt=smt: e.reduce_sum(
                            out=smt[0:tb, 20:24], in_=sq[0:tb, :].rearrange("p (h d) -> p h d", d=64), axis=AX.X),
                            reads=[sq], writes=[smt])
                        self.stt("dve", smt[0:tb, 24:28], smt[0:tb, 20:24], -1.0 / 16.0, smt[0:tb, 4:8], ALU.mult, ALU.add,
                                 [smt], [smt])
                        split3(smt[0:tb, 24:28], tb, lambda i: (Qpt[0:tb, :, 67 + i], Qpt), smt, smbt, False)
                        self.ts("pool", Qpt[0:tb, :, 0:64], qf[0:tb, :].rearrange("p (h d) -> p h d", d=64), 0.125,
                                ALU.mult, [qf], [Qpt])
                        pq_ = ps[7]
                        pqv = pq_[:].bitcast(BF16)
                        for hh in range(4):
                            self.tr(pqv[0:70, hh * 128:hh * 128 + tb], Qpt[0:tb, hh, :], C.identb[0:tb, 0:tb],
                                    [Qpt, C.identb_buf], [pq_], signal=(hh == 3))
                        self.cp("act", QTt[0:70, :, 0:tb],
                                pqv[0:70, 0:512].rearrange("p (h c) -> p h c", c=128)[:, :, 0:tb], [pq_], [QTt])
                        q0 = src.out_row0 + r0
                        S.dma("pool", J.QT[:, :, q0:q0 + tb].rearrange("h p t -> p h t"), QTt[0:70, :, 0:tb],
                              reads=[QTt], writes=[J.QT_buf])
                    blk += 1
            S.barrier()
            S.release(bufs)

    def phaseC2(self, J, C):
        nc, S, ps = self.nc, self.S, self.ps
        NK, NQ, TQ, TBq = J.NK, J.NQ, J.TQ, J.TBq
        ncache = J.ncache
        NKB = (NK + 127) // 128
        bufs = []
        with ExitStack() as es:
            sb = lambda n, s, d: self.sbuf(es, n, s, d, bufs)
            KTs = [sb("AtK", [128, NK], BF16) for _ in range(2)]
            Vs = [sb("AtV", [128, NKB, 65], BF16) for _ in range(2)]
            QTs = [sb("AtQ", [128, TQ], BF16) for _ in range(2)]
            PT = [sb("AtP", [128, TQ], BF16) for _ in range(3)]
            num = [sb("Atnum", [128, TQ], F32) for _ in range(2)]
            rd = [sb("Atrd", [128, TQ], F32) for _ in range(2)]
            oT = [sb("AtoT", [128, TQ], BF16) for _ in range(2)]
            nqt = NQ // TQ
            nsub = TQ // TBq
            LOOK = 2
            for hh in range(4):
                Kt, Vt = KTs[hh % 2], Vs[hh % 2]
                S.dma("sp", Kt[0:70, :], J.KT[hh], reads=[J.KT_buf], writes=[Kt])
                nfull = NK // 128
                if nfull:
                    S.dma("sp", Vt[:, 0:nfull, :], J.VV[hh, 0:nfull * 128, :].rearrange("(b p) e -> p b e", p=128),
                          reads=[J.VV_buf], writes=[Vt])
                if NK % 128:
                    S.dma("sp", Vt[0:NK % 128, nfull, :], J.VV[hh, nfull * 128:NK, :], reads=[J.VV_buf], writes=[Vt])
                units = []
                for qt in range(nqt):
                    kblocks = [(kb * 128, 128, None) for kb in range(ncache // 128)]
                    for u_all in range((qt + 1) * nsub):
                        k0 = ncache + u_all * TBq
                        u = u_all - qt * nsub
                        kblocks.append((k0, TBq, u if u >= 0 else None))
                    for bi, (k0, kb, u) in enumerate(kblocks):
                        units.append((qt, bi, len(kblocks), k0, kb, u))
                n = len(units)
                for i in range(n + LOOK):
                    if i < n:
                        qt, bi, nkb, k0, kb, u = units[i]
                        Qt = QTs[(hh * nqt + qt) % 2]
                        if bi == 0:
                            S.dma("sp", Qt[0:70, :], J.QT[hh, :, qt * TQ:(qt + 1) * TQ], reads=[J.QT_buf], writes=[Qt])
                        c0 = 0 if u is None else u * TBq
                        pss, pt = ps[i % 3], PT[i % 3]
                        self.mm(pss[0:kb, c0:TQ], Kt[0:70, k0:k0 + kb], Qt[0:70, c0:TQ], True, u is None, [Kt, Qt], [pss])
                        if u is not None:
                            self.mm(pss[0:kb, c0:c0 + TBq], C.identb[0:kb, 0:kb], C.negm[0:kb, 0:TBq], False, True,
                                    [C.identb_buf, C.negm_buf], [pss])
                        self.act(pt[0:kb, c0:TQ], pss[0:kb, c0:TQ], AF.Exp, [pss], [pt])
                    j = i - LOOK
                    if j < 0:
                        continue
                    qt, bi, nkb, k0, kb, u = units[j]
                    c0 = 0 if u is None else u * TBq
                    pt = PT[j % 3]
                    i2 = (hh * nqt + qt) % 2
                    po = ps[4 + i2]
                    self.mm(po[0:65, c0:TQ], Vt[0:kb, k0 // 128, :], pt[0:kb, c0:TQ], bi == 0, bi == nkb - 1,
                            [Vt, pt], [po])
                    if bi != nkb - 1:
                        continue
                    S.op("dve", lambda e, i2=i2, po=po: e.reciprocal(out=rd[i2][64:65, :], in_=po[64:65, 0:TQ]),
                         reads=[po], writes=[rd[i2]])
                    self.cp("dve", num[i2][0:64, :], po[0:64, 0:TQ], [po], [num[i2]])
                    pbc = ps[6 + i2]
                    self.mm(pbc[0:64, 0:TQ], C.ones_f[64:65, 0:64], rd[i2][64:65, :], True, True, [C.ones_buf, rd[i2]], [pbc])
                    self.tt("dve", oT[i2][0:64, :], num[i2][0:64, :], pbc[0:64, 0:TQ], ALU.mult, [num[i2], pbc], [oT[i2]])
                    S.dma("pool", J.oT_dst(hh, qt), oT[i2][0:64, :], reads=[oT[i2]], writes=[J.oT_buf])
                    if getattr(J, "after_unit", None) is not None:
                        J.after_unit(hh * nqt + qt)
            S.barrier()
            S.release(bufs)


def build_program(SEQ, stop=99):
    B = Builder(SEQ)
    nc, S = B.nc, B.S
    SEG = SEQ // 4
    NQT = SEQ // 512
    NTB_B = SEG // 128
    ntile_B = SEG // 512

    xA = B.din("xA", [SEQ, D])
    xB = B.din("xB", [SEG, D])
    xH = B.din("xH", [64, D])
    xS = B.din("xS", [NS, D])
    w_inP = B.din("w_inP", [128, 8, 1536])
    w_inS = [B.din("w_inS%d" % h, [128, 8, 1536]) for h in range(4)]
    ropeP = B.din("ropeP", [4, 128, SEQ])
    ropeS = B.din("ropeS", [4, 4, 128, NS])
    gamP = B.din("gamP", [128, 1])
    gamS = B.din("gamS", [4, 128, 1])
    lnretP = B.din("lnretP", [2, 512])
    lnretS = B.din("lnretS", [4, 2, 512])
    sret = B.din("sret", [4, 256, 512])
    w_outA = B.din("w_outA", [2, 128, 16, 512])
    w_up = B.din("w_up", [2, NFC, 128, 8, 256])
    w_down = B.din("w_down", [2, 2, 128, NFC, 512])
    convp = B.din("convp", [2, 128, NFC, 4])
    ln4 = B.din("ln4", [2, 4, D])
    sconv = B.din("sconv", [2, 128, NFC, 2])
    w_qkvP = B.din("w_qkvP", [128, 8, 776])
    w_qkvS = [B.din("w_qkvS%d" % g, [128, 8, 776]) for g in range(4)]
    bfP = B.din("bfP", [1, 4])
    bfS = B.din("bfS", [4, 4])
    w_outB = B.din("w_outB", [2, 128, 8, 512])
    ck = B.din("ck", [PAST, D])
    cv = B.din("cv", [PAST, D])
    clf = B.din("clf", [PAST, 16])
    flag_in = B.din("flag", [128, 1])
    idxB_in = B.din("idxB", [128, (NTB_B + 1) * 4], I32)
    idxD_in = B.din("idxD", [128, (ntile_B + 1) * 8 + 1], I32)
    cst = B.din("cst", [5, 128, 128])

    yP = B.dout("yP", [SEG, D])
    yS = B.dout("yS", [NS, D])
    retP = B.dout("retP", [256, 512])
    retS = B.dout("retS", [4, 256, 512])
    pk = B.dout("pk", [SEQ, 256])
    pv = B.dout("pv", [SEQ, 256])
    plf = B.dout("plf", [SEQ, 4])
    sk = B.dout("sk", [4, NS, 256])
    sv = B.dout("sv", [4, NS, 256])
    slf = B.dout("slf", [4, NS, 4])
    pconv = B.dout("pconv", [2, 128, NFC, 2])
    sconv_o = B.dout("sconv_o", [2, 128, NFC, 2])

    og_loc = B.dscr("og_loc", [SEQ, 512], BF16)
    og_all = B.dscr("og_all", [4 * SEQ, 512], BF16)
    og_s = B.dscr("og_s", [4 * NS, 512], BF16)
    x1_loc = B.dscr("x1_loc", [SEG, D], F32)
    x1_bf = B.dscr("x1_bf", [SEG, D], BF16)
    x1_all = B.dscr("x1_all", [SEQ, D], BF16)
    x1_s = B.dscr("x1_s", [NS, D], F32)
    x1_sb = B.dscr("x1_sb", [NS, D], BF16)
    KTp = B.dscr("KTp", [4, 70, SEQ], BF16)
    VVp = B.dscr("VVp", [4, SEQ, 65], BF16)
    QTp = B.dscr("QTp", [4, 70, SEQ], BF16)
    NKS = PAST + NS
    KTs_ = B.dscr("KTs", [4, 70, NKS], BF16)
    VVs_ = B.dscr("VVs", [4, NKS, 65], BF16)
    QTs_ = B.dscr("QTs", [4, 70, NS], BF16)
    oT_loc = B.dscr("oT_loc", [4 * NQT * 64, 512], BF16)
    oT_all = B.dscr("oT_all", [4 * 4 * NQT * 64, 512], BF16)
    oT_s = B.dscr("oT_s", [16 * 64, NS], BF16)
    wb_outA = B.dscr("wb_outA", [2, 128, 16, 512], BF16)
    wb_up = B.dscr("wb_up", [2, NFC, 128, 8, 256], BF16)
    wb_down = B.dscr("wb_down", [2, 2, 128, NFC, 512], BF16)
    wb_outB = B.dscr("wb_outB", [2, 128, 8, 512], BF16)

    C = Job()
    cf = nc.alloc_sbuf_tensor("c_f32", [128, 5, 128], F32)
    cb = nc.alloc_sbuf_tensor("c_bf", [128, 5, 128], BF16)
    cfb, cbb = Buf("c_f32", cf), Buf("c_bf", cb)
    S.dma("sp", cf[:], cst.t[:].rearrange("a p n -> p a n"), reads=[cst], writes=[cfb])
    S.dma("pool", cb[:], cst.t[:].rearrange("a p n -> p a n"), reads=[cst], writes=[cbb])
    C.identb, C.identb_buf = cb[:, 0, :], cbb
    C.maskT, C.maskT_buf = cf[:, 1, :], cfb
    C.tri, C.tri_buf = cf[:, 2, :], cfb
    C.negm, C.negm_buf = cb[:, 3, :], cbb
    C.ones_f, C.ones_buf = cf[:, 4, :], cfb
    C.one_col = cf[:, 4, 0:1]
    C.flag = flag_in

    for l in range(2):
        for h in range(2):
            for pg in range(0, 128, 32):
                if l == 0:
                    S.dma("pool", wb_outA.t[h, pg:pg + 32], w_outA.t[h, pg:pg + 32], reads=[w_outA], writes=[wb_outA])
                else:
                    S.dma("pool", wb_outB.t[h, pg:pg + 32], w_outB.t[h, pg:pg + 32], reads=[w_outB], writes=[wb_outB])
        for j in range(NFC):
            S.dma("pool", wb_up.t[l, j], w_up.t[l, j], reads=[w_up], writes=[wb_up])
        for h in range(2):
            for pg in range(0, 128, 32):
                S.dma("pool", wb_down.t[l, h, pg:pg + 32], w_down.t[l, h, pg:pg + 32], reads=[w_down],
                      writes=[wb_down])

    if stop <= 0:
        S.final_wait()
        return B
    def gmap(rho, h, CR):
        return (rho // CR) * 4 * CR + h * CR + (rho % CR)

    def exch_chunk(src, dst, i, CR):
        S.collective("AllGather", src, dst, src.t[i * CR:(i + 1) * CR, :], dst.t[i * 4 * CR:(i + 1) * 4 * CR, :])

    JA = Job(TB=128, NB=4, ntok=SEQ, x=xA.t, x_buf=xA, w=w_inP, rope=ropeP.t, rope_buf=ropeP,
             gam=gamP.t[:], gam_buf=gamP, lng=lnretP.t[0, :], lnb=lnretP.t[1, :], lnb_buf=lnretP,
             S0=None, S0_buf=None, Sout=retP.t[:], Sout_buf=retP,
             og_dst=lambda r0, n: og_loc.t[r0:r0 + n, :], og_buf=og_loc,
             after_tile=lambda t: exch_chunk(og_loc, og_all, t // 2, 1024) if t % 2 == 1 else None)
    B.phaseA(JA, C)
    if stop <= 1:
        S.final_wait()
        return B
    pass

    if stop <= 2:
        S.final_wait()
        return B
    for h in range(4):
        JS = Job(TB=NS, NB=1, ntok=NS, x=xS.t, x_buf=xS, w=w_inS[h], rope=ropeS.t[h], rope_buf=ropeS,
                 gam=gamS.t[h], gam_buf=gamS, lng=lnretS.t[h, 0, :], lnb=lnretS.t[h, 1, :], lnb_buf=lnretS,
                 S0=sret.t[h], S0_buf=sret, Sout=retS.t[h], Sout_buf=retS,
                 og_dst=lambda r0, n, h=h: og_s.t[h * NS + r0:h * NS + r0 + n, :], og_buf=og_s)
        B.phaseA(JS, C)

    if stop <= 3:
        S.final_wait()
        return B
    def load_mix_tok_sample(Bd, t, nb, tb, halo, mtok, idx):
        for h in range(4):
            Bd.S.dma("sp", mtok[0:tb, h * 512:(h + 1) * 512], og_s.t[h * NS:(h + 1) * NS, :], reads=[og_s],
                     writes=[mtok])

    JBs = Job(TB=NS, NB=1, ntok=NS, KC=16, tokmajor_in=True, load_mix_tok=load_mix_tok_sample, halo=False,
              xres=xS.t, xres_buf=xS, w_out=wb_outA.t, w_out_buf=wb_outA, w_up=wb_up.t[0], w_up_buf=wb_up,
              w_down=wb_down.t[0], w_down_buf=wb_down, convp=convp.t[0], convp_buf=convp, ln=ln4.t[0], ln_buf=ln4,
              aprev0=sconv.t[0], aprev0_buf=sconv, idx=None, y=x1_s.t, y_buf=x1_s, ybf=x1_sb.t, ybf_buf=x1_sb,
              conv_out=sconv_o.t[0], conv_out_buf=sconv_o)
    B.phaseB(JBs, C)

    if stop <= 4:
        S.final_wait()
        return B
    def load_mix_tok_prompt(Bd, t, nb, tb, halo, mtok, idx):
        col0 = NTB_B * 4 if halo else (t * 4 + nb) * 4
        for h in range(4):
            Bd.S.dma("pool", mtok[0:tb, h * 512:(h + 1) * 512], og_all.t[:, :], reads=[og_all, idx], writes=[mtok],
                     indirect=idx[0:tb, col0 + h:col0 + h + 1])

    def load_xres_halo_B(Bd, xres, idx, xhb_s):
        Bd.S.dma("sp", xres[0:64, 0, :], xH.t[:, :], reads=[xH], writes=[xres])

    JBp = Job(TB=128, NB=4, ntok=SEG, KC=16, tokmajor_in=True, load_mix_tok=load_mix_tok_prompt, halo=True,
              load_xres_halo=load_xres_halo_B,
              xres=xB.t, xres_buf=xB, w_out=wb_outA.t, w_out_buf=wb_outA, w_up=wb_up.t[0], w_up_buf=wb_up,
              w_down=wb_down.t[0], w_down_buf=wb_down, convp=convp.t[0], convp_buf=convp, ln=ln4.t[0], ln_buf=ln4,
              aprev0=None, aprev0_buf=None, idx=idxB_in, y=x1_loc.t, y_buf=x1_loc, ybf=x1_bf.t, ybf_buf=x1_bf,
              conv_out=pconv.t[0], conv_out_buf=pconv,
              after_tile=lambda t: exch_chunk(x1_bf, x1_all, t, 512))
    B.phaseB(JBp, C)
    if stop <= 5:
        S.final_wait()
        return B
    pass

    if stop <= 6:
        S.final_wait()
        return B
    for g in range(4):
        srcc = Job(k=ck.t[:, g * 256:(g + 1) * 256], k_buf=ck, v=cv.t[:, g * 256:(g + 1) * 256], v_buf=cv,
                   lf=clf.t[:, g * 4:(g + 1) * 4], lf_buf=clf)
        srcn = Job(x=x1_sb.t, x_buf=x1_sb, out_row0=0)
        JC = Job(w=w_qkvS[g], bf=bfS.t[g, :], bf_buf=bfS,
                 segments=[("cache", PAST // 128, 128, srcc), ("new", 1, NS, srcn)],
                 kpos0={"cache": 0, "new": PAST}, pk=sk.t[g], pk_buf=sk, pv=sv.t[g], pv_buf=sv, plf=slf.t[g],
                 plf_buf=slf, KT=KTs_.t, KT_buf=KTs_, VV=VVs_.t, VV_buf=VVs_, QT=QTs_.t, QT_buf=QTs_)
        B.phaseC1(JC, C)
        if stop == 61:
            S.final_wait()
            return B
        JC2 = Job(NK=NKS, NQ=NS, TQ=NS, TBq=NS, ncache=PAST, KT=KTs_.t, KT_buf=KTs_, VV=VVs_.t, VV_buf=VVs_,
                  QT=QTs_.t, QT_buf=QTs_,
                  oT_dst=lambda hh, qt, g=g: oT_s.t[(g * 4 + hh) * 64:(g * 4 + hh + 1) * 64, :], oT_buf=oT_s)
        B.phaseC2(JC2, C)

    if stop <= 7:
        S.final_wait()
        return B
    def load_mix_T_sample(Bd, t, halo, mixT, idx, Tt, hid):
        Bd.S.dma("sp", mixT[:, :, 0:NS], oT_s.t[:, :].rearrange("(kc p) t -> p kc t", p=128), reads=[oT_s],
                 writes=[mixT])

    JDs = Job(TB=NS, NB=1, ntok=NS, KC=8, tokmajor_in=False, load_mix_T=load_mix_T_sample, halo=False,
              xres=x1_s.t, xres_buf=x1_s, w_out=wb_outB.t, w_out_buf=wb_outB, w_up=wb_up.t[1], w_up_buf=wb_up,
              w_down=wb_down.t[1], w_down_buf=wb_down, convp=convp.t[1], convp_buf=convp, ln=ln4.t[1], ln_buf=ln4,
              aprev0=sconv.t[1], aprev0_buf=sconv, idx=None, y=yS.t, y_buf=yS, ybf=None, ybf_buf=None,
              conv_out=sconv_o.t[1], conv_out_buf=sconv_o)
    B.phaseB(JDs, C)

    if stop <= 8:
        S.final_wait()
        return B
    srcn = Job(x=x1_all.t, x_buf=x1_all, out_row0=0, xrow=lambda tok: gmap(tok % SEG, tok // SEG, 512))
    JCp = Job(w=w_qkvP, bf=bfP.t[0, :], bf_buf=bfP, segments=[("new", SEQ // 128, 128, srcn)],
              kpos0={"new": 0}, pk=pk.t, pk_buf=pk, pv=pv.t, pv_buf=pv, plf=plf.t, plf_buf=plf,
              KT=KTp.t, KT_buf=KTp, VV=VVp.t, VV_buf=VVp, QT=QTp.t, QT_buf=QTp)
    B.phaseC1(JCp, C)
    JC2p = Job(NK=SEQ, NQ=SEQ, TQ=512, TBq=128, ncache=0, KT=KTp.t, KT_buf=KTp, VV=VVp.t, VV_buf=VVp, QT=QTp.t,
               QT_buf=QTp, oT_dst=lambda hh, qt: oT_loc.t[(hh * NQT + qt) * 64:(hh * NQT + qt + 1) * 64, :],
               oT_buf=oT_loc,
               after_unit=lambda u: exch_chunk(oT_loc, oT_all, u // 16, 1024) if u % 16 == 15 else None)
    B.phaseC2(JC2p, C)
    if stop <= 9:
        S.final_wait()
        return B
    pass

    if stop <= 10:
        S.final_wait()
        return B
    def load_mix_T_prompt(Bd, t, halo, mixT, idx, Tt, hid):
        if halo:
            for kc in range(8):
                Bd.S.dma("pool", hid[:, kc, :], oT_all.t[:, :], reads=[oT_all, idx], writes=[hid],
                         indirect=idx[:, ntile_B * 8 + kc:ntile_B * 8 + kc + 1])
            Bd.cp("dve", mixT[:, :, 0:64], hid[:, 0:8, 448:512], [hid], [mixT])
        else:
            for kc in range(8):
                Bd.S.dma("pool", mixT[:, kc, :], oT_all.t[:, :], reads=[oT_all, idx], writes=[mixT],
                         indirect=idx[:, t * 8 + kc:t * 8 + kc + 1])

    def load_xres_halo_D(Bd, xres, idx, xhb_s):
        Bd.S.dma("pool", xhb_s[0:64, :], x1_all.t[:, :], reads=[x1_all, idx], writes=[xhb_s],
                 indirect=idx[0:64, (ntile_B + 1) * 8:(ntile_B + 1) * 8 + 1])
        Bd.cp("dve", xres[0:64, 0, :], xhb_s[0:64, :], [xhb_s], [xres])

    JDp = Job(TB=128, NB=4, ntok=SEG, KC=8, tokmajor_in=False, load_mix_T=load_mix_T_prompt, halo=True,
              load_xres_halo=load_xres_halo_D,
              xres=x1_loc.t, xres_buf=x1_loc, w_out=wb_outB.t, w_out_buf=wb_outB, w_up=wb_up.t[1], w_up_buf=wb_up,
              w_down=wb_down.t[1], w_down_buf=wb_down, convp=convp.t[1], convp_buf=convp, ln=ln4.t[1], ln_buf=ln4,
              aprev0=None, aprev0_buf=None, idx=idxD_in, y=yP.t, y_buf=yP, ybf=None, ybf_buf=None,
              conv_out=pconv.t[1], conv_out_buf=pconv)
    B.phaseB(JDp, C)
    S.final_wait()
    return B


def _kc_layout(wm):
    K, N = wm.shape
    return np.ascontiguousarray(wm.reshape(K // 128, 128, N).transpose(1, 0, 2))


def _rope_tables(pos, chunk, h):
    half = 128
    inv = (1.0 / (10000.0 ** (np.arange(half, dtype=np.float32) / np.float32(half)))).astype(np.float32)
    ang = pos.astype(np.float32)[:, None] * inv[None, :]
    cos = np.cos(ang).astype(np.float32).T
    sin = np.sin(ang).astype(np.float32).T
    lg = np.log1p(-np.exp2(-5.0 - h))
    i = (np.arange(len(pos)) % chunk).astype(np.float64)
    dq = np.exp((i + 1.0) * lg).astype(np.float32)[None, :]
    dk = (np.exp(-(i + 1.0) * lg) * (256.0 ** -0.5)).astype(np.float32)[None, :]
    tab = np.stack([cos * dq, sin * dq, cos * dk, sin * dk]).astype(np.float32)
    gam = np.float32(np.exp(chunk * lg))
    return tab, gam


def _gmap(rho, h, CR):
    return (rho // CR) * 4 * CR + h * CR + (rho % CR)


def _consts():
    j = np.arange(128)[:, None]
    i = np.arange(128)[None, :]
    ident = (i == j).astype(np.float32)
    maskT = (i >= j).astype(np.float32)
    tri = (j <= i).astype(np.float32)
    negm = np.where(i >= j, 0.0, NEGM).astype(np.float32)
    ones = np.ones((128, 128), np.float32)
    return np.stack([ident, maskT, tri, negm, ones])


def make_in_maps(inp, SEQ):
    f = lambda a: np.ascontiguousarray(np.asarray(a, dtype=np.float32))
    SEG = SEQ // 4
    NQT = SEQ // 512
    NTB_B = SEG // 128
    ntile_B = SEG // 512
    xp, xs = f(inp["x_prompt"]), f(inp["x_sample"])
    w_in = f(inp["w_in_a"])[0]
    HK, HV = 1024, 2048
    lnrg, lnrb = f(inp["ln_ret_g"])[0], f(inp["ln_ret_b"])[0]
    w_kvf, w_q = f(inp["w_kvf"]), f(inp["w_q_b"])[0]
    b_f = f(inp["b_f"])

    def w_in_head(h):
        cols = np.concatenate([np.arange(h * 256, (h + 1) * 256), HK + np.arange(h * 256, (h + 1) * 256),
                               2 * HK + np.arange(h * 512, (h + 1) * 512),
                               2 * HK + HV + np.arange(h * 512, (h + 1) * 512)])
        return _kc_layout(w_in[:, cols])

    def w_qkv_group(g):
        c = np.arange(g * 256, (g + 1) * 256)
        return _kc_layout(np.concatenate([w_q[:, c], w_kvf[:, c], w_kvf[:, 1024 + c],
                                          w_kvf[:, 2048 + g * 4:2048 + (g + 1) * 4],
                                          np.zeros((1024, 4), np.float32)], axis=1))

    w_in_heads = [w_in_head(h) for h in range(4)]
    w_qkv_groups = [w_qkv_group(g) for g in range(4)]
    ropeS = np.stack([_rope_tables(PAST + np.arange(NS), NS, h)[0] for h in range(4)])
    gamS = np.stack([np.full((128, 1), _rope_tables(np.arange(1), NS, h)[1], np.float32) for h in range(4)])
    lnretS = np.stack([np.stack([lnrg[h * 512:(h + 1) * 512], lnrb[h * 512:(h + 1) * 512]]) for h in range(4)])
    w_outA = f(inp["w_out_a"])[0]
    w_outA_l = np.stack([_kc_layout(w_outA[:, hf * 512:(hf + 1) * 512]) for hf in range(2)])
    w_outB = f(inp["w_out_b"])[0]
    w_outB_l = np.stack([_kc_layout(w_outB[:, hf * 512:(hf + 1) * 512]) for hf in range(2)])
    wu = f(inp["w_up"])
    w_up_l = np.empty((2, NFC, 128, 8, 256), np.float32)
    for l in range(2):
        for j in range(NFC):
            w_up_l[l, j, :, :, 0:128] = _kc_layout(wu[l][:, j * 128:(j + 1) * 128])
            w_up_l[l, j, :, :, 128:256] = _kc_layout(wu[l][:, DFF + j * 128:DFF + (j + 1) * 128])
    wdn = f(inp["w_down"])
    w_down_l = np.stack([np.stack([_kc_layout(wdn[l][:, hf * 512:(hf + 1) * 512]) for hf in range(2)])
                         for l in range(2)])
    cw, cbias = f(inp["conv_w"]), f(inp["conv_b"])
    convp = np.empty((2, 128, NFC, 4), np.float32)
    for l in range(2):
        for k in range(3):
            convp[l, :, :, k] = cw[l, k].reshape(NFC, 128).T
        convp[l, :, :, 3] = cbias[l].reshape(NFC, 128).T
    ln4 = np.stack([np.stack([f(inp["ln_mix_g"])[l], f(inp["ln_mix_b"])[l], f(inp["ln_ffn_g"])[l],
                              f(inp["ln_ffn_b"])[l]]) for l in range(2)])
    sfc = f(inp["state_ffn_conv"])
    cst = _consts()
    ckk, cvv, clf = f(inp["cache_k"]), f(inp["cache_v"]), f(inp["cache_logf"])
    sret = f(inp["state_ret"])[0]
    maps = []
    for c in range(8):
        b, r = c // 4, c % 4
        m = {}
        m["xA"] = xp[b]
        m["xB"] = np.ascontiguousarray(xp[b, r * SEG:(r + 1) * SEG])
        m["xH"] = np.ascontiguousarray(xp[b, r * SEG - 64:r * SEG]) if r > 0 else np.zeros((64, D), np.float32)
        m["xS"] = xs[c]
        m["w_inP"] = w_in_heads[r]
        for h in range(4):
            m["w_inS%d" % h] = w_in_heads[h]
        tab, gam = _rope_tables(np.arange(SEQ), 128, r)
        m["ropeP"] = tab
        m["ropeS"] = ropeS
        m["gamP"] = np.full((128, 1), gam, np.float32)
        m["gamS"] = gamS
        m["lnretP"] = lnretS[r]
        m["lnretS"] = lnretS
        m["sret"] = sret[c]
        m["w_outA"] = w_outA_l
        m["w_up"] = w_up_l
        m["w_down"] = w_down_l
        m["convp"] = convp
        m["ln4"] = ln4
        m["sconv"] = np.ascontiguousarray(
            np.stack([sfc[l, c].reshape(2, NFC, 128).transpose(2, 1, 0) for l in range(2)]))
        m["w_qkvP"] = w_qkv_groups[r]
        for g in range(4):
            m["w_qkvS%d" % g] = w_qkv_groups[g]
        m["bfP"] = b_f[r * 4:(r + 1) * 4].reshape(1, 4)
        m["bfS"] = b_f.reshape(4, 4)
        m["w_outB"] = w_outB_l
        m["ck"] = ckk[c].reshape(PAST, D)
        m["cv"] = cvv[c].reshape(PAST, D)
        m["clf"] = clf[c]
        m["flag"] = np.full((128, 1), 1.0 if r > 0 else 0.0, np.float32)
        p = np.arange(128)
        idxB = np.zeros((128, (NTB_B + 1) * 4), np.int32)
        for tbi in range(NTB_B):
            for h in range(4):
                idxB[:, tbi * 4 + h] = _gmap(r * SEG + tbi * 128 + p, h, 1024)
        h0 = max(r * SEG - 64, 0)
        for h in range(4):
            idxB[:, NTB_B * 4 + h] = _gmap(h0 + (p % 64), h, 1024)
        m["idxB"] = idxB
        idxD = np.zeros((128, (ntile_B + 1) * 8 + 1), np.int32)
        R = 4 * NQT * 64
        for t in range(ntile_B + 1):
            qt = r * ntile_B + t if t < ntile_B else max(r * ntile_B - 1, 0)
            for kc in range(8):
                g = kc // 2
                hh = 2 * (kc % 2) + p // 64
                idxD[:, t * 8 + kc] = _gmap((hh * NQT + qt) * 64 + (p % 64), g, 1024)
        tokh = h0 + (p % 64)
        idxD[:, (ntile_B + 1) * 8] = _gmap(tokh % SEG, tokh // SEG, 512)
        m["idxD"] = idxD
        m["cst"] = cst
        maps.append({k: np.ascontiguousarray(v) for k, v in m.items()})
    return maps


def assemble(res, SEQ, Bp=2):
    SEG = SEQ // 4
    R = [r for r in res]
    g = lambda c, k: np.asarray(R[c][k], dtype=np.float32)
    y_p = np.stack([np.concatenate([g(b * 4 + r, "yP") for r in range(4)], 0) for b in range(Bp)])
    y_s = np.stack([g(c, "yS") for c in range(8)])
    ret_p = np.stack([np.stack([g(b * 4 + r, "retP") for r in range(4)]) for b in range(Bp)])[None]
    k_p = np.stack([np.concatenate([g(b * 4 + r, "pk").reshape(SEQ, 4, 64) for r in range(4)], 1) for b in range(Bp)])
    v_p = np.stack([np.concatenate([g(b * 4 + r, "pv").reshape(SEQ, 4, 64) for r in range(4)], 1) for b in range(Bp)])
    lf_p = np.stack([np.concatenate([g(b * 4 + r, "plf") for r in range(4)], 1) for b in range(Bp)])

    def conv_fix(a):
        return a.transpose(2, 1, 0).reshape(2, DFF)

    conv_p = np.stack([np.stack([conv_fix(g(b * 4 + 3, "pconv")[l]) for b in range(Bp)]) for l in range(2)])
    ret_s = np.stack([g(c, "retS") for c in range(8)])[None]
    k_s = np.stack([g(c, "sk").transpose(1, 0, 2).reshape(NS, 16, 64) for c in range(8)])
    v_s = np.stack([g(c, "sv").transpose(1, 0, 2).reshape(NS, 16, 64) for c in range(8)])
    lf_s = np.stack([g(c, "slf").transpose(1, 0, 2).reshape(NS, 16) for c in range(8)])
    conv_s = np.stack([np.stack([conv_fix(g(c, "sconv_o")[l]) for c in range(8)]) for l in range(2)])
    return (y_p, y_s, ret_p, k_p, v_p, lf_p, conv_p, ret_s, k_s, v_s, lf_s, conv_s)


_CACHE = {}


def kernel(**inputs):
    SEQ = int(np.asarray(inputs["x_prompt"]).shape[1])
    if SEQ not in _CACHE:
        import os
        _CACHE[SEQ] = build_program(SEQ, int(os.environ.get("KSTOP", "99")))
    B = _CACHE[SEQ]
    in_maps = make_in_maps(inputs, SEQ)
    import os
    if os.environ.get("KTRACE"):
        res = run_bass_kernel_spmd(B.nc, in_maps, core_ids=list(range(8)), trace=True)
        print("EXEC_TIME_NS", res.exec_time_ns, flush=True)
    else:
        res = run_bass_kernel_spmd(B.nc, in_maps, core_ids=list(range(8)))
    return assemble(res.results, SEQ)
```

```python
import math
from contextlib import ExitStack
import numpy as np
import ml_dtypes
import concourse.bass as bass
import concourse.mybir as mybir
from concourse.bass_utils import run_bass_kernel_spmd

F32 = mybir.dt.float32
BF16 = mybir.dt.bfloat16
I32 = mybir.dt.int32
AF = mybir.ActivationFunctionType
ALU = mybir.AluOpType
AX = mybir.AxisListType

D = 1024
DFF = 2816
NFC = 22
PAST = 2048
NS = 64
LN_EPS = 1e-5
ALPHA = 4.0 ** 0.25
GROUPS = [[0, 1, 2, 3], [4, 5, 6, 7]]
NEGM = -30000.0


class Buf:
    __slots__ = ("name", "t", "wtok", "rtok", "dsem", "dcnt")

    def __init__(self, name, t):
        self.name = name
        self.t = t
        self.wtok = {}
        self.rtok = {}
        self.dsem = None
        self.dcnt = 0

    def __getitem__(self, k):
        return self.t[k]


class Sched:
    ENG = ("pe", "act", "dve", "pool", "sp")

    def __init__(self, nc):
        self.nc = nc
        self.e = {"pe": nc.tensor, "act": nc.scalar, "dve": nc.vector, "pool": nc.gpsimd, "sp": nc.sync}
        self.sem = {k: nc.alloc_semaphore("prog_" + k) for k in self.ENG}
        self.cnt = {k: 0 for k in self.ENG}
        self.seen = {k: {} for k in self.ENG}
        self.cc_sem = nc.alloc_semaphore("cc_sem")
        self.cc_cnt = 0
        self.dma_bufs = []
        self.sem_pool = []
        self.nsem = 0
        self.ninst = 0
        self.nwait = 0
        self.semeng = {id(self.sem[k]): k for k in self.ENG}
        self.vc = {k: {} for k in self.ENG}

    def _dsem(self, b):
        if b.dsem is None:
            if self.sem_pool:
                b.dsem, b.dcnt = self.sem_pool.pop()
            else:
                b.dsem = self.nc.alloc_semaphore("dq%d" % self.nsem)
                self.nsem += 1
            self.dma_bufs.append(b)
        return b.dsem

    def release(self, bufs):
        for b in bufs:
            if b.dsem is not None:
                self.sem_pool.append((b.dsem, b.dcnt))
                self.dma_bufs.remove(b)
                b.dsem = None

    def _wait(self, eng, deps, skip_self=False, attach=False):
        E = self.e[eng]
        seen = self.seen[eng]
        own = self.sem[eng]
        cand = []
        for sem, val in deps.items():
            if skip_self and sem is own:
                continue
            if seen.get(sem, 0) < val:
                pe = self.semeng.get(id(sem))
                snap = self.vc[pe].get(val) if pe is not None else None
                cand.append((sem, val, snap))
        need = []
        for k, (sem, val, snap) in enumerate(cand):
            implied = False
            for k2, (s2, v2, snap2) in enumerate(cand):
                if k2 != k and snap2 is not None and snap2.get(sem, 0) >= val:
                    implied = True
                    break
            if not implied:
                need.append((sem, val))
        for sem, val, snap in cand:
            if seen.get(sem, 0) < val:
                seen[sem] = val
            if snap is not None:
                for s2, v2 in snap.items():
                    if seen.get(s2, 0) < v2:
                        seen[s2] = v2
        last = need.pop() if (attach and need) else None
        for sem, val in need:
            E.wait_ge(sem, val)
            self.nwait += 1
        return last

    @staticmethod
    def _collect(reads, writes):
        deps = {}
        for r in reads:
            for s, v in r.wtok.items():
                if deps.get(s, 0) < v:
                    deps[s] = v
        for w in writes:
            for s, v in w.wtok.items():
                if deps.get(s, 0) < v:
                    deps[s] = v
            for s, v in w.rtok.items():
                if deps.get(s, 0) < v:
                    deps[s] = v
        return deps

    @staticmethod
    def _record(tok, reads, writes):
        s, v = tok
        for r in reads:
            if r.rtok.get(s, 0) < v:
                r.rtok[s] = v
        for w in writes:
            w.wtok = {s: v}
            w.rtok = {}

    def op(self, eng, fn, reads=(), writes=(), signal=True):
        deps = self._collect(reads, writes)
        last = self._wait(eng, deps, skip_self=(eng == "pe"), attach=True)
        ins = fn(self.e[eng])
        if last is not None:
            ins._wait_ge(last[0], last[1])
        self.ninst += 1
        if signal:
            self.cnt[eng] += 1
            ins.then_inc(self.sem[eng], 1)
            tok = (self.sem[eng], self.cnt[eng])
            snap = {self.sem[k]: self.seen[eng].get(self.sem[k], 0) for k in self.ENG}
            self.vc[eng][self.cnt[eng]] = snap
        else:
            tok = (self.sem[eng], self.cnt[eng] + 1)
        self._record(tok, reads, writes)
        return ins

    def dma(self, q, out_ap, in_ap, reads=(), writes=(), indirect=None):
        dst = writes[0]
        deps = self._collect(reads, writes)
        last = self._wait(q, deps, attach=(indirect is None))
        sem = self._dsem(dst)
        dst.dcnt += 1
        if indirect is not None:
            self.e[q].indirect_dma_start(out=out_ap, out_offset=None, in_=in_ap,
                                         in_offset=bass.IndirectOffsetOnAxis(ap=indirect, axis=0)).then_inc(sem, 16)
        else:
            ins = self.e[q].dma_start(out=out_ap, in_=in_ap)
            if last is not None:
                ins._wait_ge(last[0], last[1])
            ins.then_inc(sem, 16)
        self.ninst += 1
        s, v = sem, 16 * dst.dcnt
        for r in reads:
            if r.rtok.get(s, 0) < v:
                r.rtok[s] = v
        dst.wtok = {s: v}
        dst.rtok = {}

    def collective(self, kind, in_buf, out_buf, in_ap, out_ap):
        deps = self._collect([in_buf], [out_buf])
        self._wait("pool", deps)
        self.cc_cnt += 1
        self.nc.gpsimd.collective_compute(kind, ALU.bypass, replica_groups=GROUPS, ins=[in_ap],
                                          outs=[out_ap]).then_inc(self.cc_sem, 1)
        self._record((self.cc_sem, self.cc_cnt), [in_buf], [out_buf])

    def _all_tokens(self, with_cc):
        deps = {self.sem[k]: self.cnt[k] for k in self.ENG if self.cnt[k] > 0}
        for b in self.dma_bufs:
            if b.dcnt:
                deps[b.dsem] = 16 * b.dcnt
        for s, c in self.sem_pool:
            if c:
                deps[s] = 16 * c
        if with_cc and self.cc_cnt:
            deps[self.cc_sem] = self.cc_cnt
        return deps

    def barrier(self):
        deps = self._all_tokens(False)
        for k in self.ENG:
            self._wait(k, deps)

    def final_wait(self):
        self._wait("sp", self._all_tokens(True))


class Job:
    def __init__(self, **kw):
        self.__dict__.update(kw)


class Builder:
    def __init__(self, SEQ):
        self.SEQ = SEQ
        self.SEG = SEQ // 4
        self.nc = nc = bass.Bass("TRN2", target_bir_lowering=False)
        self.S = Sched(nc)
        self.uid = 0
        self.inputs = {}
        self.outputs = {}
        self.ps = [Buf("ps%d" % i, nc.alloc_psum_tensor("ps%d" % i, [128, 512], F32)) for i in range(8)]
        self.epsc = nc.alloc_sbuf_tensor("epsc", [128, 1], F32)
        self.epsb = Buf("epsc", self.epsc)
        self.S.op("pool", lambda e: e.memset(self.epsc[:], LN_EPS), writes=[self.epsb])

    def din(self, name, shape, dt=F32):
        h = self.nc.dram_tensor(name, list(shape), dt, kind="ExternalInput")
        self.inputs[name] = (tuple(shape), dt)
        return Buf(name, h)

    def dout(self, name, shape, dt=F32):
        h = self.nc.dram_tensor(name, list(shape), dt, kind="ExternalOutput")
        self.outputs[name] = (tuple(shape), dt)
        return Buf(name, h)

    def dscr(self, name, shape, dt):
        h = self.nc.dram_tensor(name, list(shape), dt, kind="Internal")
        return Buf(name, h)

    def sbuf(self, es, name, shape, dt, lst):
        self.uid += 1
        h = es.enter_context(self.nc.sbuf_tensor("%s_%d" % (name, self.uid), list(shape), dt))
        b = Buf("%s_%d" % (name, self.uid), h)
        lst.append(b)
        return b

    def mm(self, out, lhsT, rhs, start, stop, reads, writes):
        self.S.op("pe", lambda e: e.matmul(out, lhsT, rhs, start=start, stop=stop), reads=reads, writes=writes,
                  signal=bool(stop))

    def tr(self, out, in_, ident, reads, writes, signal):
        self.S.op("pe", lambda e: e.transpose(out, in_, ident), reads=reads, writes=writes, signal=signal)

    def act(self, out, in_, func, reads, writes, bias=None, scale=None, accum_out=None):
        kw = {}
        if bias is not None:
            kw["bias"] = bias
        if scale is not None:
            kw["scale"] = scale
        if accum_out is not None:
            kw["accum_out"] = accum_out
        self.S.op("act", lambda e: e.activation(out=out, in_=in_, func=func, **kw), reads=reads, writes=writes)

    def tt(self, eng, out, in0, in1, op, reads, writes):
        self.S.op(eng, lambda e: e.tensor_tensor(out=out, in0=in0, in1=in1, op=op), reads=reads, writes=writes)

    def ts(self, eng, out, in0, s1, op0, reads, writes, s2=None, op1=None):
        if op1 is None:
            self.S.op(eng, lambda e: e.tensor_scalar(out=out, in0=in0, scalar1=s1, scalar2=None, op0=op0),
                      reads=reads, writes=writes)
        else:
            self.S.op(eng, lambda e: e.tensor_scalar(out=out, in0=in0, scalar1=s1, scalar2=s2, op0=op0, op1=op1),
                      reads=reads, writes=writes)

    def stt(self, eng, out, in0, scalar, in1, op0, op1, reads, writes):
        self.S.op(eng, lambda e: e.scalar_tensor_tensor(out=out, in0=in0, scalar=scalar, in1=in1, op0=op0, op1=op1),
                  reads=reads, writes=writes)

    def cp(self, eng, out, in_, reads, writes):
        if eng == "act":
            self.S.op("act", lambda e: e.copy(out=out, in_=in_), reads=reads, writes=writes)
        else:
            self.S.op(eng, lambda e: e.tensor_copy(out=out, in_=in_), reads=reads, writes=writes)

    def ln_rows(self, src_ap_fn, nrows, width, src_bufs, stat, es_name=""):
        nchunk = (width + 511) // 512
        st = stat
        for c in range(nchunk):
            lo, hi = c * 512, min(width, (c + 1) * 512)
            self.S.op("dve", lambda e, c=c, lo=lo, hi=hi: e.bn_stats(out=st[0:nrows, 8 + 6 * c: 14 + 6 * c],
                                                                     in_=src_ap_fn(lo, hi)),
                      reads=src_bufs, writes=[st])
        self.S.op("dve", lambda e: e.bn_aggr(out=st[0:nrows, 4:6],
                                             in_=st[0:nrows, 8:8 + 6 * nchunk].rearrange("p (c s) -> p c s", s=6)),
                  reads=[st], writes=[st])
        self.act(st[0:nrows, 2:3], st[0:nrows, 5:6], AF.Sqrt, [st, self.epsb], [st], bias=self.epsc[0:nrows, 0:1],
                 scale=1.0)
        self.S.op("dve", lambda e: e.reciprocal(out=st[0:nrows, 0:1], in_=st[0:nrows, 2:3]), reads=[st], writes=[st])
        self.stt("dve", st[0:nrows, 1:2], st[0:nrows, 4:5], -1.0, st[0:nrows, 0:1], ALU.mult, ALU.mult, [st], [st])
        return st[0:nrows, 0:1], st[0:nrows, 1:2]

    def phaseA(self, J, C):
        nc, S, ps = self.nc, self.S, self.ps
        TB, NB = J.TB, J.NB
        T = TB * NB
        ntile = J.ntok // T
        bufs = []
        with ExitStack() as es:
            sb = lambda n, s, d: self.sbuf(es, n, s, d, bufs)
            w = sb("Aw", [128, 8, 1536], BF16)
            xtok = [sb("Axtok", [128, NB, 1024], BF16) for _ in range(2)]
            rope = [sb("Arope", [128, 4, T], F32) for _ in range(2)]
            xT = sb("AxT", [128, 8, T], BF16)
            t1 = sb("At1", [128, T], F32)
            t2 = sb("At2", [128, T], F32)
            qdT = sb("AqdT", [128, 2, T], BF16)
            kiT = sb("AkiT", [128, 2, T], BF16)
            kitok = sb("Akitok", [128, NB, 256], BF16)
            vtok = sb("Avtok", [128, NB, 512], BF16)
            gs = sb("Ags", [128, NB, 512], F32)
            pT = sb("ApT", [128, NB, TB], BF16)
            Sf = sb("ASf", [128, 2, 512], F32)
            Sb = sb("ASb", [128, 2, 512], BF16)
            tmp = sb("Atmp", [128, 2, 512], F32)
            on = [sb("Aon", [128, 512], F32) for _ in range(2)]
            og = [sb("Aog", [128, NB, 512], BF16) for _ in range(2)]
            lnp = sb("Alnp", [128, 2, 512], F32)
            gam = sb("Agam", [128, 1], F32)
            stat = [sb("Astat", [128, 32], F32) for _ in range(2)]

            S.dma("pool", w[:], J.w.t[:], reads=[J.w], writes=[w])
            S.dma("sp", lnp[:, 0, :], J.lng.partition_broadcast(128), reads=[J.lnb_buf], writes=[lnp])
            S.dma("sp", lnp[:, 1, :], J.lnb.partition_broadcast(128), reads=[J.lnb_buf], writes=[lnp])
            S.dma("sp", gam[:], J.gam, reads=[J.gam_buf], writes=[gam])
            if J.S0 is not None:
                S.dma("sp", Sf[:], J.S0.rearrange("(h p) e -> p h e", p=128), reads=[J.S0_buf], writes=[Sf])
            else:
                S.op("pool", lambda e: e.memset(Sf[:], 0.0), writes=[Sf])
            self.cp("pool", Sb[:], Sf[:], [Sf], [Sb])

            def load(t):
                xs = J.x[t * T:(t + 1) * T, :].rearrange("(nb p) f -> p nb f", p=TB)
                S.dma("pool", xtok[t % 2][0:TB, :, :], xs, reads=[J.x_buf], writes=[xtok[t % 2]])
                S.dma("sp", rope[t % 2][:], J.rope[:, :, t * T:(t + 1) * T].rearrange("a p t -> p a t"),
                      reads=[J.rope_buf], writes=[rope[t % 2]])

            load(0)
            for t in range(ntile):
                if t + 1 < ntile:
                    load(t + 1)
                xk, rp = xtok[t % 2], rope[t % 2]
                for kc in range(8):
                    pb = ps[kc % 2]
                    pv = pb[:].bitcast(BF16)
                    for nb in range(NB):
                        self.tr(pv[:, nb * TB:(nb + 1) * TB], xk[0:TB, nb, kc * 128:(kc + 1) * 128],
                                C.identb[0:TB, 0:TB], [xk, C.identb_buf], [pb], signal=(nb == NB - 1))
                    self.cp("act" if kc % 2 else "dve", xT[:, kc, :], pv[:, 0:T], [pb], [xT])
                for qk in range(2):
                    pa, pbk = ps[2], ps[3]
                    for half, pp in enumerate((pa, pbk)):
                        col = qk * 256 + half * 128
                        for kc in range(8):
                            self.mm(pp[:, 0:T], w[:, kc, col:col + 128], xT[:, kc, :], kc == 0, kc == 7,
                                    [w, xT], [pp])
                    cs, sn = rp[:, 2 * qk, :], rp[:, 2 * qk + 1, :]
                    dst = qdT if qk == 0 else kiT
                    self.tt("dve", t1[:], pa[:, 0:T], cs, ALU.mult, [pa, rp], [t1])
                    self.tt("dve", t2[:], pbk[:, 0:T], sn, ALU.mult, [pbk, rp], [t2])
                    self.tt("pool", dst[:, 0, :], t1[:], t2[:], ALU.subtract, [t1, t2], [dst])
                    self.tt("dve", t1[:], pbk[:, 0:T], cs, ALU.mult, [pbk, rp], [t1])
                    self.tt("dve", t2[:], pa[:, 0:T], sn, ALU.mult, [pa, rp], [t2])
                    self.tt("pool", dst[:, 1, :], t1[:], t2[:], ALU.add, [t1, t2], [dst])
                pb = ps[0]
                pv = pb[:].bitcast(BF16)
                for c in range(NB):
                    for half in range(2):
                        self.tr(pv[0:TB, (c * 2 + half) * 128:(c * 2 + half + 1) * 128],
                                kiT[:, half, c * TB:(c + 1) * TB], C.identb[:, :], [kiT, C.identb_buf], [pb],
                                signal=(c == NB - 1 and half == 1))
                self.cp("act", kitok[0:TB, :, :], pv[0:TB, 0:NB * 256].rearrange("p (c d) -> p c d", d=256),
                        [pb], [kitok])
                pb = ps[1]
                for c in range(NB):
                    for half in range(2):
                        self.mm(pb[0:TB, c * TB:(c + 1) * TB], kiT[:, half, c * TB:(c + 1) * TB],
                                qdT[:, half, c * TB:(c + 1) * TB], half == 0, half == 1, [kiT, qdT], [pb])
                for c in range(NB):
                    self.tt("dve", pT[0:TB, c, :], pb[0:TB, c * TB:(c + 1) * TB], C.maskT[0:TB, 0:TB], ALU.mult,
                            [pb, C.maskT_buf], [pT])
                for nb in range(NB):
                    pp = ps[2 + nb % 2]
                    for kc in range(8):
                        self.mm(pp[0:TB, :], xT[:, kc, nb * TB:(nb + 1) * TB], w[:, kc, 512:1024], kc == 0, kc == 7,
                                [xT, w], [pp])
                    self.cp("act", vtok[0:TB, nb, :], pp[0:TB, :], [pp], [vtok])
                for nb in range(NB):
                    pp = ps[2 + nb % 2]
                    for kc in range(8):
                        self.mm(pp[0:TB, :], xT[:, kc, nb * TB:(nb + 1) * TB], w[:, kc, 1024:1536], kc == 0, kc == 7,
                                [xT, w], [pp])
                    self.act(gs[0:TB, nb, :], pp[0:TB, :], AF.Sigmoid, [pp], [gs])
                    self.tt("dve", gs[0:TB, nb, :], gs[0:TB, nb, :], pp[0:TB, :], ALU.mult, [gs, pp], [gs])
                ogt = og[t % 2]
                for c in range(NB):
                    po = ps[4 + c % 2]
                    self.mm(po[0:TB, :], pT[0:TB, c, :], vtok[0:TB, c, :], True, False, [pT, vtok], [po])
                    for half in range(2):
                        self.mm(po[0:TB, :], qdT[:, half, c * TB:(c + 1) * TB], Sb[:, half, :], False, half == 1,
                                [qdT, Sb], [po])
                    for half in range(2):
                        pd = ps[6 + half]
                        self.mm(pd[:, :], kitok[0:TB, c, half * 128:(half + 1) * 128], vtok[0:TB, c, :], True, True,
                                [kitok, vtok], [pd])
                    for half in range(2):
                        self.act(tmp[:, half, :], ps[6 + half][:, :], AF.Identity, [ps[6 + half], gam], [tmp],
                                 scale=gam[:, 0:1])
                    self.stt("dve", Sf[:], Sf[:], gam[:, 0:1], tmp[:], ALU.mult, ALU.add, [Sf, tmp, gam], [Sf])
                    self.cp("pool", Sb[:], Sf[:], [Sf], [Sb])
                    st = stat[c % 2]
                    rstd, nbias = self.ln_rows(lambda lo, hi, po=po: po[0:TB, lo:hi], TB, 512, [po], st)
                    o_n = on[c % 2]
                    self.act(o_n[0:TB, :], po[0:TB, :], AF.Identity, [po, st], [o_n], bias=nbias, scale=rstd)
                    self.tt("pool", o_n[0:TB, :], o_n[0:TB, :], lnp[0:TB, 0, :], ALU.mult, [o_n, lnp], [o_n])
                    self.tt("pool", o_n[0:TB, :], o_n[0:TB, :], lnp[0:TB, 1, :], ALU.add, [o_n, lnp], [o_n])
                    self.tt("dve", ogt[0:TB, c, :], o_n[0:TB, :], gs[0:TB, c, :], ALU.mult, [o_n, gs], [ogt])
                S.dma("pool", J.og_dst(t * T, T).rearrange("(nb p) e -> p nb e", p=TB), ogt[0:TB, :, :],
                      reads=[ogt], writes=[J.og_buf])
                if getattr(J, "after_tile", None) is not None:
                    J.after_tile(t)
            S.dma("pool", J.Sout.rearrange("(h p) e -> p h e", p=128), Sf[:], reads=[Sf], writes=[J.Sout_buf])
            S.barrier()
            S.release(bufs)

    def phaseB(self, J, C):
        nc, S, ps = self.nc, self.S, self.ps
        TB, NB = J.TB, J.NB
        T = TB * NB
        ntile = J.ntok // T
        KC = J.KC
        bufs = []
        with ExitStack() as es:
            sb = lambda n, s, d: self.sbuf(es, n, s, d, bufs)
            mixT = sb("BmixT", [128, KC, T], BF16)
            mtok = sb("Bmtok", [128, 2048], BF16) if J.tokmajor_in else None
            xres = sb("Bxres", [128, NB, 1024], F32)
            xmid = sb("Bxmid", [128, NB, 1024], F32)
            xmb = [sb("Bxmb", [128, 1024], BF16) for _ in range(2)]
            xmT = sb("BxmT", [128, 8, T], BF16)
            hid = sb("Bhid", [128, NFC, T], BF16)
            aext = [sb("Baext", [128, T + 2], F32) for _ in range(2)]
            u = [sb("Bu", [128, T], F32) for _ in range(2)]
            ge = [sb("Bge", [128, T], F32) for _ in range(2)]
            aprev = sb("Baprev", [128, NFC, 2], F32)
            wo = [sb("Bwo", [128, 8, 512], BF16) for _ in range(2)]
            wu = [sb("Bwu", [128, 8, 256], BF16) for _ in range(3)]
            wd = [sb("Bwd", [128, 11, 512], BF16) for _ in range(2)]
            lnt = sb("Blnt", [128, 4, 1024], F32)
            cvp = sb("Bcvp", [128, NFC, 4], F32)
            yout = [sb("Byout", [128, 1024], F32) for _ in range(2)]
            stat = [sb("Bstat", [128, 32], F32) for _ in range(2)]
            flag = sb("Bflag", [128, 1], F32)
            idx = sb("Bidx", [128, J.idx.t.shape[1]], I32) if J.idx is not None else None
            xhb_s = sb("Bxhb", [64, 1024], BF16) if J.halo else None

            for i in range(4):
                S.dma("sp", lnt[:, i, :], J.ln[i, :].partition_broadcast(128), reads=[J.ln_buf], writes=[lnt])
            S.dma("sp", cvp[:], J.convp, reads=[J.convp_buf], writes=[cvp])
            S.dma("sp", flag[:], C.flag.t[:], reads=[C.flag], writes=[flag])
            if idx is not None:
                S.dma("sp", idx[:], J.idx.t[:], reads=[J.idx], writes=[idx])
            if J.aprev0 is not None:
                S.dma("sp", aprev[:], J.aprev0, reads=[J.aprev0_buf], writes=[aprev])
            else:
                S.op("pool", lambda e: e.memset(aprev[:], 0.0), writes=[aprev])

            def run_tile(tb, nb_, t, halo):
                Tt = tb * nb_
                if J.tokmajor_in:
                    for nb in range(nb_):
                        J.load_mix_tok(self, t, nb, tb, halo, mtok, idx)
                        for kc in range(KC):
                            pb = ps[4 + (kc // 4) % 2]
                            pv = pb[:].bitcast(BF16)
                            q = kc % 4
                            self.tr(pv[:, q * 128:q * 128 + tb], mtok[0:tb, kc * 128:(kc + 1) * 128],
                                    C.identb[0:tb, 0:tb], [mtok, C.identb_buf], [pb], signal=(q == 3))
                            if q == 3:
                                k0 = kc - 3
                                self.cp("act" if (kc // 4) % 2 else "dve",
                                        mixT[:, k0:k0 + 4, nb * tb:(nb + 1) * tb],
                                        pv[:, 0:512].rearrange("p (q c) -> p q c", c=128)[:, :, 0:tb], [pb], [mixT])
                else:
                    J.load_mix_T(self, t, halo, mixT, idx, Tt, hid)
                if halo:
                    J.load_xres_halo(self, xres, idx, xhb_s)
                else:
                    S.dma("sp", xres[0:tb, 0:nb_, :],
                          J.xres[t * T:(t + 1) * T, :].rearrange("(nb p) f -> p nb f", p=tb),
                          reads=[J.xres_buf], writes=[xres])
                for half in range(2):
                    for g in range(KC // 8):
                        wslot = wo[(half * (KC // 8) + g) % 2]
                        S.dma("sp", wslot[:], J.w_out[half, :, g * 8:(g + 1) * 8, :], reads=[J.w_out_buf], writes=[wslot])
                        for k8 in range(8):
                            kc = g * 8 + k8
                            for nb in range(nb_):
                                self.mm(ps[nb][0:tb, :], mixT[:, kc, nb * tb:(nb + 1) * tb], wslot[:, k8, :],
                                        kc == 0, kc == KC - 1, [mixT, wslot], [ps[nb]])
                    for nb in range(nb_):
                        self.stt("dve", xmid[0:tb, nb, half * 512:(half + 1) * 512],
                                 xres[0:tb, nb, half * 512:(half + 1) * 512], ALPHA, ps[nb][0:tb, :],
                                 ALU.mult, ALU.add, [xres, ps[nb]], [xmid])
                for nb in range(nb_):
                    st = stat[nb % 2]
                    rstd, nbias = self.ln_rows(lambda lo, hi, nb=nb: xmid[0:tb, nb, lo:hi], tb, 1024, [xmid], st)
                    self.act(xmid[0:tb, nb, :], xmid[0:tb, nb, :], AF.Identity, [xmid, st], [xmid], bias=nbias,
                             scale=rstd)
                    self.tt("pool", xmid[0:tb, nb, :], xmid[0:tb, nb, :], lnt[0:tb, 0, :], ALU.mult, [xmid, lnt], [xmid])
                    xb_ = xmb[nb % 2]
                    self.tt("pool", xmid[0:tb, nb, :], xmid[0:tb, nb, :], lnt[0:tb, 1, :], ALU.add, [xmid, lnt], [xmid])
                    self.cp("pool", xb_[0:tb, :], xmid[0:tb, nb, :], [xmid], [xb_])
                    for kc in range(8):
                        pb = ps[4 + (kc // 4) % 2]
                        pv = pb[:].bitcast(BF16)
                        q = kc % 4
                        self.tr(pv[:, q * 128:q * 128 + tb], xb_[0:tb, kc * 128:(kc + 1) * 128], C.identb[0:tb, 0:tb],
                                [xb_, C.identb_buf], [pb], signal=(q == 3))
                        if q == 3:
                            k0 = kc - 3
                            self.cp("act" if (kc // 4) % 2 else "dve", xmT[:, k0:k0 + 4, nb * tb:(nb + 1) * tb],
                                    pv[:, 0:512].rearrange("p (q c) -> p q c", c=128)[:, :, 0:tb], [pb], [xmT])
                for j in range(NFC):
                    ws = wu[j % 3]
                    S.dma("sp", ws[:], J.w_up[j], reads=[J.w_up_buf], writes=[ws])
                    pg = ps[4 + 2 * (j % 2)]
                    pvv = ps[5 + 2 * (j % 2)]
                    for kc in range(8):
                        self.mm(pg[:, 0:Tt], ws[:, kc, 128:256], xmT[:, kc, 0:Tt], kc == 0, kc == 7, [ws, xmT], [pg])
                    ae = aext[j % 2]
                    self.cp("pool", ae[:, 0:2], aprev[:, j, :], [aprev], [ae])
                    self.cp("act", ae[:, 2:2 + Tt], pg[:, 0:Tt], [pg], [ae])
                    if halo:
                        self.ts("pool", aprev[:, j, :], ae[:, Tt:Tt + 2], flag[:, 0:1], ALU.mult, [ae, flag], [aprev])
                        continue
                    self.cp("pool", aprev[:, j, :], ae[:, Tt:Tt + 2], [ae], [aprev])
                    for kc in range(8):
                        self.mm(pvv[:, 0:Tt], ws[:, kc, 0:128], xmT[:, kc, 0:Tt], kc == 0, kc == 7, [ws, xmT], [pvv])
                    uu = u[j % 2]
                    self.act(uu[:, 0:Tt], pg[:, 0:Tt], AF.Identity, [pg, cvp], [uu], bias=cvp[:, j, 3:4],
                             scale=cvp[:, j, 2:3])
                    self.stt("dve", uu[:, 0:Tt], ae[:, 1:1 + Tt], cvp[:, j, 1:2], uu[:, 0:Tt], ALU.mult, ALU.add,
                             [ae, cvp, uu], [uu])
                    self.stt("dve", uu[:, 0:Tt], ae[:, 0:Tt], cvp[:, j, 0:1], uu[:, 0:Tt], ALU.mult, ALU.add,
                             [ae, cvp, uu], [uu])
                    gg = ge[j % 2]
                    self.act(gg[:, 0:Tt], uu[:, 0:Tt], AF.Gelu, [uu], [gg])
                    self.tt("dve", hid[:, j, 0:Tt], gg[:, 0:Tt], pvv[:, 0:Tt], ALU.mult, [gg, pvv], [hid])
                if halo:
                    return
                for half in range(2):
                    for g in range(2):
                        wslot = wd[(half * 2 + g) % 2]
                        S.dma("sp", wslot[:], J.w_down[half, :, g * 11:(g + 1) * 11, :], reads=[J.w_down_buf],
                              writes=[wslot])
                        for k11 in range(11):
                            kc = g * 11 + k11
                            for nb in range(nb_):
                                self.mm(ps[nb][0:tb, :], hid[:, kc, nb * tb:(nb + 1) * tb], wslot[:, k11, :],
                                        kc == 0, kc == NFC - 1, [hid, wslot], [ps[nb]])
                    for nb in range(nb_):
                        sl = xmid[0:tb, nb, half * 512:(half + 1) * 512]
                        self.stt("dve", sl, sl, ALPHA, ps[nb][0:tb, :], ALU.mult, ALU.add, [xmid, ps[nb]], [xmid])
                for nb in range(nb_):
                    st = stat[nb % 2]
                    yo = yout[nb % 2]
                    rstd, nbias = self.ln_rows(lambda lo, hi, nb=nb: xmid[0:tb, nb, lo:hi], tb, 1024, [xmid], st)
                    self.act(yo[0:tb, :], xmid[0:tb, nb, :], AF.Identity, [xmid, st], [yo], bias=nbias, scale=rstd)
                    self.tt("pool", yo[0:tb, :], yo[0:tb, :], lnt[0:tb, 2, :], ALU.mult, [yo, lnt], [yo])
                    self.tt("pool", yo[0:tb, :], yo[0:tb, :], lnt[0:tb, 3, :], ALU.add, [yo, lnt], [yo])
                    r0 = t * T + nb * tb
                    S.dma("pool", J.y[r0:r0 + tb, :], yo[0:tb, :], reads=[yo], writes=[J.y_buf])
                    if J.ybf is not None:
                        S.dma("pool", J.ybf[r0:r0 + tb, :], yo[0:tb, :], reads=[yo], writes=[J.ybf_buf])
                if getattr(J, "after_tile", None) is not None:
                    J.after_tile(t)

            if J.halo:
                run_tile(64, 1, None, True)
            for t in range(ntile):
                run_tile(TB, NB, t, False)
            S.dma("pool", J.conv_out, aprev[:], reads=[aprev], writes=[J.conv_out_buf])
            S.barrier()
            S.release(bufs)

    def phaseC1(self, J, C):
        nc, S, ps = self.nc, self.S, self.ps
        bufs = []
        NBM = 4
        with ExitStack() as es:
            sb = lambda n, s, d: self.sbuf(es, n, s, d, bufs)
            w = sb("Cw", [128, 8, 776], BF16)
            xtok = [sb("Cxtok", [128, NBM, 1024], BF16) for _ in range(2)]
            xT = sb("CxT", [128, 8, NBM * 128], BF16)
            qk = [sb("Cqk", [128, NBM, 512], F32) for _ in range(2)]
            vf = [sb("Cvf", [128, NBM, 256], F32) for _ in range(2)]
            sq = sb("Csq", [128, NBM, 256], F32)
            sm = [sb("Csm", [128, NBM, 64], F32) for _ in range(2)]
            smb = [sb("Csmb", [128, NBM, 32], BF16) for _ in range(2)]
            lf = [sb("Clf", [128, NBM, 4], F32) for _ in range(2)]
            carry = sb("Ccarry", [1, 4], F32)
            Kp = [sb("CKp", [128, NBM, 4, 70], BF16) for _ in range(2)]
            Qp = [sb("CQp", [128, NBM, 4, 70], BF16) for _ in range(2)]
            Vp = [sb("CVp", [128, NBM, 4, 65], BF16) for _ in range(2)]
            KT = [sb("CKT", [128, 4, NBM * 128], BF16) for _ in range(2)]
            QT = [sb("CQT", [128, 4, NBM * 128], BF16) for _ in range(2)]
            bfb = sb("Cbfb", [128, 4], F32)

            S.dma("pool", w[:], J.w.t[:], reads=[J.w], writes=[w])
            S.dma("sp", bfb[:], J.bf.partition_broadcast(128), reads=[J.bf_buf], writes=[bfb])
            S.op("pool", lambda e: e.memset(carry[:], 0.0), writes=[carry])
            for i in range(2):
                S.op("pool", lambda e, i=i: e.memset(Kp[i][:, :, :, 67:70], 1.0), writes=[Kp[i]])
                S.op("pool", lambda e, i=i: e.memset(Qp[i][:, :, :, 64:67], 1.0), writes=[Qp[i]])

            def split3(src_ap, tb, NB, dst_fn, smt, smbt, neg):
                self.cp("dve", smbt[0:tb, 0:NB, 0:4], src_ap, [smt], [smbt])
                self.tt("dve", smt[0:tb, 0:NB, 36:40], src_ap, smbt[0:tb, 0:NB, 0:4], ALU.subtract, [smt, smbt], [smt])
                self.cp("dve", smbt[0:tb, 0:NB, 4:8], smt[0:tb, 0:NB, 36:40], [smt], [smbt])
                self.tt("dve", smt[0:tb, 0:NB, 44:48], smt[0:tb, 0:NB, 36:40], smbt[0:tb, 0:NB, 4:8], ALU.subtract,
                        [smt, smbt], [smt])
                self.cp("dve", smbt[0:tb, 0:NB, 8:12], smt[0:tb, 0:NB, 44:48], [smt], [smbt])
                for i in range(3):
                    dst, dbuf = dst_fn(i)
                    if neg:
                        self.ts("pool", dst, smbt[0:tb, 0:NB, 4 * i:4 * i + 4], -1.0, ALU.mult, [smbt], [dbuf])
                    else:
                        self.cp("pool", dst, smbt[0:tb, 0:NB, 4 * i:4 * i + 4], [smbt], [dbuf])

            tl = 0
            for (kind, nblk, tb, src) in J.segments:
                NB = min(NBM, nblk)
                assert nblk % NB == 0
                T = NB * tb
                for ti in range(nblk // NB):
                    i2 = tl % 2
                    tl += 1
                    qkt, vft, lft, smt, smbt = qk[i2], vf[i2], lf[i2], sm[i2], smb[i2]
                    Kpt, Qpt, Vpt, KTt, QTt = Kp[i2], Qp[i2], Vp[i2], KT[i2], QT[i2]
                    r0 = ti * T
                    if kind == "cache":
                        S.dma("sp", qkt[0:tb, 0:NB, 256:512], src.k[r0:r0 + T, :].rearrange("(n p) f -> p n f", p=tb),
                              reads=[src.k_buf], writes=[qkt])
                        S.dma("sp", vft[0:tb, 0:NB, :], src.v[r0:r0 + T, :].rearrange("(n p) f -> p n f", p=tb),
                              reads=[src.v_buf], writes=[vft])
                        S.dma("sp", lft[0:tb, 0:NB, :], src.lf[r0:r0 + T, :].rearrange("(n p) f -> p n f", p=tb),
                              reads=[src.lf_buf], writes=[lft])
                    else:
                        xk = xtok[i2]
                        xr0 = src.xrow(r0) if getattr(src, "xrow", None) is not None else r0
                        S.dma("sp", xk[0:tb, 0:NB, :], src.x[xr0:xr0 + T, :].rearrange("(n p) f -> p n f", p=tb),
                              reads=[src.x_buf], writes=[xk])
                        for kc in range(8):
                            pb = ps[kc % 2]
                            pv = pb[:].bitcast(BF16)
                            for nb in range(NB):
                                self.tr(pv[:, nb * tb:(nb + 1) * tb], xk[0:tb, nb, kc * 128:(kc + 1) * 128],
                                        C.identb[0:tb, 0:tb], [xk, C.identb_buf], [pb], signal=(nb == NB - 1))
                            self.cp("act" if kc % 2 else "dve", xT[:, kc, 0:T], pv[:, 0:T], [pb], [xT])
                        for nb in range(NB):
                            p1, p2 = ps[2 + nb % 2], ps[4 + nb % 2]
                            for kc in range(8):
                                self.mm(p1[0:tb, 0:512], xT[:, kc, nb * tb:(nb + 1) * tb], w[:, kc, 0:512], kc == 0,
                                        kc == 7, [xT, w], [p1])
                            for kc in range(8):
                                self.mm(p2[0:tb, 0:260], xT[:, kc, nb * tb:(nb + 1) * tb], w[:, kc, 512:772], kc == 0,
                                        kc == 7, [xT, w], [p2])
                            self.cp("act", qkt[0:tb, nb, :], p1[0:tb, 0:512], [p1], [qkt])
                            self.cp("act", vft[0:tb, nb, :], p2[0:tb, 0:256], [p2], [vft])
                            self.cp("act", smt[0:tb, nb, 28:32], p2[0:tb, 256:260], [p2], [smt])
                        self.tt("dve", smt[0:tb, 0:NB, 0:4], smt[0:tb, 0:NB, 28:32],
                                bfb[0:tb, :].unsqueeze(1).broadcast_to([tb, NB, 4]), ALU.add, [smt, bfb], [smt])
                        self.act(smt[0:tb, 0:NB, 0:4], smt[0:tb, 0:NB, 0:4], AF.Exp, [smt], [smt], scale=-1.0)
                        self.act(smt[0:tb, 0:NB, 0:4], smt[0:tb, 0:NB, 0:4], AF.Ln, [smt], [smt],
                                 bias=C.one_col[0:tb, 0:1])
                        self.ts("dve", lft[0:tb, 0:NB, :], smt[0:tb, 0:NB, 0:4], -1.0, ALU.mult, [smt], [lft])
                        g0 = src.out_row0 + r0
                        S.dma("pool", J.pk[g0:g0 + T, :].rearrange("(n p) f -> p n f", p=tb), qkt[0:tb, 0:NB, 256:512],
                              reads=[qkt], writes=[J.pk_buf])
                        S.dma("pool", J.pv[g0:g0 + T, :].rearrange("(n p) f -> p n f", p=tb), vft[0:tb, 0:NB, :],
                              reads=[vft], writes=[J.pv_buf])
                        S.dma("pool", J.plf[g0:g0 + T, :].rearrange("(n p) f -> p n f", p=tb), lft[0:tb, 0:NB, :],
                              reads=[lft], writes=[J.plf_buf])
                    pc, pcar = ps[6], ps[7]
                    for nb in range(NB):
                        self.mm(pc[0:tb, nb * 4:nb * 4 + 4], C.tri[0:tb, 0:tb], lft[0:tb, nb, :], True, False,
                                [C.tri_buf, lft], [pc])
                        self.mm(pc[0:tb, nb * 4:nb * 4 + 4], C.ones_f[0:1, 0:tb], carry[0:1, :], False, True,
                                [C.ones_buf, carry], [pc])
                        self.mm(pcar[0:1, 0:4], C.ones_f[0:tb, 0:1], lft[0:tb, nb, :], True, False, [C.ones_buf, lft],
                                [pcar])
                        self.mm(pcar[0:1, 0:4], C.ones_f[0:1, 0:1], carry[0:1, :], False, True, [C.ones_buf, carry],
                                [pcar])
                        self.cp("act", carry[0:1, :], pcar[0:1, 0:4], [pcar], [carry])
                    self.cp("act", smt[0:tb, 0:NB, 4:8], pc[0:tb, 0:NB * 4].rearrange("p (n h) -> p n h", h=4), [pc],
                            [smt])
                    self.tt("pool", sq[0:tb, 0:NB, :], qkt[0:tb, 0:NB, 256:512], qkt[0:tb, 0:NB, 256:512], ALU.mult,
                            [qkt], [sq])
                    S.op("dve", lambda e, tb=tb, smt=smt, NB=NB: e.reduce_sum(
                        out=smt[0:tb, 0:NB, 8:12], in_=sq[0:tb, 0:NB, :].rearrange("p n (h d) -> p n h d", d=64),
                        axis=AX.X), reads=[sq], writes=[smt])
                    self.ts("dve", smt[0:tb, 0:NB, 8:12], smt[0:tb, 0:NB, 8:12], 1.0 / 16.0, ALU.mult, [smt], [smt])
                    self.act(smt[0:tb, 0:NB, 12:16], smt[0:tb, 0:NB, 8:12], AF.Exp, [smt], [smt])
                    self.tt("dve", smt[0:tb, 0:NB, 16:20], smt[0:tb, 0:NB, 4:8], smt[0:tb, 0:NB, 8:12], ALU.add, [smt],
                            [smt])
                    split3(smt[0:tb, 0:NB, 16:20], tb, NB, lambda i: (Kpt[0:tb, 0:NB, :, 64 + i], Kpt), smt, smbt, True)
                    self.cp("pool", Kpt[0:tb, 0:NB, :, 0:64],
                            qkt[0:tb, 0:NB, 256:512].rearrange("p n (h d) -> p n h d", d=64), [qkt], [Kpt])
                    self.tt("dve", Vpt[0:tb, 0:NB, :, 0:64], vft[0:tb, 0:NB, :].rearrange("p n (h d) -> p n h d", d=64),
                            smt[0:tb, 0:NB, 12:16].unsqueeze(3).broadcast_to([tb, NB, 4, 64]), ALU.mult, [vft, smt],
                            [Vpt])
                    self.cp("dve", Vpt[0:tb, 0:NB, :, 64], smt[0:tb, 0:NB, 12:16], [smt], [Vpt])
                    k0 = J.kpos0[kind] + r0
                    for hh in range(4):
                        S.dma("pool", J.VV[hh, k0:k0 + T, :].rearrange("(n p) e -> p n e", p=tb), Vpt[0:tb, 0:NB, hh, :],
                              reads=[Vpt], writes=[J.VV_buf])

                    def transpose_out(Xpt, XTt, bank):
                        for n0 in range(0, NB, 2):
                            pk_ = ps[bank + (n0 // 2) % 2]
                            pkv = pk_[:].bitcast(BF16)
                            nn = min(2, NB - n0)
                            for n1 in range(nn):
                                for hh in range(4):
                                    self.tr(pkv[0:70, n1 * 512 + hh * 128:n1 * 512 + hh * 128 + tb],
                                            Xpt[0:tb, n0 + n1, hh, :], C.identb[0:tb, 0:tb], [Xpt, C.identb_buf], [pk_],
                                            signal=(hh == 3 and n1 == nn - 1))
                            src_v = pkv[0:70, 0:nn * 512].rearrange("p (n h c) -> p n h c", n=nn, h=4)[:, :, :, 0:tb]
                            dst_v = XTt[0:70, :, n0 * 128:(n0 + nn) * 128].rearrange("p h (n c) -> p n h c", c=128)[
                                :, :, :, 0:tb]
                            self.cp("act", dst_v, src_v, [pk_], [XTt])

                    transpose_out(Kpt, KTt, 0)
                    if tb == 128:
                        S.dma("pool", J.KT[:, :, k0:k0 + T].rearrange("h p t -> p h t"), KTt[0:70, :, 0:T], reads=[KTt],
                              writes=[J.KT_buf])
                    else:
                        S.dma("pool", J.KT[:, :, k0:k0 + tb].rearrange("h p t -> p h t"), KTt[0:70, :, 0:tb],
                              reads=[KTt], writes=[J.KT_buf])
                    if kind == "new":
                        self.tt("pool", sq[0:tb, 0:NB, :], qkt[0:tb, 0:NB, 0:256], qkt[0:tb, 0:NB, 0:256], ALU.mult,
                                [qkt], [sq])
                        S.op("dve", lambda e, tb=tb, smt=smt, NB=NB: e.reduce_sum(
                            out=smt[0:tb, 0:NB, 20:24], in_=sq[0:tb, 0:NB, :].rearrange("p n (h d) -> p n h d", d=64),
                            axis=AX.X), reads=[sq], writes=[smt])
                        self.stt("dve", smt[0:tb, 0:NB, 24:28], smt[0:tb, 0:NB, 20:24], -1.0 / 16.0, smt[0:tb, 0:NB, 4:8],
                                 ALU.mult, ALU.add, [smt], [smt])
                        split3(smt[0:tb, 0:NB, 24:28], tb, NB, lambda i: (Qpt[0:tb, 0:NB, :, 67 + i], Qpt), smt, smbt,
                               False)
                        self.ts("pool", Qpt[0:tb, 0:NB, :, 0:64],
                                qkt[0:tb, 0:NB, 0:256].rearrange("p n (h d) -> p n h d", d=64), 0.125, ALU.mult, [qkt],
                                [Qpt])
                        transpose_out(Qpt, QTt, 2)
                        q0 = src.out_row0 + r0
                        tq = T if tb == 128 else tb
                        S.dma("pool", J.QT[:, :, q0:q0 + tq].rearrange("h p t -> p h t"), QTt[0:70, :, 0:tq],
                              reads=[QTt], writes=[J.QT_buf])
            S.barrier()
            S.release(bufs)

    def phaseC2(self, J, C):
        nc, S, ps = self.nc, self.S, self.ps
        NK, NQ, TQ, TBq = J.NK, J.NQ, J.TQ, J.TBq
        ncache = J.ncache
        NKB = (NK + 127) // 128
        bufs = []
        with ExitStack() as es:
            sb = lambda n, s, d: self.sbuf(es, n, s, d, bufs)
            KTs = [sb("AtK", [128, NK], BF16) for _ in range(2)]
            Vs = [sb("AtV", [128, NKB, 65], BF16) for _ in range(2)]
            QTs = [sb("AtQ", [128, TQ], BF16) for _ in range(2)]
            PT = [sb("AtP", [128, TQ], BF16) for _ in range(3)]
            num = [sb("Atnum", [128, TQ], F32) for _ in range(2)]
            rd = [sb("Atrd", [128, TQ], F32) for _ in range(2)]
            oT = [sb("AtoT", [128, TQ], BF16) for _ in range(2)]
            nqt = NQ // TQ
            nsub = TQ // TBq
            LOOK = 2
            for hh in range(4):
                Kt, Vt = KTs[hh % 2], Vs[hh % 2]
                S.dma("sp", Kt[0:70, :], J.KT[hh], reads=[J.KT_buf], writes=[Kt])
                nfull = NK // 128
                if nfull:
                    S.dma("sp", Vt[:, 0:nfull, :], J.VV[hh, 0:nfull * 128, :].rearrange("(b p) e -> p b e", p=128),
                          reads=[J.VV_buf], writes=[Vt])
                if NK % 128:
                    S.dma("sp", Vt[0:NK % 128, nfull, :], J.VV[hh, nfull * 128:NK, :], reads=[J.VV_buf], writes=[Vt])
                units = []
                for qt in range(nqt):
                    kblocks = [(kb * 128, 128, None) for kb in range(ncache // 128)]
                    for u_all in range((qt + 1) * nsub):
                        k0 = ncache + u_all * TBq
                        u = u_all - qt * nsub
                        kblocks.append((k0, TBq, u if u >= 0 else None))
                    for bi, (k0, kb, u) in enumerate(kblocks):
                        units.append((qt, bi, len(kblocks), k0, kb, u))
                n = len(units)
                for i in range(n + LOOK):
                    if i < n:
                        qt, bi, nkb, k0, kb, u = units[i]
                        Qt = QTs[(hh * nqt + qt) % 2]
                        if bi == 0:
                            S.dma("sp", Qt[0:70, :], J.QT[hh, :, qt * TQ:(qt + 1) * TQ], reads=[J.QT_buf], writes=[Qt])
                        c0 = 0 if u is None else u * TBq
                        pss, pt = ps[i % 3], PT[i % 3]
                        self.mm(pss[0:kb, c0:TQ], Kt[0:70, k0:k0 + kb], Qt[0:70, c0:TQ], True, u is None, [Kt, Qt], [pss])
                        if u is not None:
                            self.mm(pss[0:kb, c0:c0 + TBq], C.identb[0:kb, 0:kb], C.negm[0:kb, 0:TBq], False, True,
                                    [C.identb_buf, C.negm_buf], [pss])
                        self.act(pt[0:kb, c0:TQ], pss[0:kb, c0:TQ], AF.Exp, [pss], [pt])
                    j = i - LOOK
                    if j < 0:
                        continue
                    qt, bi, nkb, k0, kb, u = units[j]
                    c0 = 0 if u is None else u * TBq
                    pt = PT[j % 3]
                    i2 = (hh * nqt + qt) % 2
                    po = ps[4 + i2]
                    self.mm(po[0:65, c0:TQ], Vt[0:kb, k0 // 128, :], pt[0:kb, c0:TQ], bi == 0, bi == nkb - 1,
                            [Vt, pt], [po])
                    if bi != nkb - 1:
                        continue
                    S.op("dve", lambda e, i2=i2, po=po: e.reciprocal(out=rd[i2][64:65, :], in_=po[64:65, 0:TQ]),
                         reads=[po], writes=[rd[i2]])
                    self.cp("dve", num[i2][0:64, :], po[0:64, 0:TQ], [po], [num[i2]])
                    pbc = ps[6 + i2]
                    self.mm(pbc[0:64, 0:TQ], C.ones_f[64:65, 0:64], rd[i2][64:65, :], True, True, [C.ones_buf, rd[i2]], [pbc])
                    self.tt("dve", oT[i2][0:64, :], num[i2][0:64, :], pbc[0:64, 0:TQ], ALU.mult, [num[i2], pbc], [oT[i2]])
                    S.dma("pool", J.oT_dst(hh, qt), oT[i2][0:64, :], reads=[oT[i2]], writes=[J.oT_buf])
                    if getattr(J, "after_unit", None) is not None:
                        J.after_unit(hh * nqt + qt)
            S.barrier()
            S.release(bufs)


def build_program(SEQ, stop=99):
    B = Builder(SEQ)
    nc, S = B.nc, B.S
    SEG = SEQ // 4
    NQT = SEQ // 512
    NTB_B = SEG // 128
    ntile_B = SEG // 512

    xA = B.din("xA", [SEQ, D])
    xB = B.din("xB", [SEG, D])
    xH = B.din("xH", [64, D])
    xS = B.din("xS", [NS, D])
    w_inP = B.din("w_inP", [128, 8, 1536])
    w_inS = [B.din("w_inS%d" % h, [128, 8, 1536]) for h in range(4)]
    ropeP = B.din("ropeP", [4, 128, SEQ])
    ropeS = B.din("ropeS", [4, 4, 128, NS])
    gamP = B.din("gamP", [128, 1])
    gamS = B.din("gamS", [4, 128, 1])
    lnretP = B.din("lnretP", [2, 512])
    lnretS = B.din("lnretS", [4, 2, 512])
    sret = B.din("sret", [4, 256, 512])
    w_outA = B.din("w_outA", [2, 128, 16, 512])
    w_up = B.din("w_up", [2, NFC, 128, 8, 256])
    w_down = B.din("w_down", [2, 2, 128, NFC, 512])
    convp = B.din("convp", [2, 128, NFC, 4])
    ln4 = B.din("ln4", [2, 4, D])
    sconv = B.din("sconv", [2, 128, NFC, 2])
    w_qkvP = B.din("w_qkvP", [128, 8, 776])
    w_qkvS = [B.din("w_qkvS%d" % g, [128, 8, 776]) for g in range(4)]
    bfP = B.din("bfP", [1, 4])
    bfS = B.din("bfS", [4, 4])
    w_outB = B.din("w_outB", [2, 128, 8, 512])
    ck = B.din("ck", [PAST, D])
    cv = B.din("cv", [PAST, D])
    clf = B.din("clf", [PAST, 16])
    flag_in = B.din("flag", [128, 1])
    idxB_in = B.din("idxB", [128, (NTB_B + 1) * 4], I32)
    idxD_in = B.din("idxD", [128, (ntile_B + 1) * 8 + 1], I32)
    cst = B.din("cst", [5, 128, 128])

    yP = B.dout("yP", [SEG, D])
    yS = B.dout("yS", [NS, D])
    retP = B.dout("retP", [256, 512])
    retS = B.dout("retS", [4, 256, 512])
    pk = B.dout("pk", [SEQ, 256])
    pv = B.dout("pv", [SEQ, 256])
    plf = B.dout("plf", [SEQ, 4])
    sk = B.dout("sk", [4, NS, 256])
    sv = B.dout("sv", [4, NS, 256])
    slf = B.dout("slf", [4, NS, 4])
    pconv = B.dout("pconv", [2, 128, NFC, 2])
    sconv_o = B.dout("sconv_o", [2, 128, NFC, 2])

    og_loc = B.dscr("og_loc", [SEQ, 512], BF16)
    og_all = B.dscr("og_all", [4 * SEQ, 512], BF16)
    og_s = B.dscr("og_s", [4 * NS, 512], BF16)
    x1_loc = B.dscr("x1_loc", [SEG, D], F32)
    x1_bf = B.dscr("x1_bf", [SEG, D], BF16)
    x1_all = B.dscr("x1_all", [SEQ, D], BF16)
    x1_s = B.dscr("x1_s", [NS, D], F32)
    x1_sb = B.dscr("x1_sb", [NS, D], BF16)
    KTp = B.dscr("KTp", [4, 70, SEQ], BF16)
    VVp = B.dscr("VVp", [4, SEQ, 65], BF16)
    QTp = B.dscr("QTp", [4, 70, SEQ], BF16)
    NKS = PAST + NS
    KTs_ = B.dscr("KTs", [4, 70, NKS], BF16)
    VVs_ = B.dscr("VVs", [4, NKS, 65], BF16)
    QTs_ = B.dscr("QTs", [4, 70, NS], BF16)
    oT_loc = B.dscr("oT_loc", [4 * NQT * 64, 512], BF16)
    oT_all = B.dscr("oT_all", [4 * 4 * NQT * 64, 512], BF16)
    oT_s = B.dscr("oT_s", [16 * 64, NS], BF16)
    wb_outA = B.dscr("wb_outA", [2, 128, 16, 512], BF16)
    wb_up = B.dscr("wb_up", [2, NFC, 128, 8, 256], BF16)
    wb_down = B.dscr("wb_down", [2, 2, 128, NFC, 512], BF16)
    wb_outB = B.dscr("wb_outB", [2, 128, 8, 512], BF16)

    C = Job()
    cf = nc.alloc_sbuf_tensor("c_f32", [128, 5, 128], F32)
    cb = nc.alloc_sbuf_tensor("c_bf", [128, 5, 128], BF16)
    cfb, cbb = Buf("c_f32", cf), Buf("c_bf", cb)
    S.dma("sp", cf[:], cst.t[:].rearrange("a p n -> p a n"), reads=[cst], writes=[cfb])
    S.dma("pool", cb[:], cst.t[:].rearrange("a p n -> p a n"), reads=[cst], writes=[cbb])
    C.identb, C.identb_buf = cb[:, 0, :], cbb
    C.maskT, C.maskT_buf = cf[:, 1, :], cfb
    C.tri, C.tri_buf = cf[:, 2, :], cfb
    C.negm, C.negm_buf = cb[:, 3, :], cbb
    C.ones_f, C.ones_buf = cf[:, 4, :], cfb
    C.one_col = cf[:, 4, 0:1]
    C.flag = flag_in

    for l in range(2):
        for h in range(2):
            for pg in range(0, 128, 32):
                if l == 0:
                    S.dma("pool", wb_outA.t[h, pg:pg + 32], w_outA.t[h, pg:pg + 32], reads=[w_outA], writes=[wb_outA])
                else:
                    S.dma("pool", wb_outB.t[h, pg:pg + 32], w_outB.t[h, pg:pg + 32], reads=[w_outB], writes=[wb_outB])
        for j in range(NFC):
            S.dma("pool", wb_up.t[l, j], w_up.t[l, j], reads=[w_up], writes=[wb_up])
        for h in range(2):
            for pg in range(0, 128, 32):
                S.dma("pool", wb_down.t[l, h, pg:pg + 32], w_down.t[l, h, pg:pg + 32], reads=[w_down],
                      writes=[wb_down])

    if stop <= 0:
        S.final_wait()
        return B
    def gmap(rho, h, CR):
        return (rho // CR) * 4 * CR + h * CR + (rho % CR)

    def exch_chunk(src, dst, i, CR):
        S.collective("AllGather", src, dst, src.t[i * CR:(i + 1) * CR, :], dst.t[i * 4 * CR:(i + 1) * 4 * CR, :])

    JA = Job(TB=128, NB=4, ntok=SEQ, x=xA.t, x_buf=xA, w=w_inP, rope=ropeP.t, rope_buf=ropeP,
             gam=gamP.t[:], gam_buf=gamP, lng=lnretP.t[0, :], lnb=lnretP.t[1, :], lnb_buf=lnretP,
             S0=None, S0_buf=None, Sout=retP.t[:], Sout_buf=retP,
             og_dst=lambda r0, n: og_loc.t[r0:r0 + n, :], og_buf=og_loc,
             after_tile=lambda t: exch_chunk(og_loc, og_all, t // 2, 1024) if t % 2 == 1 else None)
    B.phaseA(JA, C)
    if stop <= 1:
        S.final_wait()
        return B
    pass

    if stop <= 2:
        S.final_wait()
        return B
    for h in range(4):
        JS = Job(TB=NS, NB=1, ntok=NS, x=xS.t, x_buf=xS, w=w_inS[h], rope=ropeS.t[h], rope_buf=ropeS,
                 gam=gamS.t[h], gam_buf=gamS, lng=lnretS.t[h, 0, :], lnb=lnretS.t[h, 1, :], lnb_buf=lnretS,
                 S0=sret.t[h], S0_buf=sret, Sout=retS.t[h], Sout_buf=retS,
                 og_dst=lambda r0, n, h=h: og_s.t[h * NS + r0:h * NS + r0 + n, :], og_buf=og_s)
        B.phaseA(JS, C)

    if stop <= 3:
        S.final_wait()
        return B
    def load_mix_tok_sample(Bd, t, nb, tb, halo, mtok, idx):
        for h in range(4):
            Bd.S.dma("sp", mtok[0:tb, h * 512:(h + 1) * 512], og_s.t[h * NS:(h + 1) * NS, :], reads=[og_s],
                     writes=[mtok])

    JBs = Job(TB=NS, NB=1, ntok=NS, KC=16, tokmajor_in=True, load_mix_tok=load_mix_tok_sample, halo=False,
              xres=xS.t, xres_buf=xS, w_out=wb_outA.t, w_out_buf=wb_outA, w_up=wb_up.t[0], w_up_buf=wb_up,
              w_down=wb_down.t[0], w_down_buf=wb_down, convp=convp.t[0], convp_buf=convp, ln=ln4.t[0], ln_buf=ln4,
              aprev0=sconv.t[0], aprev0_buf=sconv, idx=None, y=x1_s.t, y_buf=x1_s, ybf=x1_sb.t, ybf_buf=x1_sb,
              conv_out=sconv_o.t[0], conv_out_buf=sconv_o)
    B.phaseB(JBs, C)

    if stop <= 4:
        S.final_wait()
        return B
    def load_mix_tok_prompt(Bd, t, nb, tb, halo, mtok, idx):
        col0 = NTB_B * 4 if halo else (t * 4 + nb) * 4
        for h in range(4):
            Bd.S.dma("pool", mtok[0:tb, h * 512:(h + 1) * 512], og_all.t[:, :], reads=[og_all, idx], writes=[mtok],
                     indirect=idx[0:tb, col0 + h:col0 + h + 1])

    def load_xres_halo_B(Bd, xres, idx, xhb_s):
        Bd.S.dma("sp", xres[0:64, 0, :], xH.t[:, :], reads=[xH], writes=[xres])

    JBp = Job(TB=128, NB=4, ntok=SEG, KC=16, tokmajor_in=True, load_mix_tok=load_mix_tok_prompt, halo=True,
              load_xres_halo=load_xres_halo_B,
              xres=xB.t, xres_buf=xB, w_out=wb_outA.t, w_out_buf=wb_outA, w_up=wb_up.t[0], w_up_buf=wb_up,
              w_down=wb_down.t[0], w_down_buf=wb_down, convp=convp.t[0], convp_buf=convp, ln=ln4.t[0], ln_buf=ln4,
              aprev0=None, aprev0_buf=None, idx=idxB_in, y=x1_loc.t, y_buf=x1_loc, ybf=x1_bf.t, ybf_buf=x1_bf,
              conv_out=pconv.t[0], conv_out_buf=pconv,
              after_tile=lambda t: exch_chunk(x1_bf, x1_all, t, 512))
    B.phaseB(JBp, C)
    if stop <= 5:
        S.final_wait()
        return B
    pass

    if stop <= 6:
        S.final_wait()
        return B
    for g in range(4):
        srcc = Job(k=ck.t[:, g * 256:(g + 1) * 256], k_buf=ck, v=cv.t[:, g * 256:(g + 1) * 256], v_buf=cv,
                   lf=clf.t[:, g * 4:(g + 1) * 4], lf_buf=clf)
        srcn = Job(x=x1_sb.t, x_buf=x1_sb, out_row0=0)
        JC = Job(w=w_qkvS[g], bf=bfS.t[g, :], bf_buf=bfS,
                 segments=[("cache", PAST // 128, 128, srcc), ("new", 1, NS, srcn)],
                 kpos0={"cache": 0, "new": PAST}, pk=sk.t[g], pk_buf=sk, pv=sv.t[g], pv_buf=sv, plf=slf.t[g],
                 plf_buf=slf, KT=KTs_.t, KT_buf=KTs_, VV=VVs_.t, VV_buf=VVs_, QT=QTs_.t, QT_buf=QTs_)
        B.phaseC1(JC, C)
        if stop == 61:
            S.final_wait()
            return B
        JC2 = Job(NK=NKS, NQ=NS, TQ=NS, TBq=NS, ncache=PAST, KT=KTs_.t, KT_buf=KTs_, VV=VVs_.t, VV_buf=VVs_,
                  QT=QTs_.t, QT_buf=QTs_,
                  oT_dst=lambda hh, qt, g=g: oT_s.t[(g * 4 + hh) * 64:(g * 4 + hh + 1) * 64, :], oT_buf=oT_s)
        B.phaseC2(JC2, C)

    if stop <= 7:
        S.final_wait()
        return B
    def load_mix_T_sample(Bd, t, halo, mixT, idx, Tt, hid):
        Bd.S.dma("sp", mixT[:, :, 0:NS], oT_s.t[:, :].rearrange("(kc p) t -> p kc t", p=128), reads=[oT_s],
                 writes=[mixT])

    JDs = Job(TB=NS, NB=1, ntok=NS, KC=8, tokmajor_in=False, load_mix_T=load_mix_T_sample, halo=False,
              xres=x1_s.t, xres_buf=x1_s, w_out=wb_outB.t, w_out_buf=wb_outB, w_up=wb_up.t[1], w_up_buf=wb_up,
              w_down=wb_down.t[1], w_down_buf=wb_down, convp=convp.t[1], convp_buf=convp, ln=ln4.t[1], ln_buf=ln4,
              aprev0=sconv.t[1], aprev0_buf=sconv, idx=None, y=yS.t, y_buf=yS, ybf=None, ybf_buf=None,
              conv_out=sconv_o.t[1], conv_out_buf=sconv_o)
    B.phaseB(JDs, C)

    if stop <= 8:
        S.final_wait()
        return B
    srcn = Job(x=x1_all.t, x_buf=x1_all, out_row0=0, xrow=lambda tok: gmap(tok % SEG, tok // SEG, 512))
    JCp = Job(w=w_qkvP, bf=bfP.t[0, :], bf_buf=bfP, segments=[("new", SEQ // 128, 128, srcn)],
              kpos0={"new": 0}, pk=pk.t, pk_buf=pk, pv=pv.t, pv_buf=pv, plf=plf.t, plf_buf=plf,
              KT=KTp.t, KT_buf=KTp, VV=VVp.t, VV_buf=VVp, QT=QTp.t, QT_buf=QTp)
    B.phaseC1(JCp, C)
    JC2p = Job(NK=SEQ, NQ=SEQ, TQ=512, TBq=128, ncache=0, KT=KTp.t, KT_buf=KTp, VV=VVp.t, VV_buf=VVp, QT=QTp.t,
               QT_buf=QTp, oT_dst=lambda hh, qt: oT_loc.t[(hh * NQT + qt) * 64:(hh * NQT + qt + 1) * 64, :],
               oT_buf=oT_loc,
               after_unit=lambda u: exch_chunk(oT_loc, oT_all, u // 16, 1024) if u % 16 == 15 else None)
    B.phaseC2(JC2p, C)
    if stop <= 9:
        S.final_wait()
        return B
    pass

    if stop <= 10:
        S.final_wait()
        return B
    def load_mix_T_prompt(Bd, t, halo, mixT, idx, Tt, hid):
        if halo:
            for kc in range(8):
                Bd.S.dma("pool", hid[:, kc, :], oT_all.t[:, :], reads=[oT_all, idx], writes=[hid],
                         indirect=idx[:, ntile_B * 8 + kc:ntile_B * 8 + kc + 1])
            Bd.cp("dve", mixT[:, :, 0:64], hid[:, 0:8, 448:512], [hid], [mixT])
        else:
            for kc in range(8):
                Bd.S.dma("pool", mixT[:, kc, :], oT_all.t[:, :], reads=[oT_all, idx], writes=[mixT],
                         indirect=idx[:, t * 8 + kc:t * 8 + kc + 1])

    def load_xres_halo_D(Bd, xres, idx, xhb_s):
        Bd.S.dma("pool", xhb_s[0:64, :], x1_all.t[:, :], reads=[x1_all, idx], writes=[xhb_s],
                 indirect=idx[0:64, (ntile_B + 1) * 8:(ntile_B + 1) * 8 + 1])
        Bd.cp("dve", xres[0:64, 0, :], xhb_s[0:64, :], [xhb_s], [xres])

    JDp = Job(TB=128, NB=4, ntok=SEG, KC=8, tokmajor_in=False, load_mix_T=load_mix_T_prompt, halo=True,
              load_xres_halo=load_xres_halo_D,
              xres=x1_loc.t, xres_buf=x1_loc, w_out=wb_outB.t, w_out_buf=wb_outB, w_up=wb_up.t[1], w_up_buf=wb_up,
              w_down=wb_down.t[1], w_down_buf=wb_down, convp=convp.t[1], convp_buf=convp, ln=ln4.t[1], ln_buf=ln4,
              aprev0=None, aprev0_buf=None, idx=idxD_in, y=yP.t, y_buf=yP, ybf=None, ybf_buf=None,
              conv_out=pconv.t[1], conv_out_buf=pconv)
    B.phaseB(JDp, C)
    S.final_wait()
    return B


def _kc_layout(wm):
    K, N = wm.shape
    return np.ascontiguousarray(wm.reshape(K // 128, 128, N).transpose(1, 0, 2))


def _rope_tables(pos, chunk, h):
    half = 128
    inv = (1.0 / (10000.0 ** (np.arange(half, dtype=np.float32) / np.float32(half)))).astype(np.float32)
    ang = pos.astype(np.float32)[:, None] * inv[None, :]
    cos = np.cos(ang).astype(np.float32).T
    sin = np.sin(ang).astype(np.float32).T
    lg = np.log1p(-np.exp2(-5.0 - h))
    i = (np.arange(len(pos)) % chunk).astype(np.float64)
    dq = np.exp((i + 1.0) * lg).astype(np.float32)[None, :]
    dk = (np.exp(-(i + 1.0) * lg) * (256.0 ** -0.5)).astype(np.float32)[None, :]
    tab = np.stack([cos * dq, sin * dq, cos * dk, sin * dk]).astype(np.float32)
    gam = np.float32(np.exp(chunk * lg))
    return tab, gam


def _gmap(rho, h, CR):
    return (rho // CR) * 4 * CR + h * CR + (rho % CR)


def _consts():
    j = np.arange(128)[:, None]
    i = np.arange(128)[None, :]
    ident = (i == j).astype(np.float32)
    maskT = (i >= j).astype(np.float32)
    tri = (j <= i).astype(np.float32)
    negm = np.where(i >= j, 0.0, NEGM).astype(np.float32)
    ones = np.ones((128, 128), np.float32)
    return np.stack([ident, maskT, tri, negm, ones])


def make_in_maps(inp, SEQ):
    f = lambda a: np.ascontiguousarray(np.asarray(a, dtype=np.float32))
    SEG = SEQ // 4
    NQT = SEQ // 512
    NTB_B = SEG // 128
    ntile_B = SEG // 512
    xp, xs = f(inp["x_prompt"]), f(inp["x_sample"])
    w_in = f(inp["w_in_a"])[0]
    HK, HV = 1024, 2048
    lnrg, lnrb = f(inp["ln_ret_g"])[0], f(inp["ln_ret_b"])[0]
    w_kvf, w_q = f(inp["w_kvf"]), f(inp["w_q_b"])[0]
    b_f = f(inp["b_f"])

    def w_in_head(h):
        cols = np.concatenate([np.arange(h * 256, (h + 1) * 256), HK + np.arange(h * 256, (h + 1) * 256),
                               2 * HK + np.arange(h * 512, (h + 1) * 512),
                               2 * HK + HV + np.arange(h * 512, (h + 1) * 512)])
        return _kc_layout(w_in[:, cols])

    def w_qkv_group(g):
        c = np.arange(g * 256, (g + 1) * 256)
        return _kc_layout(np.concatenate([w_q[:, c], w_kvf[:, c], w_kvf[:, 1024 + c],
                                          w_kvf[:, 2048 + g * 4:2048 + (g + 1) * 4],
                                          np.zeros((1024, 4), np.float32)], axis=1))

    w_in_heads = [w_in_head(h) for h in range(4)]
    w_qkv_groups = [w_qkv_group(g) for g in range(4)]
    ropeS = np.stack([_rope_tables(PAST + np.arange(NS), NS, h)[0] for h in range(4)])
    gamS = np.stack([np.full((128, 1), _rope_tables(np.arange(1), NS, h)[1], np.float32) for h in range(4)])
    lnretS = np.stack([np.stack([lnrg[h * 512:(h + 1) * 512], lnrb[h * 512:(h + 1) * 512]]) for h in range(4)])
    w_outA = f(inp["w_out_a"])[0]
    w_outA_l = np.stack([_kc_layout(w_outA[:, hf * 512:(hf + 1) * 512]) for hf in range(2)])
    w_outB = f(inp["w_out_b"])[0]
    w_outB_l = np.stack([_kc_layout(w_outB[:, hf * 512:(hf + 1) * 512]) for hf in range(2)])
    wu = f(inp["w_up"])
    w_up_l = np.empty((2, NFC, 128, 8, 256), np.float32)
    for l in range(2):
        for j in range(NFC):
            w_up_l[l, j, :, :, 0:128] = _kc_layout(wu[l][:, j * 128:(j + 1) * 128])
            w_up_l[l, j, :, :, 128:256] = _kc_layout(wu[l][:, DFF + j * 128:DFF + (j + 1) * 128])
    wdn = f(inp["w_down"])
    w_down_l = np.stack([np.stack([_kc_layout(wdn[l][:, hf * 512:(hf + 1) * 512]) for hf in range(2)])
                         for l in range(2)])
    cw, cbias = f(inp["conv_w"]), f(inp["conv_b"])
    convp = np.empty((2, 128, NFC, 4), np.float32)
    for l in range(2):
        for k in range(3):
            convp[l, :, :, k] = cw[l, k].reshape(NFC, 128).T
        convp[l, :, :, 3] = cbias[l].reshape(NFC, 128).T
    ln4 = np.stack([np.stack([f(inp["ln_mix_g"])[l], f(inp["ln_mix_b"])[l], f(inp["ln_ffn_g"])[l],
                              f(inp["ln_ffn_b"])[l]]) for l in range(2)])
    sfc = f(inp["state_ffn_conv"])
    cst = _consts()
    ckk, cvv, clf = f(inp["cache_k"]), f(inp["cache_v"]), f(inp["cache_logf"])
    sret = f(inp["state_ret"])[0]
    maps = []
    for c in range(8):
        b, r = c // 4, c % 4
        m = {}
        m["xA"] = xp[b]
        m["xB"] = np.ascontiguousarray(xp[b, r * SEG:(r + 1) * SEG])
        m["xH"] = np.ascontiguousarray(xp[b, r * SEG - 64:r * SEG]) if r > 0 else np.zeros((64, D), np.float32)
        m["xS"] = xs[c]
        m["w_inP"] = w_in_heads[r]
        for h in range(4):
            m["w_inS%d" % h] = w_in_heads[h]
        tab, gam = _rope_tables(np.arange(SEQ), 128, r)
        m["ropeP"] = tab
        m["ropeS"] = ropeS
        m["gamP"] = np.full((128, 1), gam, np.float32)
        m["gamS"] = gamS
        m["lnretP"] = lnretS[r]
        m["lnretS"] = lnretS
        m["sret"] = sret[c]
        m["w_outA"] = w_outA_l
        m["w_up"] = w_up_l
        m["w_down"] = w_down_l
        m["convp"] = convp
        m["ln4"] = ln4
        m["sconv"] = np.ascontiguousarray(
            np.stack([sfc[l, c].reshape(2, NFC, 128).transpose(2, 1, 0) for l in range(2)]))
        m["w_qkvP"] = w_qkv_groups[r]
        for g in range(4):
            m["w_qkvS%d" % g] = w_qkv_groups[g]
        m["bfP"] = b_f[r * 4:(r + 1) * 4].reshape(1, 4)
        m["bfS"] = b_f.reshape(4, 4)
        m["w_outB"] = w_outB_l
        m["ck"] = ckk[c].reshape(PAST, D)
        m["cv"] = cvv[c].reshape(PAST, D)
        m["clf"] = clf[c]
        m["flag"] = np.full((128, 1), 1.0 if r > 0 else 0.0, np.float32)
        p = np.arange(128)
        idxB = np.zeros((128, (NTB_B + 1) * 4), np.int32)
        for tbi in range(NTB_B):
            for h in range(4):
                idxB[:, tbi * 4 + h] = _gmap(r * SEG + tbi * 128 + p, h, 1024)
        h0 = max(r * SEG - 64, 0)
        for h in range(4):
            idxB[:, NTB_B * 4 + h] = _gmap(h0 + (p % 64), h, 1024)
        m["idxB"] = idxB
        idxD = np.zeros((128, (ntile_B + 1) * 8 + 1), np.int32)
        R = 4 * NQT * 64
        for t in range(ntile_B + 1):
            qt = r * ntile_B + t if t < ntile_B else max(r * ntile_B - 1, 0)
            for kc in range(8):
                g = kc // 2
                hh = 2 * (kc % 2) + p // 64
                idxD[:, t * 8 + kc] = _gmap((hh * NQT + qt) * 64 + (p % 64), g, 1024)
        tokh = h0 + (p % 64)
        idxD[:, (ntile_B + 1) * 8] = _gmap(tokh % SEG, tokh // SEG, 512)
        m["idxD"] = idxD
        m["cst"] = cst
        maps.append({k: np.ascontiguousarray(v) for k, v in m.items()})
    return maps


def assemble(res, SEQ, Bp=2):
    SEG = SEQ // 4
    R = [r for r in res]
    g = lambda c, k: np.asarray(R[c][k], dtype=np.float32)
    y_p = np.stack([np.concatenate([g(b * 4 + r, "yP") for r in range(4)], 0) for b in range(Bp)])
    y_s = np.stack([g(c, "yS") for c in range(8)])
    ret_p = np.stack([np.stack([g(b * 4 + r, "retP") for r in range(4)]) for b in range(Bp)])[None]
    k_p = np.stack([np.concatenate([g(b * 4 + r, "pk").reshape(SEQ, 4, 64) for r in range(4)], 1) for b in range(Bp)])
    v_p = np.stack([np.concatenate([g(b * 4 + r, "pv").reshape(SEQ, 4, 64) for r in range(4)], 1) for b in range(Bp)])
    lf_p = np.stack([np.concatenate([g(b * 4 + r, "plf") for r in range(4)], 1) for b in range(Bp)])

    def conv_fix(a):
        return a.transpose(2, 1, 0).reshape(2, DFF)

    conv_p = np.stack([np.stack([conv_fix(g(b * 4 + 3, "pconv")[l]) for b in range(Bp)]) for l in range(2)])
    ret_s = np.stack([g(c, "retS") for c in range(8)])[None]
    k_s = np.stack([g(c, "sk").transpose(1, 0, 2).reshape(NS, 16, 64) for c in range(8)])
    v_s = np.stack([g(c, "sv").transpose(1, 0, 2).reshape(NS, 16, 64) for c in range(8)])
    lf_s = np.stack([g(c, "slf").transpose(1, 0, 2).reshape(NS, 16) for c in range(8)])
    conv_s = np.stack([np.stack([conv_fix(g(c, "sconv_o")[l]) for c in range(8)]) for l in range(2)])
    return (y_p, y_s, ret_p, k_p, v_p, lf_p, conv_p, ret_s, k_s, v_s, lf_s, conv_s)


_CACHE = {}


def kernel(**inputs):
    SEQ = int(np.asarray(inputs["x_prompt"]).shape[1])
    if SEQ not in _CACHE:
        import os
        _CACHE[SEQ] = build_program(SEQ, int(os.environ.get("KSTOP", "99")))
    B = _CACHE[SEQ]
    in_maps = make_in_maps(inputs, SEQ)
    import os
    if os.environ.get("KTRACE"):
        res = run_bass_kernel_spmd(B.nc, in_maps, core_ids=list(range(8)), trace=True)
        print("EXEC_TIME_NS", res.exec_time_ns, flush=True)
    else:
        res = run_bass_kernel_spmd(B.nc, in_maps, core_ids=list(range(8)))
    return assemble(res.results, SEQ)
```

```python
import math
from contextlib import ExitStack
import numpy as np
import ml_dtypes
import concourse.bass as bass
import concourse.mybir as mybir
from concourse.bass_utils import run_bass_kernel_spmd

F32 = mybir.dt.float32
BF16 = mybir.dt.bfloat16
I32 = mybir.dt.int32
AF = mybir.ActivationFunctionType
ALU = mybir.AluOpType
AX = mybir.AxisListType

D = 1024
DFF = 2816
NFC = 22
PAST = 2048
NS = 64
LN_EPS = 1e-5
ALPHA = 4.0 ** 0.25
GROUPS = [[0, 1, 2, 3], [4, 5, 6, 7]]
NEGM = -30000.0


class Buf:
    __slots__ = ("name", "t", "wtok", "rtok", "dsem", "dcnt")

    def __init__(self, name, t):
        self.name = name
        self.t = t
        self.wtok = {}
        self.rtok = {}
        self.dsem = None
        self.dcnt = 0

    def __getitem__(self, k):
        return self.t[k]


class Sched:
    ENG = ("pe", "act", "dve", "pool", "sp")

    def __init__(self, nc):
        self.nc = nc
        self.e = {"pe": nc.tensor, "act": nc.scalar, "dve": nc.vector, "pool": nc.gpsimd, "sp": nc.sync}
        self.sem = {k: nc.alloc_semaphore("prog_" + k) for k in self.ENG}
        self.cnt = {k: 0 for k in self.ENG}
        self.seen = {k: {} for k in self.ENG}
        self.cc_sem = nc.alloc_semaphore("cc_sem")
        self.cc_cnt = 0
        self.dma_bufs = []
        self.sem_pool = []
        self.nsem = 0
        self.ninst = 0
        self.nwait = 0
        self.semeng = {id(self.sem[k]): k for k in self.ENG}
        self.vc = {k: {} for k in self.ENG}

    def _dsem(self, b):
        if b.dsem is None:
            if self.sem_pool:
                b.dsem, b.dcnt = self.sem_pool.pop()
            else:
                b.dsem = self.nc.alloc_semaphore("dq%d" % self.nsem)
                self.nsem += 1
            self.dma_bufs.append(b)
        return b.dsem

    def release(self, bufs):
        for b in bufs:
            if b.dsem is not None:
                self.sem_pool.append((b.dsem, b.dcnt))
                self.dma_bufs.remove(b)
                b.dsem = None

    def _wait(self, eng, deps, skip_self=False, attach=False):
        E = self.e[eng]
        seen = self.seen[eng]
        own = self.sem[eng]
        cand = []
        for sem, val in deps.items():
            if skip_self and sem is own:
                continue
            if seen.get(sem, 0) < val:
                pe = self.semeng.get(id(sem))
                snap = self.vc[pe].get(val) if pe is not None else None
                cand.append((sem, val, snap))
        need = []
        for k, (sem, val, snap) in enumerate(cand):
            implied = False
            for k2, (s2, v2, snap2) in enumerate(cand):
                if k2 != k and snap2 is not None and snap2.get(sem, 0) >= val:
                    implied = True
                    break
            if not implied:
                need.append((sem, val))
        for sem, val, snap in cand:
            if seen.get(sem, 0) < val:
                seen[sem] = val
            if snap is not None:
                for s2, v2 in snap.items():
                    if seen.get(s2, 0) < v2:
                        seen[s2] = v2
        last = need.pop() if (attach and need) else None
        for sem, val in need:
            E.wait_ge(sem, val)
            self.nwait += 1
        return last

    @staticmethod
    def _collect(reads, writes):
        deps = {}
        for r in reads:
            for s, v in r.wtok.items():
                if deps.get(s, 0) < v:
                    deps[s] = v
        for w in writes:
            for s, v in w.wtok.items():
                if deps.get(s, 0) < v:
                    deps[s] = v
            for s, v in w.rtok.items():
                if deps.get(s, 0) < v:
                    deps[s] = v
        return deps

    @staticmethod
    def _record(tok, reads, writes):
        s, v = tok
        for r in reads:
            if r.rtok.get(s, 0) < v:
                r.rtok[s] = v
        for w in writes:
            w.wtok = {s: v}
            w.rtok = {}

    def op(self, eng, fn, reads=(), writes=(), signal=True):
        deps = self._collect(reads, writes)
        last = self._wait(eng, deps, skip_self=(eng == "pe"), attach=True)
        ins = fn(self.e[eng])
        if last is not None:
            ins._wait_ge(last[0], last[1])
        self.ninst += 1
        if signal:
            self.cnt[eng] += 1
            ins.then_inc(self.sem[eng], 1)
            tok = (self.sem[eng], self.cnt[eng])
            snap = {self.sem[k]: self.seen[eng].get(self.sem[k], 0) for k in self.ENG}
            self.vc[eng][self.cnt[eng]] = snap
        else:
            tok = (self.sem[eng], self.cnt[eng] + 1)
        self._record(tok, reads, writes)
        return ins

    def dma(self, q, out_ap, in_ap, reads=(), writes=(), indirect=None):
        dst = writes[0]
        deps = self._collect(reads, writes)
        last = self._wait(q, deps, attach=(indirect is None))
        sem = self._dsem(dst)
        dst.dcnt += 1
        if indirect is not None:
            self.e[q].indirect_dma_start(out=out_ap, out_offset=None, in_=in_ap,
                                         in_offset=bass.IndirectOffsetOnAxis(ap=indirect, axis=0)).then_inc(sem, 16)
        else:
            ins = self.e[q].dma_start(out=out_ap, in_=in_ap)
            if last is not None:
                ins._wait_ge(last[0], last[1])
            ins.then_inc(sem, 16)
        self.ninst += 1
        s, v = sem, 16 * dst.dcnt
        for r in reads:
            if r.rtok.get(s, 0) < v:
                r.rtok[s] = v
        dst.wtok = {s: v}
        dst.rtok = {}

    def collective(self, kind, in_buf, out_buf, in_ap, out_ap):
        deps = self._collect([in_buf], [out_buf])
        self._wait("pool", deps)
        self.cc_cnt += 1
        self.nc.gpsimd.collective_compute(kind, ALU.bypass, replica_groups=GROUPS, ins=[in_ap],
                                          outs=[out_ap]).then_inc(self.cc_sem, 1)
        self._record((self.cc_sem, self.cc_cnt), [in_buf], [out_buf])

    def _all_tokens(self, with_cc):
        deps = {self.sem[k]: self.cnt[k] for k in self.ENG if self.cnt[k] > 0}
        for b in self.dma_bufs:
            if b.dcnt:
                deps[b.dsem] = 16 * b.dcnt
        for s, c in self.sem_pool:
            if c:
                deps[s] = 16 * c
        if with_cc and self.cc_cnt:
            deps[self.cc_sem] = self.cc_cnt
        return deps

    def barrier(self):
        deps = self._all_tokens(False)
        for k in self.ENG:
            self._wait(k, deps)

    def final_wait(self):
        self._wait("sp", self._all_tokens(True))


class Job:
    def __init__(self, **kw):
        self.__dict__.update(kw)


class Builder:
    def __init__(self, SEQ):
        self.SEQ = SEQ
        self.SEG = SEQ // 4
        self.nc = nc = bass.Bass("TRN2", target_bir_lowering=False)
        self.S = Sched(nc)
        self.uid = 0
        self.inputs = {}
        self.outputs = {}
        self.ps = [Buf("ps%d" % i, nc.alloc_psum_tensor("ps%d" % i, [128, 512], F32)) for i in range(8)]
        self.epsc = nc.alloc_sbuf_tensor("epsc", [128, 1], F32)
        self.epsb = Buf("epsc", self.epsc)
        self.S.op("pool", lambda e: e.memset(self.epsc[:], LN_EPS), writes=[self.epsb])

    def din(self, name, shape, dt=F32):
        h = self.nc.dram_tensor(name, list(shape), dt, kind="ExternalInput")
        self.inputs[name] = (tuple(shape), dt)
        return Buf(name, h)

    def dout(self, name, shape, dt=F32):
        h = self.nc.dram_tensor(name, list(shape), dt, kind="ExternalOutput")
        self.outputs[name] = (tuple(shape), dt)
        return Buf(name, h)

    def dscr(self, name, shape, dt):
        h = self.nc.dram_tensor(name, list(shape), dt, kind="Internal")
        return Buf(name, h)

    def sbuf(self, es, name, shape, dt, lst):
        self.uid += 1
        h = es.enter_context(self.nc.sbuf_tensor("%s_%d" % (name, self.uid), list(shape), dt))
        b = Buf("%s_%d" % (name, self.uid), h)
        lst.append(b)
        return b

    def mm(self, out, lhsT, rhs, start, stop, reads, writes):
        self.S.op("pe", lambda e: e.matmul(out, lhsT, rhs, start=start, stop=stop), reads=reads, writes=writes,
                  signal=bool(stop))

    def tr(self, out, in_, ident, reads, writes, signal):
        self.S.op("pe", lambda e: e.transpose(out, in_, ident), reads=reads, writes=writes, signal=signal)

    def act(self, out, in_, func, reads, writes, bias=None, scale=None, accum_out=None):
        kw = {}
        if bias is not None:
            kw["bias"] = bias
        if scale is not None:
            kw["scale"] = scale
        if accum_out is not None:
            kw["accum_out"] = accum_out
        self.S.op("act", lambda e: e.activation(out=out, in_=in_, func=func, **kw), reads=reads, writes=writes)

    def tt(self, eng, out, in0, in1, op, reads, writes):
        self.S.op(eng, lambda e: e.tensor_tensor(out=out, in0=in0, in1=in1, op=op), reads=reads, writes=writes)

    def ts(self, eng, out, in0, s1, op0, reads, writes, s2=None, op1=None):
        if op1 is None:
            self.S.op(eng, lambda e: e.tensor_scalar(out=out, in0=in0, scalar1=s1, scalar2=None, op0=op0),
                      reads=reads, writes=writes)
        else:
            self.S.op(eng, lambda e: e.tensor_scalar(out=out, in0=in0, scalar1=s1, scalar2=s2, op0=op0, op1=op1),
                      reads=reads, writes=writes)

    def stt(self, eng, out, in0, scalar, in1, op0, op1, reads, writes):
        self.S.op(eng, lambda e: e.scalar_tensor_tensor(out=out, in0=in0, scalar=scalar, in1=in1, op0=op0, op1=op1),
                  reads=reads, writes=writes)

    def cp(self, eng, out, in_, reads, writes):
        if eng == "act":
            self.S.op("act", lambda e: e.copy(out=out, in_=in_), reads=reads, writes=writes)
        else:
            self.S.op(eng, lambda e: e.tensor_copy(out=out, in_=in_), reads=reads, writes=writes)

    def ln_rows(self, src_ap_fn, nrows, width, src_bufs, stat, es_name=""):
        nchunk = (width + 511) // 512
        st = stat
        for c in range(nchunk):
            lo, hi = c * 512, min(width, (c + 1) * 512)
            self.S.op("dve", lambda e, c=c, lo=lo, hi=hi: e.bn_stats(out=st[0:nrows, 8 + 6 * c: 14 + 6 * c],
                                                                     in_=src_ap_fn(lo, hi)),
                      reads=src_bufs, writes=[st])
        self.S.op("dve", lambda e: e.bn_aggr(out=st[0:nrows, 4:6],
                                             in_=st[0:nrows, 8:8 + 6 * nchunk].rearrange("p (c s) -> p c s", s=6)),
                  reads=[st], writes=[st])
        self.act(st[0:nrows, 2:3], st[0:nrows, 5:6], AF.Sqrt, [st, self.epsb], [st], bias=self.epsc[0:nrows, 0:1],
                 scale=1.0)
        self.S.op("dve", lambda e: e.reciprocal(out=st[0:nrows, 0:1], in_=st[0:nrows, 2:3]), reads=[st], writes=[st])
        self.stt("dve", st[0:nrows, 1:2], st[0:nrows, 4:5], -1.0, st[0:nrows, 0:1], ALU.mult, ALU.mult, [st], [st])
        return st[0:nrows, 0:1], st[0:nrows, 1:2]

    def phaseA(self, J, C):
        nc, S, ps = self.nc, self.S, self.ps
        TB, NB = J.TB, J.NB
        T = TB * NB
        ntile = J.ntok // T
        bufs = []
        with ExitStack() as es:
            sb = lambda n, s, d: self.sbuf(es, n, s, d, bufs)
            w = sb("Aw", [128, 8, 1536], BF16)
            xtok = [sb("Axtok", [128, NB, 1024], BF16) for _ in range(2)]
            rope = [sb("Arope", [128, 4, T], F32) for _ in range(2)]
            xT = sb("AxT", [128, 8, T], BF16)
            t1 = sb("At1", [128, T], F32)
            t2 = sb("At2", [128, T], F32)
            qdTs = [sb("AqdT", [128, 2, T], BF16) for _ in range(2)]
            kiTs = [sb("AkiT", [128, 2, T], BF16) for _ in range(2)]
            kitoks = [sb("Akitok", [128, NB, 256], BF16) for _ in range(2)]
            vtoks = [sb("Avtok", [128, NB, 512], BF16) for _ in range(2)]
            gss = [sb("Ags", [128, NB, 512], F32) for _ in range(2)]
            pTs = [sb("ApT", [128, NB, TB], BF16) for _ in range(2)]
            Sf = sb("ASf", [128, 2, 512], F32)
            Sb = sb("ASb", [128, 2, 512], BF16)
            tmp = sb("Atmp", [128, 2, 512], F32)
            on = [sb("Aon", [128, 512], F32) for _ in range(2)]
            og = [sb("Aog", [128, NB, 512], BF16) for _ in range(2)]
            lnp = sb("Alnp", [128, 2, 512], F32)
            gam = sb("Agam", [128, 1], F32)
            stat = [sb("Astat", [128, 32], F32) for _ in range(2)]

            S.dma("pool", w[:], J.w.t[:], reads=[J.w], writes=[w])
            S.dma("sp", lnp[:, 0, :], J.lng.partition_broadcast(128), reads=[J.lnb_buf], writes=[lnp])
            S.dma("sp", lnp[:, 1, :], J.lnb.partition_broadcast(128), reads=[J.lnb_buf], writes=[lnp])
            S.dma("sp", gam[:], J.gam, reads=[J.gam_buf], writes=[gam])
            if J.S0 is not None:
                S.dma("sp", Sf[:], J.S0.rearrange("(h p) e -> p h e", p=128), reads=[J.S0_buf], writes=[Sf])
            else:
                S.op("pool", lambda e: e.memset(Sf[:], 0.0), writes=[Sf])
            self.cp("pool", Sb[:], Sf[:], [Sf], [Sb])

            def load(t):
                xs = J.x[t * T:(t + 1) * T, :].rearrange("(nb p) f -> p nb f", p=TB)
                S.dma("pool", xtok[t % 2][0:TB, :, :], xs, reads=[J.x_buf], writes=[xtok[t % 2]])
                S.dma("sp", rope[t % 2][:], J.rope[:, :, t * T:(t + 1) * T].rearrange("a p t -> p a t"),
                      reads=[J.rope_buf], writes=[rope[t % 2]])

            def stage1(t):
                qdT, kiT, kitok, vtok, gs, pT = (qdTs[t % 2], kiTs[t % 2], kitoks[t % 2], vtoks[t % 2], gss[t % 2],
                                                 pTs[t % 2])
                xk, rp = xtok[t % 2], rope[t % 2]
                for kc in range(8):
                    pb = ps[kc % 2]
                    pv = pb[:].bitcast(BF16)
                    for nb in range(NB):
                        self.tr(pv[:, nb * TB:(nb + 1) * TB], xk[0:TB, nb, kc * 128:(kc + 1) * 128],
                                C.identb[0:TB, 0:TB], [xk, C.identb_buf], [pb], signal=(nb == NB - 1))
                    self.cp("act" if kc % 2 else "dve", xT[:, kc, :], pv[:, 0:T], [pb], [xT])
                for qk in range(2):
                    pa, pbk = ps[2], ps[3]
                    for half, pp in enumerate((pa, pbk)):
                        col = qk * 256 + half * 128
                        for kc in range(8):
                            self.mm(pp[:, 0:T], w[:, kc, col:col + 128], xT[:, kc, :], kc == 0, kc == 7,
                                    [w, xT], [pp])
                    cs, sn = rp[:, 2 * qk, :], rp[:, 2 * qk + 1, :]
                    dst = qdT if qk == 0 else kiT
                    self.tt("dve", t1[:], pa[:, 0:T], cs, ALU.mult, [pa, rp], [t1])
                    self.tt("dve", t2[:], pbk[:, 0:T], sn, ALU.mult, [pbk, rp], [t2])
                    self.tt("pool", dst[:, 0, :], t1[:], t2[:], ALU.subtract, [t1, t2], [dst])
                    self.tt("dve", t1[:], pbk[:, 0:T], cs, ALU.mult, [pbk, rp], [t1])
                    self.tt("dve", t2[:], pa[:, 0:T], sn, ALU.mult, [pa, rp], [t2])
                    self.tt("pool", dst[:, 1, :], t1[:], t2[:], ALU.add, [t1, t2], [dst])
                pb = ps[0]
                pv = pb[:].bitcast(BF16)
                for c in range(NB):
                    for half in range(2):
                        self.tr(pv[0:TB, (c * 2 + half) * 128:(c * 2 + half + 1) * 128],
                                kiT[:, half, c * TB:(c + 1) * TB], C.identb[:, :], [kiT, C.identb_buf], [pb],
                                signal=(c == NB - 1 and half == 1))
                self.cp("act", kitok[0:TB, :, :], pv[0:TB, 0:NB * 256].rearrange("p (c d) -> p c d", d=256),
                        [pb], [kitok])
                pb = ps[1]
                for c in range(NB):
                    for half in range(2):
                        self.mm(pb[0:TB, c * TB:(c + 1) * TB], kiT[:, half, c * TB:(c + 1) * TB],
                                qdT[:, half, c * TB:(c + 1) * TB], half == 0, half == 1, [kiT, qdT], [pb])
                for c in range(NB):
                    self.tt("dve", pT[0:TB, c, :], pb[0:TB, c * TB:(c + 1) * TB], C.maskT[0:TB, 0:TB], ALU.mult,
                            [pb, C.maskT_buf], [pT])
                for nb in range(NB):
                    pp = ps[2 + nb % 2]
                    for kc in range(8):
                        self.mm(pp[0:TB, :], xT[:, kc, nb * TB:(nb + 1) * TB], w[:, kc, 512:1024], kc == 0, kc == 7,
                                [xT, w], [pp])
                    self.cp("act", vtok[0:TB, nb, :], pp[0:TB, :], [pp], [vtok])
                for nb in range(NB):
                    pp = ps[2 + nb % 2]
                    for kc in range(8):
                        self.mm(pp[0:TB, :], xT[:, kc, nb * TB:(nb + 1) * TB], w[:, kc, 1024:1536], kc == 0, kc == 7,
                                [xT, w], [pp])
                    self.act(gs[0:TB, nb, :], pp[0:TB, :], AF.Sigmoid, [pp], [gs])
                    self.tt("dve", gs[0:TB, nb, :], gs[0:TB, nb, :], pp[0:TB, :], ALU.mult, [gs, pp], [gs])
            def stage2(t):
                qdT, kiT, kitok, vtok, gs, pT = (qdTs[t % 2], kiTs[t % 2], kitoks[t % 2], vtoks[t % 2], gss[t % 2],
                                                 pTs[t % 2])
                ogt = og[t % 2]
                for c in range(NB):
                    po = ps[4 + c % 2]
                    self.mm(po[0:TB, :], pT[0:TB, c, :], vtok[0:TB, c, :], True, False, [pT, vtok], [po])
                    for half in range(2):
                        self.mm(po[0:TB, :], qdT[:, half, c * TB:(c + 1) * TB], Sb[:, half, :], False, half == 1,
                                [qdT, Sb], [po])
                    for half in range(2):
                        pd = ps[6 + half]
                        self.mm(pd[:, :], kitok[0:TB, c, half * 128:(half + 1) * 128], vtok[0:TB, c, :], True, True,
                                [kitok, vtok], [pd])
                    for half in range(2):
                        self.act(tmp[:, half, :], ps[6 + half][:, :], AF.Identity, [ps[6 + half], gam], [tmp],
                                 scale=gam[:, 0:1])
                    self.stt("dve", Sf[:], Sf[:], gam[:, 0:1], tmp[:], ALU.mult, ALU.add, [Sf, tmp, gam], [Sf])
                    self.cp("pool", Sb[:], Sf[:], [Sf], [Sb])
                    st = stat[c % 2]
                    rstd, nbias = self.ln_rows(lambda lo, hi, po=po: po[0:TB, lo:hi], TB, 512, [po], st)
                    o_n = on[c % 2]
                    self.act(o_n[0:TB, :], po[0:TB, :], AF.Identity, [po, st], [o_n], bias=nbias, scale=rstd)
                    self.tt("pool", o_n[0:TB, :], o_n[0:TB, :], lnp[0:TB, 0, :], ALU.mult, [o_n, lnp], [o_n])
                    self.tt("pool", o_n[0:TB, :], o_n[0:TB, :], lnp[0:TB, 1, :], ALU.add, [o_n, lnp], [o_n])
                    self.tt("dve", ogt[0:TB, c, :], o_n[0:TB, :], gs[0:TB, c, :], ALU.mult, [o_n, gs], [ogt])
                S.dma("pool", J.og_dst(t * T, T).rearrange("(nb p) e -> p nb e", p=TB), ogt[0:TB, :, :],
                      reads=[ogt], writes=[J.og_buf])
                if getattr(J, "after_tile", None) is not None:
                    J.after_tile(t)

            load(0)
            if ntile > 1:
                load(1)
            stage1(0)
            for t in range(ntile):
                if t + 1 < ntile:
                    stage1(t + 1)
                if t + 2 < ntile:
                    load(t + 2)
                stage2(t)
            S.dma("pool", J.Sout.rearrange("(h p) e -> p h e", p=128), Sf[:], reads=[Sf], writes=[J.Sout_buf])
            S.barrier()
            S.release(bufs)

    def phaseB(self, J, C):
        nc, S, ps = self.nc, self.S, self.ps
        TB, NB = J.TB, J.NB
        T = TB * NB
        ntile = J.ntok // T
        KC = J.KC
        bufs = []
        with ExitStack() as es:
            sb = lambda n, s, d: self.sbuf(es, n, s, d, bufs)
            mixT = sb("BmixT", [128, KC, T], BF16)
            mtok = sb("Bmtok", [128, 2048], BF16) if J.tokmajor_in else None
            xres = sb("Bxres", [128, NB, 1024], F32)
            xmid = sb("Bxmid", [128, NB, 1024], F32)
            xmb = [sb("Bxmb", [128, 1024], BF16) for _ in range(2)]
            xmT = sb("BxmT", [128, 8, T], BF16)
            hid = sb("Bhid", [128, NFC, T], BF16)
            aext = [sb("Baext", [128, T + 2], F32) for _ in range(2)]
            u = [sb("Bu", [128, T], F32) for _ in range(2)]
            ge = [sb("Bge", [128, T], F32) for _ in range(2)]
            aprev = sb("Baprev", [128, NFC, 2], F32)
            wo = [sb("Bwo", [128, 8, 512], BF16) for _ in range(2)]
            wu = [sb("Bwu", [128, 8, 256], BF16) for _ in range(3)]
            wd = [sb("Bwd", [128, 11, 512], BF16) for _ in range(2)]
            lnt = sb("Blnt", [128, 4, 1024], F32)
            cvp = sb("Bcvp", [128, NFC, 4], F32)
            yout = [sb("Byout", [128, 1024], F32) for _ in range(2)]
            stat = [sb("Bstat", [128, 32], F32) for _ in range(2)]
            flag = sb("Bflag", [128, 1], F32)
            idx = sb("Bidx", [128, J.idx.t.shape[1]], I32) if J.idx is not None else None
            xhb_s = sb("Bxhb", [64, 1024], BF16) if J.halo else None

            for i in range(4):
                S.dma("sp", lnt[:, i, :], J.ln[i, :].partition_broadcast(128), reads=[J.ln_buf], writes=[lnt])
            S.dma("sp", cvp[:], J.convp, reads=[J.convp_buf], writes=[cvp])
            S.dma("sp", flag[:], C.flag.t[:], reads=[C.flag], writes=[flag])
            if idx is not None:
                S.dma("sp", idx[:], J.idx.t[:], reads=[J.idx], writes=[idx])
            if J.aprev0 is not None:
                S.dma("sp", aprev[:], J.aprev0, reads=[J.aprev0_buf], writes=[aprev])
            else:
                S.op("pool", lambda e: e.memset(aprev[:], 0.0), writes=[aprev])

            def run_tile(tb, nb_, t, halo):
                Tt = tb * nb_
                if J.tokmajor_in:
                    for nb in range(nb_):
                        J.load_mix_tok(self, t, nb, tb, halo, mtok, idx)
                        for kc in range(KC):
                            pb = ps[4 + (kc // 4) % 2]
                            pv = pb[:].bitcast(BF16)
                            q = kc % 4
                            self.tr(pv[:, q * 128:q * 128 + tb], mtok[0:tb, kc * 128:(kc + 1) * 128],
                                    C.identb[0:tb, 0:tb], [mtok, C.identb_buf], [pb], signal=(q == 3))
                            if q == 3:
                                k0 = kc - 3
                                self.cp("act" if (kc // 4) % 2 else "dve",
                                        mixT[:, k0:k0 + 4, nb * tb:(nb + 1) * tb],
                                        pv[:, 0:512].rearrange("p (q c) -> p q c", c=128)[:, :, 0:tb], [pb], [mixT])
                else:
                    J.load_mix_T(self, t, halo, mixT, idx, Tt, hid)
                if halo:
                    J.load_xres_halo(self, xres, idx, xhb_s)
                else:
                    S.dma("sp", xres[0:tb, 0:nb_, :],
                          J.xres[t * T:(t + 1) * T, :].rearrange("(nb p) f -> p nb f", p=tb),
                          reads=[J.xres_buf], writes=[xres])
                for half in range(2):
                    for g in range(KC // 8):
                        wslot = wo[(half * (KC // 8) + g) % 2]
                        S.dma("sp", wslot[:], J.w_out[half, :, g * 8:(g + 1) * 8, :], reads=[J.w_out_buf], writes=[wslot])
                        for k8 in range(8):
                            kc = g * 8 + k8
                            for nb in range(nb_):
                                self.mm(ps[nb][0:tb, :], mixT[:, kc, nb * tb:(nb + 1) * tb], wslot[:, k8, :],
                                        kc == 0, kc == KC - 1, [mixT, wslot], [ps[nb]])
                    for nb in range(nb_):
                        self.stt("dve", xmid[0:tb, nb, half * 512:(half + 1) * 512],
                                 xres[0:tb, nb, half * 512:(half + 1) * 512], ALPHA, ps[nb][0:tb, :],
                                 ALU.mult, ALU.add, [xres, ps[nb]], [xmid])
                for nb in range(nb_):
                    st = stat[nb % 2]
                    rstd, nbias = self.ln_rows(lambda lo, hi, nb=nb: xmid[0:tb, nb, lo:hi], tb, 1024, [xmid], st)
                    self.act(xmid[0:tb, nb, :], xmid[0:tb, nb, :], AF.Identity, [xmid, st], [xmid], bias=nbias,
                             scale=rstd)
                    self.tt("pool", xmid[0:tb, nb, :], xmid[0:tb, nb, :], lnt[0:tb, 0, :], ALU.mult, [xmid, lnt], [xmid])
                    xb_ = xmb[nb % 2]
                    self.tt("pool", xmid[0:tb, nb, :], xmid[0:tb, nb, :], lnt[0:tb, 1, :], ALU.add, [xmid, lnt], [xmid])
                    self.cp("pool", xb_[0:tb, :], xmid[0:tb, nb, :], [xmid], [xb_])
                    for kc in range(8):
                        pb = ps[4 + (kc // 4) % 2]
                        pv = pb[:].bitcast(BF16)
                        q = kc % 4
                        self.tr(pv[:, q * 128:q * 128 + tb], xb_[0:tb, kc * 128:(kc + 1) * 128], C.identb[0:tb, 0:tb],
                                [xb_, C.identb_buf], [pb], signal=(q == 3))
                        if q == 3:
                            k0 = kc - 3
                            self.cp("act" if (kc // 4) % 2 else "dve", xmT[:, k0:k0 + 4, nb * tb:(nb + 1) * tb],
                                    pv[:, 0:512].rearrange("p (q c) -> p q c", c=128)[:, :, 0:tb], [pb], [xmT])
                for j in range(NFC):
                    ws = wu[j % 3]
                    S.dma("sp", ws[:], J.w_up[j], reads=[J.w_up_buf], writes=[ws])
                    pg = ps[4 + 2 * (j % 2)]
                    pvv = ps[5 + 2 * (j % 2)]
                    for kc in range(8):
                        self.mm(pg[:, 0:Tt], ws[:, kc, 128:256], xmT[:, kc, 0:Tt], kc == 0, kc == 7, [ws, xmT], [pg])
                    ae = aext[j % 2]
                    self.cp("pool", ae[:, 0:2], aprev[:, j, :], [aprev], [ae])
                    self.cp("act", ae[:, 2:2 + Tt], pg[:, 0:Tt], [pg], [ae])
                    if halo:
                        self.ts("pool", aprev[:, j, :], ae[:, Tt:Tt + 2], flag[:, 0:1], ALU.mult, [ae, flag], [aprev])
                        continue
                    self.cp("pool", aprev[:, j, :], ae[:, Tt:Tt + 2], [ae], [aprev])
                    for kc in range(8):
                        self.mm(pvv[:, 0:Tt], ws[:, kc, 0:128], xmT[:, kc, 0:Tt], kc == 0, kc == 7, [ws, xmT], [pvv])
                    uu = u[j % 2]
                    self.act(uu[:, 0:Tt], pg[:, 0:Tt], AF.Identity, [pg, cvp], [uu], bias=cvp[:, j, 3:4],
                             scale=cvp[:, j, 2:3])
                    self.stt("dve", uu[:, 0:Tt], ae[:, 1:1 + Tt], cvp[:, j, 1:2], uu[:, 0:Tt], ALU.mult, ALU.add,
                             [ae, cvp, uu], [uu])
                    self.stt("dve", uu[:, 0:Tt], ae[:, 0:Tt], cvp[:, j, 0:1], uu[:, 0:Tt], ALU.mult, ALU.add,
                             [ae, cvp, uu], [uu])
                    gg = ge[j % 2]
                    self.act(gg[:, 0:Tt], uu[:, 0:Tt], AF.Gelu, [uu], [gg])
                    self.tt("dve", hid[:, j, 0:Tt], gg[:, 0:Tt], pvv[:, 0:Tt], ALU.mult, [gg, pvv], [hid])
                if halo:
                    return
                for half in range(2):
                    for g in range(2):
                        wslot = wd[(half * 2 + g) % 2]
                        S.dma("sp", wslot[:], J.w_down[half, :, g * 11:(g + 1) * 11, :], reads=[J.w_down_buf],
                              writes=[wslot])
                        for k11 in range(11):
                            kc = g * 11 + k11
                            for nb in range(nb_):
                                self.mm(ps[nb][0:tb, :], hid[:, kc, nb * tb:(nb + 1) * tb], wslot[:, k11, :],
                                        kc == 0, kc == NFC - 1, [hid, wslot], [ps[nb]])
                    for nb in range(nb_):
                        sl = xmid[0:tb, nb, half * 512:(half + 1) * 512]
                        self.stt("dve", sl, sl, ALPHA, ps[nb][0:tb, :], ALU.mult, ALU.add, [xmid, ps[nb]], [xmid])
                for nb in range(nb_):
                    st = stat[nb % 2]
                    yo = yout[nb % 2]
                    rstd, nbias = self.ln_rows(lambda lo, hi, nb=nb: xmid[0:tb, nb, lo:hi], tb, 1024, [xmid], st)
                    self.act(yo[0:tb, :], xmid[0:tb, nb, :], AF.Identity, [xmid, st], [yo], bias=nbias, scale=rstd)
                    self.tt("pool", yo[0:tb, :], yo[0:tb, :], lnt[0:tb, 2, :], ALU.mult, [yo, lnt], [yo])
                    self.tt("pool", yo[0:tb, :], yo[0:tb, :], lnt[0:tb, 3, :], ALU.add, [yo, lnt], [yo])
                    r0 = t * T + nb * tb
                    S.dma("pool", J.y[r0:r0 + tb, :], yo[0:tb, :], reads=[yo], writes=[J.y_buf])
                    if J.ybf is not None:
                        S.dma("pool", J.ybf[r0:r0 + tb, :], yo[0:tb, :], reads=[yo], writes=[J.ybf_buf])
                if getattr(J, "after_tile", None) is not None:
                    J.after_tile(t)

            if J.halo:
                run_tile(64, 1, None, True)
            for t in range(ntile):
                run_tile(TB, NB, t, False)
            S.dma("pool", J.conv_out, aprev[:], reads=[aprev], writes=[J.conv_out_buf])
            S.barrier()
            S.release(bufs)

    def phaseC1(self, J, C):
        nc, S, ps = self.nc, self.S, self.ps
        bufs = []
        NBM = 4
        with ExitStack() as es:
            sb = lambda n, s, d: self.sbuf(es, n, s, d, bufs)
            w = sb("Cw", [128, 8, 776], BF16)
            xtok = [sb("Cxtok", [128, NBM, 1024], BF16) for _ in range(2)]
            xT = sb("CxT", [128, 8, NBM * 128], BF16)
            qk = [sb("Cqk", [128, NBM, 512], F32) for _ in range(2)]
            vf = [sb("Cvf", [128, NBM, 256], F32) for _ in range(2)]
            sq = sb("Csq", [128, NBM, 256], F32)
            sm = [sb("Csm", [128, NBM, 64], F32) for _ in range(2)]
            smb = [sb("Csmb", [128, NBM, 32], BF16) for _ in range(2)]
            lf = [sb("Clf", [128, NBM, 4], F32) for _ in range(2)]
            carry = sb("Ccarry", [1, 4], F32)
            Kp = [sb("CKp", [128, NBM, 4, 70], BF16) for _ in range(2)]
            Qp = [sb("CQp", [128, NBM, 4, 70], BF16) for _ in range(2)]
            Vp = [sb("CVp", [128, NBM, 4, 65], BF16) for _ in range(2)]
            KT = [sb("CKT", [128, 4, NBM * 128], BF16) for _ in range(2)]
            QT = [sb("CQT", [128, 4, NBM * 128], BF16) for _ in range(2)]
            bfb = sb("Cbfb", [128, 4], F32)

            S.dma("pool", w[:], J.w.t[:], reads=[J.w], writes=[w])
            S.dma("sp", bfb[:], J.bf.partition_broadcast(128), reads=[J.bf_buf], writes=[bfb])
            S.op("pool", lambda e: e.memset(carry[:], 0.0), writes=[carry])
            for i in range(2):
                S.op("pool", lambda e, i=i: e.memset(Kp[i][:, :, :, 67:70], 1.0), writes=[Kp[i]])
                S.op("pool", lambda e, i=i: e.memset(Qp[i][:, :, :, 64:67], 1.0), writes=[Qp[i]])

            def split3(src_ap, tb, NB, dst_fn, smt, smbt, neg):
                self.cp("dve", smbt[0:tb, 0:NB, 0:4], src_ap, [smt], [smbt])
                self.tt("dve", smt[0:tb, 0:NB, 36:40], src_ap, smbt[0:tb, 0:NB, 0:4], ALU.subtract, [smt, smbt], [smt])
                self.cp("dve", smbt[0:tb, 0:NB, 4:8], smt[0:tb, 0:NB, 36:40], [smt], [smbt])
                self.tt("dve", smt[0:tb, 0:NB, 44:48], smt[0:tb, 0:NB, 36:40], smbt[0:tb, 0:NB, 4:8], ALU.subtract,
                        [smt, smbt], [smt])
                self.cp("dve", smbt[0:tb, 0:NB, 8:12], smt[0:tb, 0:NB, 44:48], [smt], [smbt])
                for i in range(3):
                    dst, dbuf = dst_fn(i)
                    if neg:
                        self.ts("pool", dst, smbt[0:tb, 0:NB, 4 * i:4 * i + 4], -1.0, ALU.mult, [smbt], [dbuf])
                    else:
                        self.cp("pool", dst, smbt[0:tb, 0:NB, 4 * i:4 * i + 4], [smbt], [dbuf])

            tiles = []
            for (kind, nblk, tb, src) in J.segments:
                NB = min(NBM, nblk)
                assert nblk % NB == 0
                for ti in range(nblk // NB):
                    tiles.append((kind, tb, NB, src, ti))

            def issue_load(idx):
                kind, tb, NB, src, ti = tiles[idx]
                i2 = idx % 2
                T = NB * tb
                r0 = ti * T
                if kind == "cache":
                    S.dma("sp", qk[i2][0:tb, 0:NB, 256:512], src.k[r0:r0 + T, :].rearrange("(n p) f -> p n f", p=tb),
                          reads=[src.k_buf], writes=[qk[i2]])
                    S.dma("sp", vf[i2][0:tb, 0:NB, :], src.v[r0:r0 + T, :].rearrange("(n p) f -> p n f", p=tb),
                          reads=[src.v_buf], writes=[vf[i2]])
                    S.dma("sp", lf[i2][0:tb, 0:NB, :], src.lf[r0:r0 + T, :].rearrange("(n p) f -> p n f", p=tb),
                          reads=[src.lf_buf], writes=[lf[i2]])
                else:
                    xr0 = src.xrow(r0) if getattr(src, "xrow", None) is not None else r0
                    S.dma("sp", xtok[i2][0:tb, 0:NB, :], src.x[xr0:xr0 + T, :].rearrange("(n p) f -> p n f", p=tb),
                          reads=[src.x_buf], writes=[xtok[i2]])

            issue_load(0)
            for tl in range(len(tiles)):
                if True:
                    kind, tb, NB, src, ti = tiles[tl]
                    T = NB * tb
                    if tl + 1 < len(tiles):
                        issue_load(tl + 1)
                    i2 = tl % 2
                    qkt, vft, lft, smt, smbt = qk[i2], vf[i2], lf[i2], sm[i2], smb[i2]
                    Kpt, Qpt, Vpt, KTt, QTt = Kp[i2], Qp[i2], Vp[i2], KT[i2], QT[i2]
                    r0 = ti * T
                    if kind == "new":
                        xk = xtok[i2]
                        for kc in range(8):
                            pb = ps[kc % 2]
                            pv = pb[:].bitcast(BF16)
                            for nb in range(NB):
                                self.tr(pv[:, nb * tb:(nb + 1) * tb], xk[0:tb, nb, kc * 128:(kc + 1) * 128],
                                        C.identb[0:tb, 0:tb], [xk, C.identb_buf], [pb], signal=(nb == NB - 1))
                            self.cp("act" if kc % 2 else "dve", xT[:, kc, 0:T], pv[:, 0:T], [pb], [xT])
                        for nb in range(NB):
                            p1, p2 = ps[2 + nb % 2], ps[4 + nb % 2]
                            for kc in range(8):
                                self.mm(p1[0:tb, 0:512], xT[:, kc, nb * tb:(nb + 1) * tb], w[:, kc, 0:512], kc == 0,
                                        kc == 7, [xT, w], [p1])
                            for kc in range(8):
                                self.mm(p2[0:tb, 0:260], xT[:, kc, nb * tb:(nb + 1) * tb], w[:, kc, 512:772], kc == 0,
                                        kc == 7, [xT, w], [p2])
                            self.cp("act", qkt[0:tb, nb, :], p1[0:tb, 0:512], [p1], [qkt])
                            self.cp("act", vft[0:tb, nb, :], p2[0:tb, 0:256], [p2], [vft])
                            self.cp("act", smt[0:tb, nb, 28:32], p2[0:tb, 256:260], [p2], [smt])
                        self.tt("dve", smt[0:tb, 0:NB, 0:4], smt[0:tb, 0:NB, 28:32],
                                bfb[0:tb, :].unsqueeze(1).broadcast_to([tb, NB, 4]), ALU.add, [smt, bfb], [smt])
                        self.act(smt[0:tb, 0:NB, 0:4], smt[0:tb, 0:NB, 0:4], AF.Exp, [smt], [smt], scale=-1.0)
                        self.act(smt[0:tb, 0:NB, 0:4], smt[0:tb, 0:NB, 0:4], AF.Ln, [smt], [smt],
                                 bias=C.one_col[0:tb, 0:1])
                        self.ts("dve", lft[0:tb, 0:NB, :], smt[0:tb, 0:NB, 0:4], -1.0, ALU.mult, [smt], [lft])
                        g0 = src.out_row0 + r0
                        S.dma("sp", J.pk[g0:g0 + T, :].rearrange("(n p) f -> p n f", p=tb), qkt[0:tb, 0:NB, 256:512],
                              reads=[qkt], writes=[J.pk_buf])
                        S.dma("sp", J.pv[g0:g0 + T, :].rearrange("(n p) f -> p n f", p=tb), vft[0:tb, 0:NB, :],
                              reads=[vft], writes=[J.pv_buf])
                        S.dma("sp", J.plf[g0:g0 + T, :].rearrange("(n p) f -> p n f", p=tb), lft[0:tb, 0:NB, :],
                              reads=[lft], writes=[J.plf_buf])
                    pc, pcar = ps[6], ps[7]
                    for nb in range(NB):
                        self.mm(pc[0:tb, nb * 4:nb * 4 + 4], C.tri[0:tb, 0:tb], lft[0:tb, nb, :], True, False,
                                [C.tri_buf, lft], [pc])
                        self.mm(pc[0:tb, nb * 4:nb * 4 + 4], C.ones_f[0:1, 0:tb], carry[0:1, :], False, True,
                                [C.ones_buf, carry], [pc])
                        self.mm(pcar[0:1, 0:4], C.ones_f[0:tb, 0:1], lft[0:tb, nb, :], True, False, [C.ones_buf, lft],
                                [pcar])
                        self.mm(pcar[0:1, 0:4], C.ones_f[0:1, 0:1], carry[0:1, :], False, True, [C.ones_buf, carry],
                                [pcar])
                        self.cp("act", carry[0:1, :], pcar[0:1, 0:4], [pcar], [carry])
                    self.cp("act", smt[0:tb, 0:NB, 4:8], pc[0:tb, 0:NB * 4].rearrange("p (n h) -> p n h", h=4), [pc],
                            [smt])
                    self.tt("pool", sq[0:tb, 0:NB, :], qkt[0:tb, 0:NB, 256:512], qkt[0:tb, 0:NB, 256:512], ALU.mult,
                            [qkt], [sq])
                    S.op("dve", lambda e, tb=tb, smt=smt, NB=NB: e.reduce_sum(
                        out=smt[0:tb, 0:NB, 8:12], in_=sq[0:tb, 0:NB, :].rearrange("p n (h d) -> p n h d", d=64),
                        axis=AX.X), reads=[sq], writes=[smt])
                    self.ts("dve", smt[0:tb, 0:NB, 8:12], smt[0:tb, 0:NB, 8:12], 1.0 / 16.0, ALU.mult, [smt], [smt])
                    self.act(smt[0:tb, 0:NB, 12:16], smt[0:tb, 0:NB, 8:12], AF.Exp, [smt], [smt])
                    self.tt("dve", smt[0:tb, 0:NB, 16:20], smt[0:tb, 0:NB, 4:8], smt[0:tb, 0:NB, 8:12], ALU.add, [smt],
                            [smt])
                    split3(smt[0:tb, 0:NB, 16:20], tb, NB, lambda i: (Kpt[0:tb, 0:NB, :, 64 + i], Kpt), smt, smbt, True)
                    self.cp("pool", Kpt[0:tb, 0:NB, :, 0:64],
                            qkt[0:tb, 0:NB, 256:512].rearrange("p n (h d) -> p n h d", d=64), [qkt], [Kpt])
                    self.tt("dve", Vpt[0:tb, 0:NB, :, 0:64], vft[0:tb, 0:NB, :].rearrange("p n (h d) -> p n h d", d=64),
                            smt[0:tb, 0:NB, 12:16].unsqueeze(3).broadcast_to([tb, NB, 4, 64]), ALU.mult, [vft, smt],
                            [Vpt])
                    self.cp("dve", Vpt[0:tb, 0:NB, :, 64], smt[0:tb, 0:NB, 12:16], [smt], [Vpt])
                    k0 = J.kpos0[kind] + r0
                    for hh in range(4):
                        S.dma("sp", J.VV[hh, k0:k0 + T, :].rearrange("(n p) e -> p n e", p=tb), Vpt[0:tb, 0:NB, hh, :],
                              reads=[Vpt], writes=[J.VV_buf])

                    def transpose_out(Xpt, XTt, bank):
                        for n0 in range(0, NB, 2):
                            pk_ = ps[bank + (n0 // 2) % 2]
                            pkv = pk_[:].bitcast(BF16)
                            nn = min(2, NB - n0)
                            for n1 in range(nn):
                                for hh in range(4):
                                    self.tr(pkv[0:70, n1 * 512 + hh * 128:n1 * 512 + hh * 128 + tb],
                                            Xpt[0:tb, n0 + n1, hh, :], C.identb[0:tb, 0:tb], [Xpt, C.identb_buf], [pk_],
                                            signal=(hh == 3 and n1 == nn - 1))
                            src_v = pkv[0:70, 0:nn * 512].rearrange("p (n h c) -> p n h c", n=nn, h=4)[:, :, :, 0:tb]
                            dst_v = XTt[0:70, :, n0 * 128:(n0 + nn) * 128].rearrange("p h (n c) -> p n h c", c=128)[
                                :, :, :, 0:tb]
                            self.cp("act", dst_v, src_v, [pk_], [XTt])

                    transpose_out(Kpt, KTt, 0)
                    if tb == 128:
                        S.dma("sp", J.KT[:, :, k0:k0 + T].rearrange("h p t -> p h t"), KTt[0:70, :, 0:T], reads=[KTt],
                              writes=[J.KT_buf])
                    else:
                        S.dma("sp", J.KT[:, :, k0:k0 + tb].rearrange("h p t -> p h t"), KTt[0:70, :, 0:tb],
                              reads=[KTt], writes=[J.KT_buf])
                    if kind == "new":
                        self.tt("pool", sq[0:tb, 0:NB, :], qkt[0:tb, 0:NB, 0:256], qkt[0:tb, 0:NB, 0:256], ALU.mult,
                                [qkt], [sq])
                        S.op("dve", lambda e, tb=tb, smt=smt, NB=NB: e.reduce_sum(
                            out=smt[0:tb, 0:NB, 20:24], in_=sq[0:tb, 0:NB, :].rearrange("p n (h d) -> p n h d", d=64),
                            axis=AX.X), reads=[sq], writes=[smt])
                        self.stt("dve", smt[0:tb, 0:NB, 24:28], smt[0:tb, 0:NB, 20:24], -1.0 / 16.0, smt[0:tb, 0:NB, 4:8],
                                 ALU.mult, ALU.add, [smt], [smt])
                        split3(smt[0:tb, 0:NB, 24:28], tb, NB, lambda i: (Qpt[0:tb, 0:NB, :, 67 + i], Qpt), smt, smbt,
                               False)
                        self.ts("pool", Qpt[0:tb, 0:NB, :, 0:64],
                                qkt[0:tb, 0:NB, 0:256].rearrange("p n (h d) -> p n h d", d=64), 0.125, ALU.mult, [qkt],
                                [Qpt])
                        transpose_out(Qpt, QTt, 2)
                        q0 = src.out_row0 + r0
                        tq = T if tb == 128 else tb
                        S.dma("sp", J.QT[:, :, q0:q0 + tq].rearrange("h p t -> p h t"), QTt[0:70, :, 0:tq],
                              reads=[QTt], writes=[J.QT_buf])
            S.barrier()
            S.release(bufs)

    def phaseC2(self, J, C):
        nc, S, ps = self.nc, self.S, self.ps
        NK, NQ, TQ, TBq = J.NK, J.NQ, J.TQ, J.TBq
        ncache = J.ncache
        NKB = (NK + 127) // 128
        bufs = []
        with ExitStack() as es:
            sb = lambda n, s, d: self.sbuf(es, n, s, d, bufs)
            KTs = [sb("AtK", [128, NK], BF16) for _ in range(2)]
            Vs = [sb("AtV", [128, NKB, 65], BF16) for _ in range(2)]
            QTs = [sb("AtQ", [128, TQ], BF16) for _ in range(2)]
            PT = [sb("AtP", [128, TQ], BF16) for _ in range(3)]
            num = [sb("Atnum", [128, TQ], F32) for _ in range(2)]
            rd = [sb("Atrd", [128, TQ], F32) for _ in range(2)]
            oT = [sb("AtoT", [128, TQ], BF16) for _ in range(2)]
            nqt = NQ // TQ
            nsub = TQ // TBq
            LOOK = 2
            for hh in range(4):
                Kt, Vt = KTs[hh % 2], Vs[hh % 2]
                S.dma("sp", Kt[0:70, :], J.KT[hh], reads=[J.KT_buf], writes=[Kt])
                nfull = NK // 128
                if nfull:
                    S.dma("sp", Vt[:, 0:nfull, :], J.VV[hh, 0:nfull * 128, :].rearrange("(b p) e -> p b e", p=128),
                          reads=[J.VV_buf], writes=[Vt])
                if NK % 128:
                    S.dma("sp", Vt[0:NK % 128, nfull, :], J.VV[hh, nfull * 128:NK, :], reads=[J.VV_buf], writes=[Vt])
                units = []
                for qt in range(nqt):
                    kblocks = [(kb * 128, 128, None) for kb in range(ncache // 128)]
                    for u_all in range((qt + 1) * nsub):
                        k0 = ncache + u_all * TBq
                        u = u_all - qt * nsub
                        kblocks.append((k0, TBq, u if u >= 0 else None))
                    for bi, (k0, kb, u) in enumerate(kblocks):
                        units.append((qt, bi, len(kblocks), k0, kb, u))
                n = len(units)
                for i in range(n + LOOK):
                    if i < n:
                        qt, bi, nkb, k0, kb, u = units[i]
                        Qt = QTs[(hh * nqt + qt) % 2]
                        if bi == 0:
                            S.dma("sp", Qt[0:70, :], J.QT[hh, :, qt * TQ:(qt + 1) * TQ], reads=[J.QT_buf], writes=[Qt])
                        c0 = 0 if u is None else u * TBq
                        pss, pt = ps[i % 3], PT[i % 3]
                        self.mm(pss[0:kb, c0:TQ], Kt[0:70, k0:k0 + kb], Qt[0:70, c0:TQ], True, u is None, [Kt, Qt], [pss])
                        if u is not None:
                            self.mm(pss[0:kb, c0:c0 + TBq], C.identb[0:kb, 0:kb], C.negm[0:kb, 0:TBq], False, True,
                                    [C.identb_buf, C.negm_buf], [pss])
                        self.act(pt[0:kb, c0:TQ], pss[0:kb, c0:TQ], AF.Exp, [pss], [pt])
                    j = i - LOOK
                    if j < 0:
                        continue
                    qt, bi, nkb, k0, kb, u = units[j]
                    c0 = 0 if u is None else u * TBq
                    pt = PT[j % 3]
                    i2 = (hh * nqt + qt) % 2
                    po = ps[4 + i2]
                    self.mm(po[0:65, c0:TQ], Vt[0:kb, k0 // 128, :], pt[0:kb, c0:TQ], bi == 0, bi == nkb - 1,
                            [Vt, pt], [po])
                    if bi != nkb - 1:
                        continue
                    S.op("dve", lambda e, i2=i2, po=po: e.reciprocal(out=rd[i2][64:65, :], in_=po[64:65, 0:TQ]),
                         reads=[po], writes=[rd[i2]])
                    self.cp("dve", num[i2][0:64, :], po[0:64, 0:TQ], [po], [num[i2]])
                    pbc = ps[6 + i2]
                    self.mm(pbc[0:64, 0:TQ], C.ones_f[64:65, 0:64], rd[i2][64:65, :], True, True, [C.ones_buf, rd[i2]], [pbc])
                    self.tt("dve", oT[i2][0:64, :], num[i2][0:64, :], pbc[0:64, 0:TQ], ALU.mult, [num[i2], pbc], [oT[i2]])
                    S.dma("pool", J.oT_dst(hh, qt), oT[i2][0:64, :], reads=[oT[i2]], writes=[J.oT_buf])
                    if getattr(J, "after_unit", None) is not None:
                        J.after_unit(hh * nqt + qt)
            S.barrier()
            S.release(bufs)


def build_program(SEQ, stop=99):
    B = Builder(SEQ)
    nc, S = B.nc, B.S
    SEG = SEQ // 4
    NQT = SEQ // 512
    NTB_B = SEG // 128
    ntile_B = SEG // 512

    xA = B.din("xA", [SEQ, D])
    xB = B.din("xB", [SEG, D])
    xH = B.din("xH", [64, D])
    xS = B.din("xS", [NS, D])
    w_inP = B.din("w_inP", [128, 8, 1536])
    w_inS = [B.din("w_inS%d" % h, [128, 8, 1536]) for h in range(4)]
    ropeP = B.din("ropeP", [4, 128, SEQ])
    ropeS = B.din("ropeS", [4, 4, 128, NS])
    gamP = B.din("gamP", [128, 1])
    gamS = B.din("gamS", [4, 128, 1])
    lnretP = B.din("lnretP", [2, 512])
    lnretS = B.din("lnretS", [4, 2, 512])
    sret = B.din("sret", [4, 256, 512])
    w_outA = B.din("w_outA", [2, 128, 16, 512])
    w_up = B.din("w_up", [2, NFC, 128, 8, 256])
    w_down = B.din("w_down", [2, 2, 128, NFC, 512])
    convp = B.din("convp", [2, 128, NFC, 4])
    ln4 = B.din("ln4", [2, 4, D])
    sconv = B.din("sconv", [2, 128, NFC, 2])
    w_qkvP = B.din("w_qkvP", [128, 8, 776])
    w_qkvS = [B.din("w_qkvS%d" % g, [128, 8, 776]) for g in range(4)]
    bfP = B.din("bfP", [1, 4])
    bfS = B.din("bfS", [4, 4])
    w_outB = B.din("w_outB", [2, 128, 8, 512])
    ck = B.din("ck", [PAST, D])
    cv = B.din("cv", [PAST, D])
    clf = B.din("clf", [PAST, 16])
    flag_in = B.din("flag", [128, 1])
    idxB_in = B.din("idxB", [128, (NTB_B + 1) * 4], I32)
    idxD_in = B.din("idxD", [128, (ntile_B + 1) * 8 + 1], I32)
    cst = B.din("cst", [5, 128, 128])

    yP = B.dout("yP", [SEG, D])
    yS = B.dout("yS", [NS, D])
    retP = B.dout("retP", [256, 512])
    retS = B.dout("retS", [4, 256, 512])
    pk = B.dout("pk", [SEQ, 256])
    pv = B.dout("pv", [SEQ, 256])
    plf = B.dout("plf", [SEQ, 4])
    sk = B.dout("sk", [4, NS, 256])
    sv = B.dout("sv", [4, NS, 256])
    slf = B.dout("slf", [4, NS, 4])
    pconv = B.dout("pconv", [2, 128, NFC, 2])
    sconv_o = B.dout("sconv_o", [2, 128, NFC, 2])

    og_loc = B.dscr("og_loc", [SEQ, 512], BF16)
    og_all = B.dscr("og_all", [4 * SEQ, 512], BF16)
    og_s = B.dscr("og_s", [4 * NS, 512], BF16)
    x1_loc = B.dscr("x1_loc", [SEG, D], F32)
    x1_bf = B.dscr("x1_bf", [SEG, D], BF16)
    x1_all = B.dscr("x1_all", [SEQ, D], BF16)
    x1_s = B.dscr("x1_s", [NS, D], F32)
    x1_sb = B.dscr("x1_sb", [NS, D], BF16)
    KTp = B.dscr("KTp", [4, 70, SEQ], BF16)
    VVp = B.dscr("VVp", [4, SEQ, 65], BF16)
    QTp = B.dscr("QTp", [4, 70, SEQ], BF16)
    NKS = PAST + NS
    KTs_ = B.dscr("KTs", [4, 70, NKS], BF16)
    VVs_ = B.dscr("VVs", [4, NKS, 65], BF16)
    QTs_ = B.dscr("QTs", [4, 70, NS], BF16)
    oT_loc = B.dscr("oT_loc", [4 * NQT * 64, 512], BF16)
    oT_all = B.dscr("oT_all", [4 * 4 * NQT * 64, 512], BF16)
    oT_s = B.dscr("oT_s", [16 * 64, NS], BF16)
    wb_outA = B.dscr("wb_outA", [2, 128, 16, 512], BF16)
    wb_up = B.dscr("wb_up", [2, NFC, 128, 8, 256], BF16)
    wb_down = B.dscr("wb_down", [2, 2, 128, NFC, 512], BF16)
    wb_outB = B.dscr("wb_outB", [2, 128, 8, 512], BF16)

    C = Job()
    cf = nc.alloc_sbuf_tensor("c_f32", [128, 5, 128], F32)
    cb = nc.alloc_sbuf_tensor("c_bf", [128, 5, 128], BF16)
    cfb, cbb = Buf("c_f32", cf), Buf("c_bf", cb)
    S.dma("sp", cf[:], cst.t[:].rearrange("a p n -> p a n"), reads=[cst], writes=[cfb])
    S.dma("pool", cb[:], cst.t[:].rearrange("a p n -> p a n"), reads=[cst], writes=[cbb])
    C.identb, C.identb_buf = cb[:, 0, :], cbb
    C.maskT, C.maskT_buf = cf[:, 1, :], cfb
    C.tri, C.tri_buf = cf[:, 2, :], cfb
    C.negm, C.negm_buf = cb[:, 3, :], cbb
    C.ones_f, C.ones_buf = cf[:, 4, :], cfb
    C.one_col = cf[:, 4, 0:1]
    C.flag = flag_in

    for l in range(2):
        for h in range(2):
            for pg in range(0, 128, 32):
                if l == 0:
                    S.dma("pool", wb_outA.t[h, pg:pg + 32], w_outA.t[h, pg:pg + 32], reads=[w_outA], writes=[wb_outA])
                else:
                    S.dma("pool", wb_outB.t[h, pg:pg + 32], w_outB.t[h, pg:pg + 32], reads=[w_outB], writes=[wb_outB])
        for j in range(NFC):
            S.dma("pool", wb_up.t[l, j], w_up.t[l, j], reads=[w_up], writes=[wb_up])
        for h in range(2):
            for pg in range(0, 128, 32):
                S.dma("pool", wb_down.t[l, h, pg:pg + 32], w_down.t[l, h, pg:pg + 32], reads=[w_down],
                      writes=[wb_down])

    if stop <= 0:
        S.final_wait()
        return B
    def gmap(rho, h, CR):
        return (rho // CR) * 4 * CR + h * CR + (rho % CR)

    def exch_chunk(src, dst, i, CR):
        S.collective("AllGather", src, dst, src.t[i * CR:(i + 1) * CR, :], dst.t[i * 4 * CR:(i + 1) * 4 * CR, :])

    JA = Job(TB=128, NB=4, ntok=SEQ, x=xA.t, x_buf=xA, w=w_inP, rope=ropeP.t, rope_buf=ropeP,
             gam=gamP.t[:], gam_buf=gamP, lng=lnretP.t[0, :], lnb=lnretP.t[1, :], lnb_buf=lnretP,
             S0=None, S0_buf=None, Sout=retP.t[:], Sout_buf=retP,
             og_dst=lambda r0, n: og_loc.t[r0:r0 + n, :], og_buf=og_loc,
             after_tile=lambda t: exch_chunk(og_loc, og_all, t // 2, 1024) if t % 2 == 1 else None)
    B.phaseA(JA, C)
    if stop <= 1:
        S.final_wait()
        return B
    pass

    if stop <= 2:
        S.final_wait()
        return B
    for h in range(4):
        JS = Job(TB=NS, NB=1, ntok=NS, x=xS.t, x_buf=xS, w=w_inS[h], rope=ropeS.t[h], rope_buf=ropeS,
                 gam=gamS.t[h], gam_buf=gamS, lng=lnretS.t[h, 0, :], lnb=lnretS.t[h, 1, :], lnb_buf=lnretS,
                 S0=sret.t[h], S0_buf=sret, Sout=retS.t[h], Sout_buf=retS,
                 og_dst=lambda r0, n, h=h: og_s.t[h * NS + r0:h * NS + r0 + n, :], og_buf=og_s)
        B.phaseA(JS, C)

    if stop <= 3:
        S.final_wait()
        return B
    def load_mix_tok_sample(Bd, t, nb, tb, halo, mtok, idx):
        for h in range(4):
            Bd.S.dma("sp", mtok[0:tb, h * 512:(h + 1) * 512], og_s.t[h * NS:(h + 1) * NS, :], reads=[og_s],
                     writes=[mtok])

    JBs = Job(TB=NS, NB=1, ntok=NS, KC=16, tokmajor_in=True, load_mix_tok=load_mix_tok_sample, halo=False,
              xres=xS.t, xres_buf=xS, w_out=wb_outA.t, w_out_buf=wb_outA, w_up=wb_up.t[0], w_up_buf=wb_up,
              w_down=wb_down.t[0], w_down_buf=wb_down, convp=convp.t[0], convp_buf=convp, ln=ln4.t[0], ln_buf=ln4,
              aprev0=sconv.t[0], aprev0_buf=sconv, idx=None, y=x1_s.t, y_buf=x1_s, ybf=x1_sb.t, ybf_buf=x1_sb,
              conv_out=sconv_o.t[0], conv_out_buf=sconv_o)
    B.phaseB(JBs, C)

    if stop <= 4:
        S.final_wait()
        return B
    def load_mix_tok_prompt(Bd, t, nb, tb, halo, mtok, idx):
        col0 = NTB_B * 4 if halo else (t * 4 + nb) * 4
        for h in range(4):
            Bd.S.dma("pool", mtok[0:tb, h * 512:(h + 1) * 512], og_all.t[:, :], reads=[og_all, idx], writes=[mtok],
                     indirect=idx[0:tb, col0 + h:col0 + h + 1])

    def load_xres_halo_B(Bd, xres, idx, xhb_s):
        Bd.S.dma("sp", xres[0:64, 0, :], xH.t[:, :], reads=[xH], writes=[xres])

    JBp = Job(TB=128, NB=4, ntok=SEG, KC=16, tokmajor_in=True, load_mix_tok=load_mix_tok_prompt, halo=True,
              load_xres_halo=load_xres_halo_B,
              xres=xB.t, xres_buf=xB, w_out=wb_outA.t, w_out_buf=wb_outA, w_up=wb_up.t[0], w_up_buf=wb_up,
              w_down=wb_down.t[0], w_down_buf=wb_down, convp=convp.t[0], convp_buf=convp, ln=ln4.t[0], ln_buf=ln4,
              aprev0=None, aprev0_buf=None, idx=idxB_in, y=x1_loc.t, y_buf=x1_loc, ybf=x1_bf.t, ybf_buf=x1_bf,
              conv_out=pconv.t[0], conv_out_buf=pconv,
              after_tile=lambda t: exch_chunk(x1_bf, x1_all, t, 512))
    B.phaseB(JBp, C)
    if stop <= 5:
        S.final_wait()
        return B
    pass

    if stop <= 6:
        S.final_wait()
        return B
    for g in range(4):
        srcc = Job(k=ck.t[:, g * 256:(g + 1) * 256], k_buf=ck, v=cv.t[:, g * 256:(g + 1) * 256], v_buf=cv,
                   lf=clf.t[:, g * 4:(g + 1) * 4], lf_buf=clf)
        srcn = Job(x=x1_sb.t, x_buf=x1_sb, out_row0=0)
        JC = Job(w=w_qkvS[g], bf=bfS.t[g, :], bf_buf=bfS,
                 segments=[("cache", PAST // 128, 128, srcc), ("new", 1, NS, srcn)],
                 kpos0={"cache": 0, "new": PAST}, pk=sk.t[g], pk_buf=sk, pv=sv.t[g], pv_buf=sv, plf=slf.t[g],
                 plf_buf=slf, KT=KTs_.t, KT_buf=KTs_, VV=VVs_.t, VV_buf=VVs_, QT=QTs_.t, QT_buf=QTs_)
        B.phaseC1(JC, C)
        if stop == 61:
            S.final_wait()
            return B
        JC2 = Job(NK=NKS, NQ=NS, TQ=NS, TBq=NS, ncache=PAST, KT=KTs_.t, KT_buf=KTs_, VV=VVs_.t, VV_buf=VVs_,
                  QT=QTs_.t, QT_buf=QTs_,
                  oT_dst=lambda hh, qt, g=g: oT_s.t[(g * 4 + hh) * 64:(g * 4 + hh + 1) * 64, :], oT_buf=oT_s)
        B.phaseC2(JC2, C)

    if stop <= 7:
        S.final_wait()
        return B
    def load_mix_T_sample(Bd, t, halo, mixT, idx, Tt, hid):
        Bd.S.dma("sp", mixT[:, :, 0:NS], oT_s.t[:, :].rearrange("(kc p) t -> p kc t", p=128), reads=[oT_s],
                 writes=[mixT])

    JDs = Job(TB=NS, NB=1, ntok=NS, KC=8, tokmajor_in=False, load_mix_T=load_mix_T_sample, halo=False,
              xres=x1_s.t, xres_buf=x1_s, w_out=wb_outB.t, w_out_buf=wb_outB, w_up=wb_up.t[1], w_up_buf=wb_up,
              w_down=wb_down.t[1], w_down_buf=wb_down, convp=convp.t[1], convp_buf=convp, ln=ln4.t[1], ln_buf=ln4,
              aprev0=sconv.t[1], aprev0_buf=sconv, idx=None, y=yS.t, y_buf=yS, ybf=None, ybf_buf=None,
              conv_out=sconv_o.t[1], conv_out_buf=sconv_o)
    B.phaseB(JDs, C)

    if stop <= 8:
        S.final_wait()
        return B
    srcn = Job(x=x1_all.t, x_buf=x1_all, out_row0=0, xrow=lambda tok: gmap(tok % SEG, tok // SEG, 512))
    JCp = Job(w=w_qkvP, bf=bfP.t[0, :], bf_buf=bfP, segments=[("new", SEQ // 128, 128, srcn)],
              kpos0={"new": 0}, pk=pk.t, pk_buf=pk, pv=pv.t, pv_buf=pv, plf=plf.t, plf_buf=plf,
              KT=KTp.t, KT_buf=KTp, VV=VVp.t, VV_buf=VVp, QT=QTp.t, QT_buf=QTp)
    B.phaseC1(JCp, C)
    JC2p = Job(NK=SEQ, NQ=SEQ, TQ=512, TBq=128, ncache=0, KT=KTp.t, KT_buf=KTp, VV=VVp.t, VV_buf=VVp, QT=QTp.t,
               QT_buf=QTp, oT_dst=lambda hh, qt: oT_loc.t[(hh * NQT + qt) * 64:(hh * NQT + qt + 1) * 64, :],
               oT_buf=oT_loc,
               after_unit=lambda u: exch_chunk(oT_loc, oT_all, u // 16, 1024) if u % 16 == 15 else None)
    B.phaseC2(JC2p, C)
    if stop <= 9:
        S.final_wait()
        return B
    pass

    if stop <= 10:
        S.final_wait()
        return B
    def load_mix_T_prompt(Bd, t, halo, mixT, idx, Tt, hid):
        if halo:
            for kc in range(8):
                Bd.S.dma("pool", hid[:, kc, :], oT_all.t[:, :], reads=[oT_all, idx], writes=[hid],
                         indirect=idx[:, ntile_B * 8 + kc:ntile_B * 8 + kc + 1])
            Bd.cp("dve", mixT[:, :, 0:64], hid[:, 0:8, 448:512], [hid], [mixT])
        else:
            for kc in range(8):
                Bd.S.dma("pool", mixT[:, kc, :], oT_all.t[:, :], reads=[oT_all, idx], writes=[mixT],
                         indirect=idx[:, t * 8 + kc:t * 8 + kc + 1])

    def load_xres_halo_D(Bd, xres, idx, xhb_s):
        Bd.S.dma("pool", xhb_s[0:64, :], x1_all.t[:, :], reads=[x1_all, idx], writes=[xhb_s],
                 indirect=idx[0:64, (ntile_B + 1) * 8:(ntile_B + 1) * 8 + 1])
        Bd.cp("dve", xres[0:64, 0, :], xhb_s[0:64, :], [xhb_s], [xres])

    JDp = Job(TB=128, NB=4, ntok=SEG, KC=8, tokmajor_in=False, load_mix_T=load_mix_T_prompt, halo=True,
              load_xres_halo=load_xres_halo_D,
              xres=x1_loc.t, xres_buf=x1_loc, w_out=wb_outB.t, w_out_buf=wb_outB, w_up=wb_up.t[1], w_up_buf=wb_up,
              w_down=wb_down.t[1], w_down_buf=wb_down, convp=convp.t[1], convp_buf=convp, ln=ln4.t[1], ln_buf=ln4,
              aprev0=None, aprev0_buf=None, idx=idxD_in, y=yP.t, y_buf=yP, ybf=None, ybf_buf=None,
              conv_out=pconv.t[1], conv_out_buf=pconv)
    B.phaseB(JDp, C)
    S.final_wait()
    return B


def _kc_layout(wm):
    K, N = wm.shape
    return np.ascontiguousarray(wm.reshape(K // 128, 128, N).transpose(1, 0, 2))


def _rope_tables(pos, chunk, h):
    half = 128
    inv = (1.0 / (10000.0 ** (np.arange(half, dtype=np.float32) / np.float32(half)))).astype(np.float32)
    ang = pos.astype(np.float32)[:, None] * inv[None, :]
    cos = np.cos(ang).astype(np.float32).T
    sin = np.sin(ang).astype(np.float32).T
    lg = np.log1p(-np.exp2(-5.0 - h))
    i = (np.arange(len(pos)) % chunk).astype(np.float64)
    dq = np.exp((i + 1.0) * lg).astype(np.float32)[None, :]
    dk = (np.exp(-(i + 1.0) * lg) * (256.0 ** -0.5)).astype(np.float32)[None, :]
    tab = np.stack([cos * dq, sin * dq, cos * dk, sin * dk]).astype(np.float32)
    gam = np.float32(np.exp(chunk * lg))
    return tab, gam


def _gmap(rho, h, CR):
    return (rho // CR) * 4 * CR + h * CR + (rho % CR)


def _consts():
    j = np.arange(128)[:, None]
    i = np.arange(128)[None, :]
    ident = (i == j).astype(np.float32)
    maskT = (i >= j).astype(np.float32)
    tri = (j <= i).astype(np.float32)
    negm = np.where(i >= j, 0.0, NEGM).astype(np.float32)
    ones = np.ones((128, 128), np.float32)
    return np.stack([ident, maskT, tri, negm, ones])


def make_in_maps(inp, SEQ):
    f = lambda a: np.ascontiguousarray(np.asarray(a, dtype=np.float32))
    SEG = SEQ // 4
    NQT = SEQ // 512
    NTB_B = SEG // 128
    ntile_B = SEG // 512
    xp, xs = f(inp["x_prompt"]), f(inp["x_sample"])
    w_in = f(inp["w_in_a"])[0]
    HK, HV = 1024, 2048
    lnrg, lnrb = f(inp["ln_ret_g"])[0], f(inp["ln_ret_b"])[0]
    w_kvf, w_q = f(inp["w_kvf"]), f(inp["w_q_b"])[0]
    b_f = f(inp["b_f"])

    def w_in_head(h):
        cols = np.concatenate([np.arange(h * 256, (h + 1) * 256), HK + np.arange(h * 256, (h + 1) * 256),
                               2 * HK + np.arange(h * 512, (h + 1) * 512),
                               2 * HK + HV + np.arange(h * 512, (h + 1) * 512)])
        return _kc_layout(w_in[:, cols])

    def w_qkv_group(g):
        c = np.arange(g * 256, (g + 1) * 256)
        return _kc_layout(np.concatenate([w_q[:, c], w_kvf[:, c], w_kvf[:, 1024 + c],
                                          w_kvf[:, 2048 + g * 4:2048 + (g + 1) * 4],
                                          np.zeros((1024, 4), np.float32)], axis=1))

    w_in_heads = [w_in_head(h) for h in range(4)]
    w_qkv_groups = [w_qkv_group(g) for g in range(4)]
    ropeS = np.stack([_rope_tables(PAST + np.arange(NS), NS, h)[0] for h in range(4)])
    gamS = np.stack([np.full((128, 1), _rope_tables(np.arange(1), NS, h)[1], np.float32) for h in range(4)])
    lnretS = np.stack([np.stack([lnrg[h * 512:(h + 1) * 512], lnrb[h * 512:(h + 1) * 512]]) for h in range(4)])
    w_outA = f(inp["w_out_a"])[0]
    w_outA_l = np.stack([_kc_layout(w_outA[:, hf * 512:(hf + 1) * 512]) for hf in range(2)])
    w_outB = f(inp["w_out_b"])[0]
    w_outB_l = np.stack([_kc_layout(w_outB[:, hf * 512:(hf + 1) * 512]) for hf in range(2)])
    wu = f(inp["w_up"])
    w_up_l = np.empty((2, NFC, 128, 8, 256), np.float32)
    for l in range(2):
        for j in range(NFC):
            w_up_l[l, j, :, :, 0:128] = _kc_layout(wu[l][:, j * 128:(j + 1) * 128])
            w_up_l[l, j, :, :, 128:256] = _kc_layout(wu[l][:, DFF + j * 128:DFF + (j + 1) * 128])
    wdn = f(inp["w_down"])
    w_down_l = np.stack([np.stack([_kc_layout(wdn[l][:, hf * 512:(hf + 1) * 512]) for hf in range(2)])
                         for l in range(2)])
    cw, cbias = f(inp["conv_w"]), f(inp["conv_b"])
    convp = np.empty((2, 128, NFC, 4), np.float32)
    for l in range(2):
        for k in range(3):
            convp[l, :, :, k] = cw[l, k].reshape(NFC, 128).T
        convp[l, :, :, 3] = cbias[l].reshape(NFC, 128).T
    ln4 = np.stack([np.stack([f(inp["ln_mix_g"])[l], f(inp["ln_mix_b"])[l], f(inp["ln_ffn_g"])[l],
                              f(inp["ln_ffn_b"])[l]]) for l in range(2)])
    sfc = f(inp["state_ffn_conv"])
    cst = _consts()
    ckk, cvv, clf = f(inp["cache_k"]), f(inp["cache_v"]), f(inp["cache_logf"])
    sret = f(inp["state_ret"])[0]
    maps = []
    for c in range(8):
        b, r = c // 4, c % 4
        m = {}
        m["xA"] = xp[b]
        m["xB"] = np.ascontiguousarray(xp[b, r * SEG:(r + 1) * SEG])
        m["xH"] = np.ascontiguousarray(xp[b, r * SEG - 64:r * SEG]) if r > 0 else np.zeros((64, D), np.float32)
        m["xS"] = xs[c]
        m["w_inP"] = w_in_heads[r]
        for h in range(4):
            m["w_inS%d" % h] = w_in_heads[h]
        tab, gam = _rope_tables(np.arange(SEQ), 128, r)
        m["ropeP"] = tab
        m["ropeS"] = ropeS
        m["gamP"] = np.full((128, 1), gam, np.float32)
        m["gamS"] = gamS
        m["lnretP"] = lnretS[r]
        m["lnretS"] = lnretS
        m["sret"] = sret[c]
        m["w_outA"] = w_outA_l
        m["w_up"] = w_up_l
        m["w_down"] = w_down_l
        m["convp"] = convp
        m["ln4"] = ln4
        m["sconv"] = np.ascontiguousarray(
            np.stack([sfc[l, c].reshape(2, NFC, 128).transpose(2, 1, 0) for l in range(2)]))
        m["w_qkvP"] = w_qkv_groups[r]
        for g in range(4):
            m["w_qkvS%d" % g] = w_qkv_groups[g]
        m["bfP"] = b_f[r * 4:(r + 1) * 4].reshape(1, 4)
        m["bfS"] = b_f.reshape(4, 4)
        m["w_outB"] = w_outB_l
        m["ck"] = ckk[c].reshape(PAST, D)
        m["cv"] = cvv[c].reshape(PAST, D)
        m["clf"] = clf[c]
        m["flag"] = np.full((128, 1), 1.0 if r > 0 else 0.0, np.float32)
        p = np.arange(128)
        idxB = np.zeros((128, (NTB_B + 1) * 4), np.int32)
        for tbi in range(NTB_B):
            for h in range(4):
                idxB[:, tbi * 4 + h] = _gmap(r * SEG + tbi * 128 + p, h, 1024)
        h0 = max(r * SEG - 64, 0)
        for h in range(4):
            idxB[:, NTB_B * 4 + h] = _gmap(h0 + (p % 64), h, 1024)
        m["idxB"] = idxB
        idxD = np.zeros((128, (ntile_B + 1) * 8 + 1), np.int32)
        R = 4 * NQT * 64
        for t in range(ntile_B + 1):
            qt = r * ntile_B + t if t < ntile_B else max(r * ntile_B - 1, 0)
            for kc in range(8):
                g = kc // 2
                hh = 2 * (kc % 2) + p // 64
                idxD[:, t * 8 + kc] = _gmap((hh * NQT + qt) * 64 + (p % 64), g, 1024)
        tokh = h0 + (p % 64)
        idxD[:, (ntile_B + 1) * 8] = _gmap(tokh % SEG, tokh // SEG, 512)
        m["idxD"] = idxD
        m["cst"] = cst
        maps.append({k: np.ascontiguousarray(v) for k, v in m.items()})
    return maps


def assemble(res, SEQ, Bp=2):
    SEG = SEQ // 4
    R = [r for r in res]
    g = lambda c, k: np.asarray(R[c][k], dtype=np.float32)
    y_p = np.stack([np.concatenate([g(b * 4 + r, "yP") for r in range(4)], 0) for b in range(Bp)])
    y_s = np.stack([g(c, "yS") for c in range(8)])
    ret_p = np.stack([np.stack([g(b * 4 + r, "retP") for r in range(4)]) for b in range(Bp)])[None]
    k_p = np.stack([np.concatenate([g(b * 4 + r, "pk").reshape(SEQ, 4, 64) for r in range(4)], 1) for b in range(Bp)])
    v_p = np.stack([np.concatenate([g(b * 4 + r, "pv").reshape(SEQ, 4, 64) for r in range(4)], 1) for b in range(Bp)])
    lf_p = np.stack([np.concatenate([g(b * 4 + r, "plf") for r in range(4)], 1) for b in range(Bp)])

    def conv_fix(a):
        return a.transpose(2, 1, 0).reshape(2, DFF)

    conv_p = np.stack([np.stack([conv_fix(g(b * 4 + 3, "pconv")[l]) for b in range(Bp)]) for l in range(2)])
    ret_s = np.stack([g(c, "retS") for c in range(8)])[None]
    k_s = np.stack([g(c, "sk").transpose(1, 0, 2).reshape(NS, 16, 64) for c in range(8)])
    v_s = np.stack([g(c, "sv").transpose(1, 0, 2).reshape(NS, 16, 64) for c in range(8)])
    lf_s = np.stack([g(c, "slf").transpose(1, 0, 2).reshape(NS, 16) for c in range(8)])
    conv_s = np.stack([np.stack([conv_fix(g(c, "sconv_o")[l]) for c in range(8)]) for l in range(2)])
    return (y_p, y_s, ret_p, k_p, v_p, lf_p, conv_p, ret_s, k_s, v_s, lf_s, conv_s)


_CACHE = {}


def kernel(**inputs):
    SEQ = int(np.asarray(inputs["x_prompt"]).shape[1])
    if SEQ not in _CACHE:
        import os
        _CACHE[SEQ] = build_program(SEQ, int(os.environ.get("KSTOP", "99")))
    B = _CACHE[SEQ]
    in_maps = make_in_maps(inputs, SEQ)
    import os
    if os.environ.get("KTRACE"):
        res = run_bass_kernel_spmd(B.nc, in_maps, core_ids=list(range(8)), trace=True)
        print("EXEC_TIME_NS", res.exec_time_ns, flush=True)
    else:
        res = run_bass_kernel_spmd(B.nc, in_maps, core_ids=list(range(8)))
    return assemble(res.results, SEQ)
```
